# Optimizing a Trainium2 kernel written in Bass

```python
import math
import jax, jax.numpy as jnp
from jax import lax
import numpy as np

D_MODEL = 1024
BATCH = 8
SEQ = 8192
DEPTH = 1

PLE_DIM = 256
CHUNK = 128
EPS = 1e-6
E_A = D_MODEL
G_A = 4
D_INNER = 2 * D_MODEL
HEAD_DIM = 64
N_HEADS = D_INNER // HEAD_DIM
N_STATE = 128
N_GROUPS = 4
CONV_K = 4
CONV_DIM = D_INNER + 2 * N_GROUPS * N_STATE
COL_SIZES = (E_A, E_A, E_A, D_INNER, CONV_DIM, N_HEADS, D_MODEL, D_MODEL)
N_IN = sum(COL_SIZES)

kernel_name = 'hybrid_gmlp_ssd_gated_merge_ple'


def rms_norm(x, g):
    xf = x.astype(jnp.float32)
    y = xf * lax.rsqrt(jnp.mean(xf * xf, axis=-1, keepdims=True) + EPS)
    return (y * g.astype(jnp.float32)).astype(x.dtype)


def layer_norm(x, g, b):
    xf = x.astype(jnp.float32)
    mu = jnp.mean(xf, axis=-1, keepdims=True)
    xc = xf - mu
    y = xc * lax.rsqrt(jnp.mean(xc * xc, axis=-1, keepdims=True) + EPS)
    return (y * g.astype(jnp.float32) + b.astype(jnp.float32)).astype(x.dtype)


def gmlp_branch(u, v, z, ln_g, ln_b, w_s, b_s):
    bsz, s, e = u.shape
    nc = s // CHUNK
    u = jax.nn.gelu(u)
    v = layer_norm(jax.nn.gelu(v), ln_g, ln_b)
    mask = jnp.tril(jnp.ones((CHUNK, CHUNK), dtype=bool))
    ws = jnp.where(mask[None], w_s, jnp.zeros_like(w_s)).astype(v.dtype)
    vc = v.reshape(bsz, nc, CHUNK, G_A, e // G_A)
    sv = jnp.einsum('gts,bcsgd->bctgd', ws, vc) + b_s.T.astype(v.dtype)[None, None, :, :, None]
    return u * sv.reshape(bsz, s, e) * jax.nn.silu(z)


def causal_depthwise_conv(x, w, b):
    c = x.shape[-1]
    y = lax.conv_general_dilated(x, w.astype(x.dtype)[:, None, :], window_strides=(1,),
                                 padding=[(CONV_K - 1, 0)],
                                 dimension_numbers=('NWC', 'WIO', 'NWC'),
                                 feature_group_count=c)
    return y + b.astype(x.dtype)


def ssd_scan(xs, dt, a_log, bm, cm, d_skip):
    bsz, s, h, pdim = xs.shape
    nc = s // CHUNK
    g = N_GROUPS
    r = h // g
    dtype = xs.dtype
    a = -jnp.exp(a_log.astype(jnp.float32)).reshape(g, r)
    X = xs.reshape(bsz, nc, CHUNK, g, r, pdim)
    dtc = dt.reshape(bsz, nc, CHUNK, g, r)
    Xdt = X * dtc[..., None].astype(dtype)
    dA_cs = jnp.cumsum(dtc * a, axis=2)
    Bc = bm.reshape(bsz, nc, CHUNK, g, N_STATE)
    Cc = cm.reshape(bsz, nc, CHUNK, g, N_STATE)
    mask = jnp.tril(jnp.ones((CHUNK, CHUNK), dtype=bool))[None, None, :, :, None, None]
    seg = dA_cs[:, :, :, None] - dA_cs[:, :, None, :]
    Lmat = jnp.exp(jnp.where(mask, seg, -jnp.inf)).astype(dtype)
    CB = jnp.einsum('bclgn,bcsgn->bclsg', Cc, Bc)
    y_diag = jnp.einsum('bclsg,bclsgr,bcsgrp->bclgrp', CB, Lmat, Xdt)
    decay_s = jnp.exp(dA_cs[:, :, -1:] - dA_cs).astype(dtype)
    states = jnp.einsum('bcsgn,bcsgr,bcsgrp->bcgrpn', Bc, decay_s, Xdt)
    chunk_decay = jnp.exp(dA_cs[:, :, -1]).astype(dtype)

    def step(hstate, inp):
        st, dec = inp
        return dec[..., None, None] * hstate + st, hstate

    h0 = jnp.zeros((bsz, g, r, pdim, N_STATE), dtype=states.dtype)
    _, prev = lax.scan(step, h0, (jnp.moveaxis(states, 1, 0), jnp.moveaxis(chunk_decay, 1, 0)))
    prev = jnp.moveaxis(prev, 0, 1)
    y_off = jnp.einsum('bclgn,bcgrpn,bclgr->bclgrp', Cc, prev, jnp.exp(dA_cs).astype(dtype))
    y = y_diag + y_off + X * d_skip.astype(dtype).reshape(g, r)[..., None]
    return y.reshape(bsz, s, h * pdim)


def gated_group_rms_norm(y, z, g):
    yz = y * jax.nn.silu(z)
    bsz, s, e = yz.shape
    yg = yz.reshape(bsz, s, N_GROUPS, e // N_GROUPS).astype(jnp.float32)
    yg = yg * lax.rsqrt(jnp.mean(yg * yg, axis=-1, keepdims=True) + EPS)
    return (yg.reshape(bsz, s, e) * g.astype(jnp.float32)).astype(y.dtype)


def setup_inputs(seed: int = 0) -> dict:
    key = jax.random.key(seed)
    ks = jax.random.split(key, 24)
    f32 = jnp.float32
    nrm = lambda k, shape, scale: jax.random.normal(k, shape, f32) * scale
    gain = lambda k, shape: 1.0 + 0.02 * jax.random.normal(k, shape, f32)
    dt0 = jnp.exp(jax.random.uniform(ks[10], (DEPTH, N_HEADS), f32) * (math.log(0.1) - math.log(0.001)) + math.log(0.001))
    return {
        'x': jax.random.normal(ks[0], (BATCH, SEQ, D_MODEL), f32),
        'p': jax.random.normal(ks[1], (DEPTH, BATCH, SEQ, PLE_DIM), f32),
        'norm_g': gain(ks[2], (DEPTH, D_MODEL)),
        'w_in': nrm(ks[3], (DEPTH, D_MODEL, N_IN), D_MODEL ** -0.5),
        'ln_a_g': gain(ks[4], (DEPTH, E_A)),
        'ln_a_b': nrm(ks[5], (DEPTH, E_A), 0.02),
        'w_s': nrm(ks[6], (DEPTH, G_A, CHUNK, CHUNK), CHUNK ** -0.5),
        'b_s': gain(ks[7], (DEPTH, G_A, CHUNK)),
        'conv_w': nrm(ks[8], (DEPTH, CONV_K, CONV_DIM), CONV_K ** -0.5),
        'conv_b': nrm(ks[9], (DEPTH, CONV_DIM), 0.02),
        'dt_bias': dt0 + jnp.log(-jnp.expm1(-dt0)),
        'a_log': jnp.log(jax.random.uniform(ks[11], (DEPTH, N_HEADS), f32, 1.0, 16.0)),
        'd_skip': gain(ks[12], (DEPTH, N_HEADS)),
        'ssm_norm_g': gain(ks[13], (DEPTH, D_INNER)),
        'w_oa': nrm(ks[14], (DEPTH, E_A, D_MODEL), E_A ** -0.5),
        'w_ob': nrm(ks[15], (DEPTH, D_INNER, D_MODEL), D_INNER ** -0.5),
        'w_out': nrm(ks[16], (DEPTH, D_MODEL, D_MODEL), D_MODEL ** -0.5),
        'ple_norm_g': gain(ks[17], (DEPTH, D_MODEL)),
        'w_pg': nrm(ks[18], (DEPTH, D_MODEL, D_MODEL), D_MODEL ** -0.5),
        'w_ple': nrm(ks[19], (DEPTH, PLE_DIM, D_MODEL), PLE_DIM ** -0.5),
        'final_g': gain(ks[20], (D_MODEL,)),
    }


def reference(x, p, norm_g, w_in, ln_a_g, ln_a_b, w_s, b_s, conv_w, conv_b, dt_bias, a_log,
              d_skip, ssm_norm_g, w_oa, w_ob, w_out, ple_norm_g, w_pg, w_ple, final_g):
    bsz, s, _ = x.shape
    splits = list(np.cumsum(COL_SIZES)[:-1])
    for i in range(DEPTH):
        h = rms_norm(x, norm_g[i])
        proj = h @ w_in[i].astype(h.dtype)
        u, v, z_a, z_b, xbc, dt_raw, g_a, g_b = jnp.split(proj, splits, axis=-1)
        y_a = gmlp_branch(u, v, z_a, ln_a_g[i], ln_a_b[i], w_s[i], b_s[i])
        o_a = y_a @ w_oa[i].astype(y_a.dtype)
        xbc = jax.nn.silu(causal_depthwise_conv(xbc, conv_w[i], conv_b[i]))
        xs, bm, cm = jnp.split(xbc, [D_INNER, D_INNER + N_GROUPS * N_STATE], axis=-1)
        dt = jax.nn.softplus(dt_raw.astype(jnp.float32) + dt_bias[i].astype(jnp.float32))
        y = ssd_scan(xs.reshape(bsz, s, N_HEADS, HEAD_DIM), dt, a_log[i],
                     bm.reshape(bsz, s, N_GROUPS, N_STATE), cm.reshape(bsz, s, N_GROUPS, N_STATE), d_skip[i])
        y_b = gated_group_rms_norm(y, z_b, ssm_norm_g[i])
        o_b = y_b @ w_ob[i].astype(y_b.dtype)
        merged = jax.nn.sigmoid(g_a) * o_a + jax.nn.sigmoid(g_b) * o_b
        x = x + merged @ w_out[i].astype(merged.dtype)
        hp = rms_norm(x, ple_norm_g[i])
        x = x + jax.nn.sigmoid(hp @ w_pg[i].astype(hp.dtype)) * (p[i] @ w_ple[i].astype(p.dtype))
    return rms_norm(x, final_g)
```

```python
import numpy as np
from contextlib import ExitStack
import concourse.bass as bass
import concourse.mybir as mybir
from concourse.bass_utils import run_bass_kernel_spmd

F32 = mybir.dt.float32
BF16 = mybir.dt.bfloat16
AF = mybir.ActivationFunctionType
ALU = mybir.AluOpType

D = 1024
SEQ = 8192
NB = 8
PLE = 256
T = 512
NCH = 4
EPS = 1e-6
NIN = 10272
C_U, C_V, C_ZA, C_ZB, C_XBC, C_DT, C_GA, C_GB = 0, 1024, 2048, 3072, 5120, 8192, 8224, 9248
MAXV = 30000
import os
STOPC = int(os.environ.get('STOPC', '0'))
USE_POW = int(os.environ.get('USE_POW', '0'))
NO_SELF = tuple(x for x in os.environ.get('NO_SELF', '').split(',') if x)


class _Stop(Exception):
    pass


class Buf:
    def __init__(self, name, raw=None, nbytes=0):
        self.name = name
        self.raw = raw
        self.nbytes = nbytes
        self.recs = {}
        self.ranges = {}

    def v(self, dt=BF16, pat=None, **kw):
        ap = self.raw if dt == BF16 else self.raw.bitcast(dt)
        if pat is not None:
            ap = ap.rearrange(pat, **kw)
        return ap


class Rec:
    __slots__ = ("writer", "readers")

    def __init__(self):
        self.writer = None
        self.readers = []


class Op:
    __slots__ = ("eng", "fn", "deps", "dma", "token", "needed", "seq")


def _acc(a):
    if not isinstance(a, tuple):
        return (a, None)
    if len(a) == 4:
        a[0].ranges[a[1]] = (a[2], a[3])
        return (a[0], a[1])
    return a


def _conf(buf, k1, k2):
    if k1 == k2 or k1 is None or k2 is None:
        return True
    r1 = buf.ranges.get(k1)
    r2 = buf.ranges.get(k2)
    if r1 is None or r2 is None:
        return False
    return r1[0] < r2[1] and r2[0] < r1[1]


class Prog:
    def __init__(self, nc, es):
        self.nc = nc
        self.es = es
        self.ops = {e: [] for e in ("pe", "act", "dve", "pool", "sp")}
        self.dma_sems = {}
        self.dma_counts = {}
        self.all_ops = []

    def _rec(self, buf, key):
        r = buf.recs.get(key)
        if r is None:
            r = buf.recs[key] = Rec()
        return r

    def add(self, eng, fn, reads=(), writes=(), dma=None, reg_reads=True):
        op = Op()
        op.eng = eng
        op.fn = fn
        op.dma = dma is not None
        op.needed = False
        op.token = None
        deps = []
        xr = [a for a in reads if getattr(_acc(a)[0], "excl", False)]
        if xr:
            reads = [a for a in reads if not getattr(_acc(a)[0], "excl", False)]
            writes = list(writes) + [_acc(a)[0] for a in xr]
        writes = [(_acc(a)[0] if getattr(_acc(a)[0], "excl", False) else a) for a in writes]
        for a in reads:
            buf, key = _acc(a)
            for k, r in buf.recs.items():
                if r.writer is not None and _conf(buf, key, k):
                    deps.append(r.writer)
        for a in writes:
            buf, key = _acc(a)
            for k, r in buf.recs.items():
                if _conf(buf, key, k):
                    if r.writer is not None:
                        deps.append(r.writer)
                    deps.extend(r.readers)
        for a in (reads if reg_reads else ()):
            buf, key = _acc(a)
            r = self._rec(buf, key)
            if not op.dma:
                r.readers = [x for x in r.readers if x.dma or x.eng != eng]
            r.readers.append(op)
        for a in writes:
            buf, key = _acc(a)
            if key is None:
                buf.recs = {}
            r = self._rec(buf, key)
            r.writer = op
            r.readers = []
        ud = []
        seen = set()
        for d in deps:
            if id(d) in seen or d is op:
                continue
            seen.add(id(d))
            if not (eng == "pe" and d.eng == "pe" and not d.dma and not op.dma):
                d.needed = True
            ud.append(d)
        op.deps = ud
        if op.dma:
            if dma not in self.dma_sems:
                self.dma_sems[dma] = self.es.enter_context(self.nc.semaphore("d_" + dma))
                self.dma_counts[dma] = 0
            self.dma_counts[dma] += 16
            op.token = (self.dma_sems[dma], self.dma_counts[dma])
        self.ops[eng].append(op)
        self.all_ops.append(op)
        return op

    def finalize(self):
        self.eng_sems = {}
        for eng, lst in self.ops.items():
            n = 0
            sems = []
            for op in lst:
                if op.dma or not op.needed:
                    continue
                ep = n // MAXV
                if ep >= len(sems):
                    sems.append(self.es.enter_context(self.nc.semaphore("e_%s%d" % (eng, ep))))
                op.token = (sems[ep], n % MAXV + 1)
                n += 1
            self.eng_sems[eng] = sems

    def emit(self, engname, e, final_waits=()):
        seen = {}
        for op in self.ops[engname]:
            need = {}
            for d in op.deps:
                if (not d.dma) and d.eng == engname and (engname == "pe" or engname in NO_SELF):
                    continue
                sem, val = d.token
                k = id(sem)
                if k not in need or need[k][1] < val:
                    need[k] = (sem, val)
            for k, (sem, val) in need.items():
                if seen.get(k, 0) >= val:
                    continue
                e.wait_ge(sem, val)
                seen[k] = val
            ins = op.fn(e)
            if op.dma:
                ins.then_inc(op.token[0], 16)
            elif op.needed:
                ins.then_inc(op.token[0], 1)
        for d in final_waits:
            sem, val = d.token
            if seen.get(id(sem), 0) >= val:
                continue
            e.wait_ge(sem, val)
            seen[id(sem)] = val


def weight_units():
    u = []
    for i in range(2):
        u.append(("u%d" % i, "w_in", 8, C_U + 512 * i, 512, "ng"))
    for i in range(2):
        u.append(("za%d" % i, "w_in", 8, C_ZA + 512 * i, 512, "ng"))
    for i in range(2):
        u.append(("v%d" % i, "w_in", 8, C_V + 512 * i, 512, "ng"))
    for i in range(2):
        u.append(("ga%d" % i, "w_in", 8, C_GA + 512 * i, 512, "ng"))
    for i in range(2):
        u.append(("oa%d" % i, "w_oa", 8, 512 * i, 512, None))
    u.append(("dt", "w_in", 8, C_DT, 32, "ng"))
    for g in range(4):
        u.append(("xs%d" % g, "w_in", 8, C_XBC + 512 * g, 512, "ng"))
    for g in range(4):
        u.append(("bm%d" % g, "w_in", 8, C_XBC + 2048 + 128 * g, 128, "ng"))
        u.append(("cm%d" % g, "w_in", 8, C_XBC + 2560 + 128 * g, 128, "ng"))
    for g in range(4):
        u.append(("zb%d" % g, "w_in", 8, C_ZB + 512 * g, 512, "ng"))
    for i in range(2):
        u.append(("gb%d" % i, "w_in", 8, C_GB + 512 * i, 512, "ng"))
    for i in range(4):
        u.append(("ob%d" % i, "w_ob", 16, 256 * i, 256, "sng"))
    for i in range(2):
        u.append(("out%d" % i, "w_out", 8, 512 * i, 512, None))
    for i in range(2):
        u.append(("pg%d" % i, "w_pg", 8, 512 * i, 512, "png"))
    for i in range(2):
        u.append(("ple%d" % i, "w_ple", 2, 512 * i, 512, None))
    return u


CONST_SPECS = [
    ("ident_b", [128], BF16), ("tri_b", [128], BF16), ("sut_b", [128], BF16),
    ("tri_f", [128], F32), ("ones_f", [128], F32),
    ("ng", [8], F32), ("png", [8], F32), ("sng", [16], F32), ("lng", [8], F32), ("lnb", [8], F32),
    ("convw", [96], F32), ("convb", [24], F32),
    ("fg_bc", [1024], F32), ("dtb_bc", [32], F32), ("alog_bc", [32], F32), ("dskip_bc", [32], F32),
    ("wsT", [512], F32), ("bs_bc", [512], F32),
]


def build(n_tiles, debug=None, stop_after=None):
    nc = bass.Bass("TRN2", target_bir_lowering=False)
    ntok = n_tiles * T
    dram = {}
    dram["x"] = nc.dram_tensor("x", [ntok, D], F32, kind="ExternalInput").ap()
    dram["p"] = nc.dram_tensor("p", [ntok, PLE], F32, kind="ExternalInput").ap()
    dram["w_in"] = nc.dram_tensor("w_in", [D, NIN], F32, kind="ExternalInput").ap()
    dram["w_oa"] = nc.dram_tensor("w_oa", [D, D], F32, kind="ExternalInput").ap()
    dram["w_ob"] = nc.dram_tensor("w_ob", [2 * D, D], F32, kind="ExternalInput").ap()
    dram["w_out"] = nc.dram_tensor("w_out", [D, D], F32, kind="ExternalInput").ap()
    dram["w_pg"] = nc.dram_tensor("w_pg", [D, D], F32, kind="ExternalInput").ap()
    dram["w_ple"] = nc.dram_tensor("w_ple", [PLE, D], F32, kind="ExternalInput").ap()
    for name, shp, dt in CONST_SPECS:
        dram[name] = nc.dram_tensor("c_" + name, [128] + shp, F32, kind="ExternalInput").ap()
    out_d = nc.dram_tensor("out", [ntok, D], F32, kind="ExternalOutput").ap()
    units = weight_units()
    uoff = {}
    off = 0
    for (nm, src, K, c0, ncol, rs) in units:
        uoff[nm] = off
        off += 128 * K * ncol
    wq = nc.dram_tensor("wq", [off], BF16, kind="Internal").ap()
    dbg_d = {}
    if debug is not None:
        for hook, lst in debug.items():
            for (nm, ncols, fn) in lst:
                dbg_d[nm] = nc.dram_tensor("dbg_" + nm, [128, ncols], F32, kind="ExternalOutput").ap()

    es = ExitStack()
    with es:
        P = Prog(nc, es)
        ARENA = 206 * 1024
        arena = es.enter_context(nc.sbuf_tensor("arena", [128, ARENA // 2], BF16))
        apos = [0]

        def alloc(name, nbytes):
            nbytes = (nbytes + 63) // 64 * 64
            o = apos[0]
            apos[0] += nbytes
            assert apos[0] <= ARENA, (name, apos[0])
            return Buf(name, arena[:, o // 2:(o + nbytes) // 2], nbytes)

        pm = []
        for i in range(6):
            t = es.enter_context(nc.psum_tensor("pm%d" % i, [128, 512], F32))
            b = Buf("pm%d" % i)
            b.ap = t
            b.excl = True
            pm.append(b)
        ptb = []
        for i in range(2):
            t = es.enter_context(nc.psum_tensor("pt%d" % i, [128, 1024], BF16))
            b = Buf("pt%d" % i)
            b.ap = t
            b.excl = True
            ptb.append(b)
        pmi = [0]
        pti = [0]

        spool = {"pm0": [0, [0, 1]], "pm1": [0, [2, 3]], "pm2": [0, [4, 5]], "pt0": [0, [0]], "pt1": [0, [1]]}

        def nextpm(pool=None):
            if pool is not None:
                st = spool[pool]
                b = pm[st[1][st[0] % len(st[1])]]
                st[0] += 1
                return b
            b = pm[pmi[0] % 6]
            pmi[0] += 1
            return b

        def nextpt(pool=None):
            if pool is not None:
                st = spool[pool]
                b = ptb[st[1][st[0] % len(st[1])]]
                st[0] += 1
                return b
            b = ptb[pti[0] % 2]
            pti[0] += 1
            return b

        xa = [alloc("xa%d" % i, 4096) for i in range(2)]
        hT = [alloc("hT%d" % i, 8192) for i in range(2)]
        hb = [alloc("hb%d" % i, 2048) for i in range(2)]
        mg = alloc("mg", 8192)
        ybT = alloc("ybT", 16384)
        state = alloc("state", 8192)
        prevbf = alloc("prevbf", 4096)
        ring = [alloc("ring%d" % i, 8192) for i in range(4)]
        wdt = alloc("wdt", 512)
        dtr = alloc("dtr", 512)
        dtt = alloc("dtt", 512)
        dA = alloc("dA", 512)
        halo = alloc("halo", 24 * 3 * 2)
        junk = alloc("junk", 64)
        zbx = alloc("zbx", 8192)
        small = [alloc("small%d" % i, 256) for i in range(4)]
        P1 = alloc("P1", 8192)
        P2 = alloc("P2", 8192)
        P3 = alloc("P3", 16384)
        t1 = [alloc("t1_%d" % i, 2048) for i in range(2)]
        gab = [alloc("gab%d" % i, 1024) for i in range(2)]
        raw = [alloc("raw%d" % i, 515 * 2) for i in range(3)]
        acc = [alloc("acc%d" % i, 2048) for i in range(2)]
        Xdt = [alloc("Xdt%d" % i, 1024) for i in range(2)]
        XD = [alloc("XD%d" % i, 1024) for i in range(2)]
        XdD = [alloc("XdD%d" % i, 1024) for i in range(2)]
        Btm = [alloc("Btm%d" % i, 256) for i in range(2)]
        S4 = [alloc("S4_%d" % i, 4096) for i in range(2)]
        Lb = [alloc("L%d" % i, 2048) for i in range(2)]
        Gb = [alloc("G%d" % i, 2048) for i in range(2)]
        CBm = [alloc("CBm%d" % i, 256) for i in range(2)]
        y1 = [alloc("y1_%d" % i, 2048) for i in range(2)]
        ybb = [alloc("yb%d" % i, 1024) for i in range(2)]
        tmpst = [alloc("tmp%d" % i, 2048) for i in range(2)]
        ssd_s = [alloc("ssd_s%d" % i, 1024) for i in range(2)]
        C = {}
        for name, shp, dt in CONST_SPECS:
            n = shp[0]
            if name == "wsT":
                C[name] = y1[0]
            elif name == "bs_bc":
                C[name] = y1[1]
            else:
                C[name] = alloc("c_" + name, n * (2 if dt == BF16 else 4))
        a_bc = alloc("a_bc", 32 * 4)
        wsTm_f = tmpst[0]
        wsTm_b = alloc("wsTm_b", 512 * 2)
        BR = alloc("BR", 1024 * 4)
        neghalf = alloc("neghalf", 64)
        print("arena used", apos[0])

        def k1(j): return (P1, j, j * 1024, (j + 1) * 1024)
        def k1hp(c): return (P1, ("hp", c), 0, 8192)
        def k2(j): return (P2, j, j * 1024, (j + 1) * 1024)
        def k2zb(s): return (P2, ("zb", s), s * 4096, (s + 1) * 4096)
        def k2gate(c, i): return (P2, ("gate", c, i), c * 2048 + i * 1024, c * 2048 + (i + 1) * 1024)
        def k3vg(s): return (P3, ("vg", s), s * 4096, (s + 1) * 4096)
        def k3vn(c): return (P3, ("vn", c), 8192 + c * 2048, 8192 + (c + 1) * 2048)
        def k3xc(s): return (P3, ("xc", s), s * 6144, (s + 1) * 6144)
        def k3x1(c): return (P3, ("x1", c), c * 4096, (c + 1) * 4096)
        def kL(q, hh): return (Lb[q], hh, hh * 1024, (hh + 1) * 1024)
        def kLpT(c): return (Lb[0], ("pT", c), 0, 2048)
        def kS4(q, ab): return (S4[q], ab, 0, 2048) if ab == "a" else (S4[q], ab, 2048, 4096)

        ident_b = C["ident_b"].v(BF16)
        tri_b = C["tri_b"].v(BF16)
        sut_b = C["sut_b"].v(BF16)
        tri_f = C["tri_f"].v(F32)
        ones_f = C["ones_f"].v(F32)

        ssem_cnt = [0]
        store_ops = []

        def act(out, in_, func, reads, writes, bias=None, scale=None, accum=None):
            kw = {}
            if bias is not None:
                kw["bias"] = bias
            if scale is not None:
                kw["scale"] = scale
            if accum is not None:
                kw["accum_out"] = accum
            return P.add("act", lambda e: e.activation(out=out, in_=in_, func=func, **kw), reads, writes)

        def tt(eng, out, in0, in1, op, reads, writes):
            return P.add(eng, lambda e: e.tensor_tensor(out=out, in0=in0, in1=in1, op=op), reads, writes)

        def ts(eng, out, in0, s1, s2, op0, op1, reads, writes):
            if op1 is None:
                return P.add(eng, lambda e: e.tensor_scalar(out=out, in0=in0, scalar1=s1, scalar2=None, op0=op0), reads, writes)
            return P.add(eng, lambda e: e.tensor_scalar(out=out, in0=in0, scalar1=s1, scalar2=s2, op0=op0, op1=op1), reads, writes)

        def stt(out, in0, scalar, in1, op0, op1, reads, writes):
            return P.add("dve", lambda e: e.scalar_tensor_tensor(out=out, in0=in0, scalar=scalar, in1=in1, op0=op0, op1=op1), reads, writes)

        def cp(eng, out, in_, reads, writes):
            if eng == "act":
                return P.add("act", lambda e: e.copy(out=out, in_=in_), reads, writes)
            return P.add(eng, lambda e: e.tensor_copy(out=out, in_=in_), reads, writes)

        mm_pend = []

        def mm(out, lhsT, rhs, start, stop, reads, writes):
            if not stop:
                mm_pend.extend(reads)
                return P.add("pe", lambda e: e.matmul(out, lhsT, rhs, start=start, stop=stop), reads, writes, reg_reads=False)
            allr = list(reads) + list(mm_pend)
            del mm_pend[:]
            return P.add("pe", lambda e: e.matmul(out, lhsT, rhs, start=start, stop=stop), allr, writes)

        def tp(out, in_, reads, writes, ident=None):
            idn = ident_b if ident is None else ident
            return P.add("pe", lambda e: e.transpose(out, in_, idn), list(reads) + [C["ident_b"]], writes)

        def dma(q, out, in_, reads, writes, sem):
            return P.add(q, lambda e: e.dma_start(out=out, in_=in_), reads, writes, dma=sem)

        def rsqrt_chain(dst_ap, src_ap, mulc, srcbuf, dstbuf):
            ts("dve", dst_ap, src_ap, mulc, EPS, ALU.mult, ALU.add, [srcbuf], [dstbuf])
            act(dst_ap, dst_ap, AF.Ln, [dstbuf], [dstbuf])
            act(dst_ap, dst_ap, AF.Exp, [dstbuf], [dstbuf], scale=-0.5)

        for name, shp, dt in CONST_SPECS:
            b = C[name]
            dma("pool", b.v(dt)[:, :shp[0]], dram[name], [], [b], "const")
        P.add("pool", lambda e: e.memset(junk.v(BF16)[:, 0:8], 0.0), list(C.values()), list(C.values()))
        act(a_bc.v(F32), C["alog_bc"].v(F32), AF.Exp, [C["alog_bc"]], [a_bc])
        ts("dve", a_bc.v(F32), a_bc.v(F32), -1.0, None, ALU.mult, None, [a_bc], [a_bc])
        wsT3 = C["wsT"].v(F32, "p (g t) -> p g t", g=4)
        tt("dve", wsTm_f.v(F32, "p (g t) -> p g t", g=4), wsT3,
           tri_f.unsqueeze(1).to_broadcast([128, 4, 128]), ALU.mult, [C["wsT"], C["tri_f"]], [wsTm_f])
        cp("dve", wsTm_b.v(BF16), wsTm_f.v(F32), [wsTm_f], [wsTm_b])
        bk = nextpm()
        mm(bk.ap[:, :], ones_f, wsTm_f.v(F32), True, True, [C["ones_f"], wsTm_f], [bk])
        BR3 = BR.v(F32, "p (j t) -> p j t", j=8)
        bs3 = C["bs_bc"].v(F32, "p (g t) -> p g t", g=4)
        lnb = C["lnb"].v(F32)
        lng = C["lng"].v(F32)
        for j in range(8):
            g = j // 2
            stt(BR3[:, j, :], bk.ap[:, g * 128:(g + 1) * 128], lnb[:, j:j + 1], bs3[:, g, :], ALU.mult, ALU.add,
                [bk, C["lnb"], C["bs_bc"]], [(BR, j)])
        P.add("pool", lambda e: e.memset(neghalf.v(F32), -0.5), [], [neghalf])
        P.add("pool", lambda e: e.memset(state.v(F32), 0.0), [], [state])
        P.add("pool", lambda e: e.memset(prevbf.v(BF16), 0.0), [], [prevbf])
        P.add("pool", lambda e: e.memset(halo.v(BF16), 0.0), [], [halo])

        wq_bufs = {nm: Buf("wq_" + nm) for (nm, *_r) in units}
        stg_in = [t1[0], t1[1], acc[0], acc[1], Lb[0], Lb[1], Gb[0], Gb[1]]
        stg_out = [gab[0], gab[1], Xdt[0], Xdt[1], XD[0], XD[1], XdD[0], XdD[1]]
        NSTG = 8
        cnt = 0
        for (nm, src, K, c0, ncol, rs) in units:
            dst = wq[uoff[nm]:uoff[nm] + 128 * K * ncol].rearrange("(p k c) -> p k c", p=128, k=K)
            for k in range(K):
                si = stg_in[cnt % NSTG]
                so = stg_out[cnt % NSTG]
                dma("sp", si.v(F32)[:, :ncol], dram[src][k * 128:(k + 1) * 128, c0:c0 + ncol], [], [si], "stgi%d" % (cnt % NSTG))
                eng = "act" if cnt % 2 == 0 else "dve"
                if rs is None:
                    cp(eng, so.v(BF16)[:, :ncol], si.v(F32)[:, :ncol], [si], [so])
                else:
                    sc = C[rs].v(F32)[:, k:k + 1]
                    if eng == "act":
                        act(so.v(BF16)[:, :ncol], si.v(F32)[:, :ncol], AF.Identity, [si, C[rs]], [so], scale=sc)
                    else:
                        ts("dve", so.v(BF16)[:, :ncol], si.v(F32)[:, :ncol], sc, None, ALU.mult, None, [si, C[rs]], [so])
                dma("pool", dst[:, k, :], so.v(BF16)[:, :ncol], [so], [(wq_bufs[nm], k)], "stgo%d" % (cnt % NSTG))
                cnt += 1

        ring_i = [0]

        def load_unit(nm):
            (_, src, K, c0, ncol, rs) = [u for u in units if u[0] == nm][0]
            if nm == "dt":
                slot = wdt
                semn = "wdt"
            else:
                slot = ring[ring_i[0] % 4]
                semn = "ring%d" % (ring_i[0] % 4)
                ring_i[0] += 1
            srcap = wq[uoff[nm]:uoff[nm] + 128 * K * ncol].rearrange("(p k c) -> p k c", p=128, k=K)
            view = slot.v(BF16)[:, :K * ncol].rearrange("p (k c) -> p k c", k=K)
            dma("sp", view, srcap, [wq_bufs[nm]], [slot], semn)
            return slot, view

        def zipper(gens):
            gens = list(gens)
            while gens:
                for gen in list(gens):
                    try:
                        next(gen)
                    except StopIteration:
                        gens.remove(gen)

        def stage_a(ti):
            for _ in stage_a_gen(ti):
                pass

        def stage_a_gen(ti):
            sl = ti % 2
            hT3 = hT[sl].v(BF16, "p (k t) -> p k t", k=8)
            for c in range(NCH):
                xb = xa[c % 2]
                r0 = ti * T + c * 128
                dma("sp", xb.v(F32), dram["x"][r0:r0 + 128, :], [], [xb], "xa%d" % (c % 2))
                ss = small[0]
                hbb = hb[c % 2]
                act(hbb.v(BF16), xb.v(F32), AF.Square, [xb], [(ss, 0), hbb], accum=ss.v(F32)[:, 0:1])
                rsqrt_chain(ss.v(F32)[:, 1:2], ss.v(F32)[:, 0:1], 1.0 / D, (ss, 0), (ss, 1))
                yield
                act(hbb.v(BF16), xb.v(F32), AF.Identity, [xb, (ss, 1)], [hbb], scale=ss.v(F32)[:, 1:2])
                pb_ = nextpt()
                for k in range(8):
                    tp(pb_.ap[:, k * 128:(k + 1) * 128], hbb.v(BF16)[:, k * 128:(k + 1) * 128], [hbb], [(pb_, k)])
                cp("dve", hT3[:, :, c * 128:(c + 1) * 128], pb_.ap.rearrange("p (k t) -> p k t", k=8), [pb_], [(hT[sl], c)])
                yield

        def fm_proj(view, hT3, jj, hTbuf, slot, pool=None):
            bk = nextpm(pool)
            for k in range(8):
                mm(bk.ap[:, :], view[:, k, jj * 128:(jj + 1) * 128], hT3[:, k, :], k == 0, k == 7, [slot, hTbuf], [bk])
            return bk

        def a_branch(ti):
            sl = ti % 2
            hTb = hT[sl]
            hT3 = hTb.v(BF16, "p (k t) -> p k t", k=8)
            ug3 = P1.v(BF16, "p (k t) -> p k t", k=8)
            zs3 = P2.v(BF16, "p (k t) -> p k t", k=8)
            sv = [load_unit("v0"), load_unit("v1")]
            vg = [Buf("vg0", P3.raw[:, 0:2048], 4096), Buf("vg1", P3.raw[:, 2048:4096], 4096)]
            vn = Buf("vn", P3.raw[:, 4096:8192], 8192)
            vn3 = vn.v(BF16, "p (c d) -> p c d", c=4)
            st = small[1]
            stv = st.v(F32)
            for cp_ in range(2):
                for s in range(2):
                    c = cp_ * 2 + s
                    vgv = vg[s].v(F32)
                    for i in range(2):
                        slot, view = sv[i]
                        bk = nextpm()
                        for k in range(8):
                            mm(bk.ap[:, :], hT3[:, k, c * 128:(c + 1) * 128], view[:, k, :], k == 0, k == 7, [slot, hTb], [bk])
                        act(vgv[:, i * 512:(i + 1) * 512], bk.ap[:, :], AF.Gelu_apprx_tanh, [bk], [k3vg(s)])
                        P.add("dve", lambda e, i=i, s=s, vgv=vgv: e.bn_stats(out=stv[:, s * 12 + i * 6:s * 12 + (i + 1) * 6], in_=vgv[:, i * 512:(i + 1) * 512]),
                              [k3vg(s)], [(st, ("bs", s, i))])
                    P.add("dve", lambda e, s=s: e.bn_aggr(out=stv[:, 32 + 2 * s:34 + 2 * s], in_=stv[:, s * 12:(s + 1) * 12]),
                          [(st, ("bs", s, 0)), (st, ("bs", s, 1))], [(st, ("ag", s))])
                    ts("dve", stv[:, 40 + s:41 + s], stv[:, 33 + 2 * s:34 + 2 * s], 1.0, EPS, ALU.mult, ALU.add, [(st, ("ag", s))], [(st, "rs")])
                act(stv[:, 40:42], stv[:, 40:42], AF.Ln, [(st, "rs")], [(st, "rs")])
                act(stv[:, 40:42], stv[:, 40:42], AF.Exp, [(st, "rs")], [(st, "rs")], scale=-0.5)
                for s in range(2):
                    c = cp_ * 2 + s
                    ts("dve", vn3[:, c, :], vg[s].v(F32), stv[:, 32 + 2 * s:33 + 2 * s], stv[:, 40 + s:41 + s], ALU.subtract, ALU.mult,
                       [k3vg(s), (st, ("ag", s)), (st, "rs")], [k3vn(c)])
            for i in range(2):
                slot, view = load_unit("u%d" % i)
                for jj in range(4):
                    j = i * 4 + jj
                    bk = fm_proj(view, hT3, jj, hTb, slot)
                    act(ug3[:, j, :], bk.ap[:, :], AF.Gelu_apprx_tanh, [bk], [k1(j)])
            for i in range(2):
                slot, view = load_unit("za%d" % i)
                for jj in range(4):
                    j = i * 4 + jj
                    bk = fm_proj(view, hT3, jj, hTb, slot)
                    act(zs3[:, j, :], bk.ap[:, :], AF.Silu, [bk], [k2(j)])
                    tt("dve", ug3[:, j, :], ug3[:, j, :], zs3[:, j, :], ALU.mult, [k1(j), k2(j)], [k1(j)])
            wsb3 = wsTm_b.v(BF16, "p (g t) -> p g t", g=4)
            for j in range(8):
                g = j // 2
                bk = nextpm()
                for c in range(NCH):
                    mm(bk.ap[:, c * 128:(c + 1) * 128], vn3[:, c, j * 128:(j + 1) * 128], wsb3[:, g, :], True, True,
                       [k3vn(c), wsTm_b], [bk])
                tb = t1[j % 2]
                stt(tb.v(F32, "p (c t) -> p c t", c=4), bk.ap.rearrange("p (c t) -> p c t", c=4), lng[:, j:j + 1],
                    BR3[:, j, :].unsqueeze(1).to_broadcast([128, 4, 128]), ALU.mult, ALU.add,
                    [bk, C["lng"], (BR, j)], [tb])
                tt("dve", ug3[:, j, :], tb.v(F32), ug3[:, j, :], ALU.mult, [tb, k1(j)], [k1(j)])
            mg3 = mg.v(BF16, "p (k t) -> p k t", k=8)
            sga = [load_unit("ga0"), load_unit("ga1")]
            soa = [load_unit("oa0"), load_unit("oa1")]
            for j in range(8):
                i, jj = j // 4, j % 4
                bk = fm_proj(sga[i][1], hT3, jj, hTb, sga[i][0])
                gt = gab[j % 2]
                act(gt.v(BF16), bk.ap[:, :], AF.Sigmoid, [bk], [gt])
                bk2 = nextpm()
                for k in range(8):
                    mm(bk2.ap[:, :], soa[i][1][:, k, jj * 128:(jj + 1) * 128], ug3[:, k, :], k == 0, k == 7, [soa[i][0], P1], [bk2])
                tt("dve", mg3[:, j, :], bk2.ap[:, :], gt.v(BF16), ALU.mult, [bk2, gt], [(mg, j)])

        def b_branch(ti):
            sl = ti % 2
            hTb = hT[sl]
            hT3 = hTb.v(BF16, "p (k t) -> p k t", k=8)
            slot, view = load_unit("dt")
            bk = nextpm()
            for c in range(NCH):
                for k in range(8):
                    mm(bk.ap[:, c * 32:(c + 1) * 32], hT3[:, k, c * 128:(c + 1) * 128], view[:, k, :], k == 0, k == 7, [slot, hTb], [bk])
            dtr3 = dtr.v(F32, "p (c h) -> p c h", c=4)
            tt("dve", dtr3, bk.ap[:, 0:128].rearrange("p (c h) -> p c h", c=4),
               C["dtb_bc"].v(F32).unsqueeze(1).to_broadcast([128, 4, 32]), ALU.add, [bk, C["dtb_bc"]], [dtr])
            stt(dtt.v(F32), dtr.v(F32), -1.0, dtr.v(F32), ALU.mult, ALU.max, [dtr], [dtt])
            act(dtt.v(F32), dtt.v(F32), AF.Exp, [dtt], [dtt], scale=-1.0)
            act(dtt.v(F32), dtt.v(F32), AF.Ln, [dtt], [dtt], bias=1.0)
            stt(dtt.v(F32), dtr.v(F32), 0.0, dtt.v(F32), ALU.max, ALU.add, [dtr, dtt], [dtt])
            tt("dve", dA.v(F32, "p (c h) -> p c h", c=4), dtt.v(F32, "p (c h) -> p c h", c=4),
               a_bc.v(F32).unsqueeze(1).to_broadcast([128, 4, 32]), ALU.mult, [dtt, a_bc], [dA])
            dt3 = dtt.v(F32, "p (c h) -> p c h", c=4)
            dA3 = dA.v(F32, "p (c h) -> p c h", c=4)
            if stop_after == "b_dt":
                raise _Stop()
            halo3 = halo.v(BF16)[:, :72].rearrange("p (j t) -> p j t", j=24)
            cw = C["convw"].v(F32, "p (j k) -> p j k", j=24)
            cb = C["convb"].v(F32)
            ybT3 = ybT.v(BF16, "p (k t) -> p k t", k=16)
            st4 = state.v(F32, "p (g f) -> p g f", g=4)
            pv4 = prevbf.v(BF16, "p (g f) -> p g f", g=4)
            groupctx = {}

            P1t = P1.v(BF16, "p (k t) -> p k t", k=8)

            def xc_tile(g, li):
                if g < 2:
                    ap = P3.raw[:, g * 3072 + li * 512:g * 3072 + (li + 1) * 512]
                    return ap, (P3, ("xc", g, li), g * 6144 + li * 1024, g * 6144 + (li + 1) * 1024)
                idx = (g - 2) * 6 + li
                if idx < 8:
                    return P1t[:, idx, :], k1(idx)
                idx -= 8
                tb_ = t1[idx // 2]
                h = idx % 2
                return tb_.raw[:, h * 512:(h + 1) * 512], (tb_, ("xc", h), h * 1024, (h + 1) * 1024)

            def zb_view(g):
                if g < 2:
                    ap = P2.raw[:, g * 2048:(g + 1) * 2048]
                    key = k2zb(g)
                else:
                    ap = zbx.raw[:, (g - 2) * 2048:(g - 1) * 2048]
                    key = (zbx, g - 2, (g - 2) * 4096, (g - 1) * 4096)
                return ap.rearrange("p (c f) -> p c f", c=4), key

            def proj_group_gen(g, pool=None):
                wu = [load_unit("xs%d" % g)]
                s_bm = load_unit("bm%d" % g)
                s_cm = load_unit("cm%d" % g)
                tiles = [(wu[0], jj, g * 4 + jj) for jj in range(4)] + [(s_bm, 0, 16 + g), (s_cm, 0, 20 + g)]
                for li, ((slot, view), jj, jglob) in enumerate(tiles):
                    bk = fm_proj(view, hT3, jj, hTb, slot, pool)
                    rw = raw[(g * 6 + li) % 3]
                    rv = rw.v(BF16)
                    cp("act", rv[:, 3:515], bk.ap[:, :], [bk], [(rw, "b")])
                    cp("pool", rv[:, 0:3], halo3[:, jglob, :], [(halo, jglob)], [(rw, "h")])
                    ab = acc[(g * 6 + li) % 2]
                    av = ab.v(F32)
                    ts("dve", av, bk.ap[:, :], cw[:, jglob, 3:4], cb[:, jglob:jglob + 1], ALU.mult, ALU.add,
                       [bk, C["convw"], C["convb"]], [ab])
                    cp("pool", halo3[:, jglob, :], rv[:, 512:515], [(rw, "b")], [(halo, jglob)])
                    yield
                    for kk in (2, 1, 0):
                        stt(av, rv[:, kk:kk + 512], cw[:, jglob, kk:kk + 1], av, ALU.mult, ALU.add, [rw, ab, C["convw"]], [ab])
                    xap, xkey = xc_tile(g, li)
                    act(xap, av, AF.Silu, [ab], [xkey])
                    yield
                zb3, zkey = zb_view(g)
                slot, view = load_unit("zb%d" % g)
                for c in range(NCH):
                    bk = nextpm(pool)
                    for k in range(8):
                        mm(bk.ap[:, :], hT3[:, k, c * 128:(c + 1) * 128], view[:, k, :], k == 0, k == 7, [slot, hTb], [bk])
                    act(zb3[:, c, :], bk.ap[:, :], AF.Silu, [bk], [zkey])
                    yield

            def ssd_iter(g, c, q):
                zb3, zk = zb_view(g)
                xt_ = [xc_tile(g, li) for li in range(6)]
                hs = slice(g * 8, g * 8 + 8)
                ssb = ssd_s[q]
                sv_ = ssb.v(F32)
                cs_ap, ecs_ap, dec_ap, cd_ap = sv_[:, 0:8], sv_[:, 8:16], sv_[:, 16:24], sv_[:, 24:32]
                dAh = ssb.v(BF16)[:, 64:72]
                dAl = ssb.v(BF16)[:, 72:80]
                csl = slice(c * 128, (c + 1) * 128)
                pb_ = nextpt("pt%d" % q)
                bk = pb_
                pf = pb_.ap[:, :].bitcast(F32)
                bkap = pf[:, 320:336]
                mm(bkap[:, 0:8], tri_f, dA3[:, c, hs], True, True, [C["tri_f"], dA], [(bk, 0)])
                mm(bkap[:, 8:16], ones_f, dA3[:, c, hs], True, True, [C["ones_f"], dA], [(bk, 1)])
                cp("dve", dAh, dA3[:, c, hs], [dA], [(ssb, "hi")])
                tt("dve", dAl, dA3[:, c, hs], dAh, ALU.subtract, [dA, (ssb, "hi")], [(ssb, "lo")])
                for jj in range(4):
                    tp(pb_.ap[:, jj * 128:(jj + 1) * 128], xt_[jj][0][:, csl], [xt_[jj][1]], [(pb_, jj)])
                tp(pb_.ap[:, 512:640], xt_[4][0][:, csl], [xt_[4][1]], [(pb_, 4)])
                yield
                rhi = S4[q].v(BF16)[:, 0:1024].rearrange("p (h l) -> p h l", h=8)
                rlo = S4[q].v(BF16)[:, 1024:2048].rearrange("p (h l) -> p h l", h=8)
                trib = tri_b.unsqueeze(1).to_broadcast([128, 8, 128])
                tt("dve", rhi, trib, dAh.unsqueeze(2).to_broadcast([128, 8, 128]), ALU.mult, [C["tri_b"], (ssb, "hi")], [kS4(q, "a")])
                tt("dve", rlo, trib, dAl.unsqueeze(2).to_broadcast([128, 8, 128]), ALU.mult, [C["tri_b"], (ssb, "lo")], [kS4(q, "b")])
                cp("dve", cs_ap, bkap[:, 0:8], [bk], [(ssb, "cs")])
                tt("dve", dec_ap, bkap[:, 8:16], cs_ap, ALU.subtract, [bk, (ssb, "cs")], [(ssb, "dec")])
                act(ecs_ap, bkap[:, 0:8], AF.Exp, [bk], [(ssb, "ecs")])
                act(cd_ap, bkap[:, 8:16], AF.Exp, [bk], [(ssb, "cd")])
                act(dec_ap, dec_ap, AF.Exp, [(ssb, "dec")], [(ssb, "dec")])
                yield
                xv = pb_.ap[:, 0:512].rearrange("p (h d) -> p h d", h=8)
                tt("dve", Xdt[q].v(BF16, "p (h d) -> p h d", h=8), xv,
                   dt3[:, c, hs].unsqueeze(2).to_broadcast([128, 8, 64]), ALU.mult, [pb_, dtt], [Xdt[q]])
                tt("dve", XD[q].v(BF16, "p (h d) -> p h d", h=8), xv,
                   C["dskip_bc"].v(F32)[:, hs].unsqueeze(2).to_broadcast([128, 8, 64]), ALU.mult, [pb_, C["dskip_bc"]], [XD[q]])
                cp("dve", Btm[q].v(BF16), pb_.ap[:, 512:640], [pb_], [Btm[q]])
                bk3 = pb_
                bk3ap = pf[:, 336:464]
                mm(bk3ap, xt_[4][0][:, csl], xt_[5][0][:, csl], True, True, [xt_[4][1], xt_[5][1]], [bk3])
                yield
                L3 = Lb[q].v(BF16, "p (h l) -> p h l", h=8)
                for hh in range(2):
                    bk2 = nextpm("pm%d" % q)
                    mm(bk2.ap[:, :], sut_b, S4[q].v(BF16)[:, hh * 512:(hh + 1) * 512], True, False, [C["sut_b"], kS4(q, "a")], [bk2])
                    mm(bk2.ap[:, :], sut_b, S4[q].v(BF16)[:, 1024 + hh * 512:1024 + (hh + 1) * 512], False, True, [C["sut_b"], kS4(q, "b")], [bk2])
                    act(Lb[q].v(BF16)[:, hh * 512:(hh + 1) * 512], bk2.ap[:, :], AF.Exp, [bk2], [kL(q, hh)])
                tt("dve", CBm[q].v(BF16), bk3ap, tri_b, ALU.mult, [bk3, C["tri_b"]], [CBm[q]])
                yield
                G3 = Gb[q].v(BF16, "p (h l) -> p h l", h=8)
                tt("dve", G3, L3, CBm[q].v(BF16).unsqueeze(1).to_broadcast([128, 8, 128]), ALU.mult, [Lb[q], CBm[q]], [Gb[q]])
                tt("dve", XdD[q].v(BF16, "p (h d) -> p h d", h=8), Xdt[q].v(BF16, "p (h d) -> p h d", h=8),
                   dec_ap.unsqueeze(2).to_broadcast([128, 8, 64]), ALU.mult, [Xdt[q], (ssb, "dec")], [XdD[q]])
                yield
                bko = nextpm("pm%d" % q)
                mm(bko.ap[:, :], xt_[5][0][:, csl], pv4[:, g, :], True, True, [xt_[5][1], (prevbf, g)], [bko])
                bkd = nextpm("pm%d" % q)
                mm(bkd.ap[:, :], ident_b, XD[q].v(BF16), True, False, [C["ident_b"], XD[q]], [bkd])
                for r in range(8):
                    mm(bkd.ap[:, r * 64:(r + 1) * 64], G3[:, r, :], Xdt[q].v(BF16)[:, r * 64:(r + 1) * 64], False, r == 7, [Gb[q], Xdt[q]], [bkd])
                tt("pool", tmpst[q].v(F32, "p (h d) -> p h d", h=8), st4[:, g, :].rearrange("p (h d) -> p h d", h=8),
                   cd_ap.unsqueeze(2).to_broadcast([128, 8, 64]), ALU.mult, [(state, g), (ssb, "cd")], [tmpst[q]])
                yield
                yv = y1[q].v(F32)
                tt("dve", y1[q].v(F32, "p (h d) -> p h d", h=8), bko.ap.rearrange("p (h d) -> p h d", h=8),
                   ecs_ap.unsqueeze(2).to_broadcast([128, 8, 64]), ALU.mult, [bko, (ssb, "ecs")], [y1[q]])
                tt("dve", yv, yv, bkd.ap[:, :], ALU.add, [bkd, y1[q]], [y1[q]])
                bks = nextpm("pm%d" % q)
                mm(bks.ap[:, :], Btm[q].v(BF16), XdD[q].v(BF16), True, True, [Btm[q], XdD[q]], [bks])
                yield
                tt("dve", st4[:, g, :], tmpst[q].v(F32), bks.ap[:, :], ALU.add, [bks, tmpst[q]], [(state, g)])
                cp("act", pv4[:, g, :], st4[:, g, :], [(state, g)], [(prevbf, g)])
                yield
                tt("pool", yv, yv, zb3[:, c, :], ALU.mult, [y1[q], zk], [y1[q]])
                act(ybb[q].v(BF16), yv, AF.Square, [y1[q]], [(ssb, "ss"), ybb[q]], accum=sv_[:, 50:51])
                yield
                rsqrt_chain(sv_[:, 51:52], sv_[:, 50:51], 1.0 / 512, (ssb, "ss"), (ssb, "rs"))
                yield
                act(ybb[q].v(BF16), yv, AF.Identity, [y1[q], (ssb, "rs")], [ybb[q]], scale=sv_[:, 51:52])
                pb2 = nextpt("pt%d" % q)
                for jj in range(4):
                    tp(pb2.ap[:, jj * 128:(jj + 1) * 128], ybb[q].v(BF16)[:, jj * 128:(jj + 1) * 128], [ybb[q]], [(pb2, jj)])
                cp("act", ybT3[:, g * 4:(g + 1) * 4, csl], pb2.ap[:, 0:512].rearrange("p (k t) -> p k t", k=4), [pb2], [(ybT, (g, c))])
                yield

            def stream(g, q):
                for c in range(NCH):
                    yield from ssd_iter(g, c, q)

            def zipper(gens):
                gens = list(gens)
                while gens:
                    for gen in list(gens):
                        try:
                            next(gen)
                        except StopIteration:
                            gens.remove(gen)

            def proj_pair(pair, pool=None):
                yield from proj_group_gen(2 * pair, pool)
                yield from proj_group_gen(2 * pair + 1, pool)

            mark("b_proj0", ti)
            zipper([proj_pair(0)])
            mark("b_ssd0", ti)
            zipper([stream(0, 0), stream(1, 1), proj_pair(1, "pm2")])
            mark("b_ssd1", ti)
            zipper([stream(2, 0), stream(3, 1)])

        def ob_merge(ti):
            gens = [ob_merge_gen(ti)]
            if ti + 1 < n_tiles:
                gens.append(stage_a_gen(ti + 1))
            zipper(gens)

        def ob_merge_gen(ti):
            sl = ti % 2
            hTb = hT[sl]
            hT3 = hTb.v(BF16, "p (k t) -> p k t", k=8)
            mg3 = mg.v(BF16, "p (k t) -> p k t", k=8)
            ybT3 = ybT.v(BF16, "p (k t) -> p k t", k=16)
            sgb = [None, None]
            for j in range(8):
                i, jj = j // 4, j % 4
                if jj == 0:
                    sgb[i] = load_unit("gb%d" % i)
                if j % 2 == 0:
                    sob = load_unit("ob%d" % (j // 2))
                    obv = sob[0].v(BF16)[:, :16 * 256].rearrange("p (k c) -> p k c", k=16)
                bk = fm_proj(sgb[i][1], hT3, jj, hTb, sgb[i][0])
                gt = gab[j % 2]
                act(gt.v(BF16), bk.ap[:, :], AF.Sigmoid, [bk], [gt])
                bk2 = nextpm()
                for k in range(16):
                    mm(bk2.ap[:, :], obv[:, k, (j % 2) * 128:(j % 2 + 1) * 128], ybT3[:, k, :], k == 0, k == 15, [sob[0], ybT], [bk2])
                tb = t1[j % 2]
                tt("dve", tb.v(F32), bk2.ap[:, :], gt.v(BF16), ALU.mult, [bk2, gt], [tb])
                tt("pool", mg3[:, j, :], mg3[:, j, :], tb.v(F32), ALU.add, [(mg, j), tb], [(mg, j)])
                yield

        def tail(ti):
            mg3 = mg.v(BF16, "p (k t) -> p k t", k=8)
            x1 = P3.v(F32, "p (c d) -> p c d", c=4)
            hpT3 = P1.v(BF16, "p (k t) -> p k t", k=8)
            gate3 = P2.v(BF16, "p (c d) -> p c d", c=4)
            pT3 = Lb[0].v(BF16)[:, :1024].rearrange("p (k t) -> p k t", k=2)
            for c in range(NCH):
                r0 = ti * T + c * 128
                dma("sp", x1[:, c, :], dram["x"][r0:r0 + 128, :], [], [k3x1(c)], "x1_%d" % c)
            for i in range(2):
                slot, view = load_unit("out%d" % i)
                for c in range(NCH):
                    bk = nextpm()
                    for k in range(8):
                        mm(bk.ap[:, :], mg3[:, k, c * 128:(c + 1) * 128], view[:, k, :], k == 0, k == 7, [slot, mg], [bk])
                    tt("dve", x1[:, c, i * 512:(i + 1) * 512], x1[:, c, i * 512:(i + 1) * 512], bk.ap[:, :], ALU.add,
                       [bk, k3x1(c)], [k3x1(c)])
            for c in range(NCH):
                ss = small[2]
                hpb = acc[c % 2]
                act(hpb.v(BF16)[:, :1024], x1[:, c, :], AF.Square, [k3x1(c)], [(ss, 0), hpb], accum=ss.v(F32)[:, 0:1])
                rsqrt_chain(ss.v(F32)[:, 1:2], ss.v(F32)[:, 0:1], 1.0 / D, (ss, 0), (ss, 1))
                act(hpb.v(BF16)[:, :1024], x1[:, c, :], AF.Identity, [k3x1(c), (ss, 1)], [hpb], scale=ss.v(F32)[:, 1:2])
                pb_ = nextpt()
                for k in range(8):
                    tp(pb_.ap[:, k * 128:(k + 1) * 128], hpb.v(BF16)[:, k * 128:(k + 1) * 128], [hpb], [(pb_, k)])
                cp("dve", hpT3[:, :, c * 128:(c + 1) * 128], pb_.ap.rearrange("p (k t) -> p k t", k=8), [pb_], [k1hp(c)])
                r0 = ti * T + c * 128
                ptb_ = Xdt[c % 2]
                pbb = XD[c % 2]
                dma("sp", ptb_.v(F32), dram["p"][r0:r0 + 128, :], [], [ptb_], "pt%d" % (c % 2))
                cp("dve", pbb.v(BF16)[:, :256], ptb_.v(F32), [ptb_], [pbb])
                pb3 = nextpt()
                for k in range(2):
                    tp(pb3.ap[:, k * 128:(k + 1) * 128], pbb.v(BF16)[:, k * 128:(k + 1) * 128], [pbb], [(pb3, k)])
                cp("act", pT3[:, :, c * 128:(c + 1) * 128], pb3.ap[:, 0:256].rearrange("p (k t) -> p k t", k=2), [pb3], [kLpT(c)])
            for i in range(2):
                slot, view = load_unit("pg%d" % i)
                for c in range(NCH):
                    bk = nextpm()
                    for k in range(8):
                        mm(bk.ap[:, :], hpT3[:, k, c * 128:(c + 1) * 128], view[:, k, :], k == 0, k == 7, [slot, P1], [bk])
                    act(gate3[:, c, i * 512:(i + 1) * 512], bk.ap[:, :], AF.Sigmoid, [bk], [k2gate(c, i)])
            for i in range(2):
                slot, view0 = load_unit("ple%d" % i)
                view = slot.v(BF16)[:, :1024].rearrange("p (k c) -> p k c", k=2)
                for c in range(NCH):
                    bk = nextpm()
                    for k in range(2):
                        mm(bk.ap[:, :], pT3[:, k, c * 128:(c + 1) * 128], view[:, k, :], k == 0, k == 1, [slot, Lb[0]], [bk])
                    tb = t1[(i * 4 + c) % 2]
                    tt("dve", tb.v(F32), bk.ap[:, :], gate3[:, c, i * 512:(i + 1) * 512], ALU.mult, [bk, k2gate(c, i)], [tb])
                    tt("pool", x1[:, c, i * 512:(i + 1) * 512], x1[:, c, i * 512:(i + 1) * 512], tb.v(F32), ALU.add,
                       [k3x1(c), tb], [k3x1(c)])
            for c in range(NCH):
                ss = small[3]
                act(acc[c % 2].v(BF16)[:, :1024], x1[:, c, :], AF.Square, [k3x1(c)], [(ss, 0), acc[c % 2]], accum=ss.v(F32)[:, 0:1])
                rsqrt_chain(ss.v(F32)[:, 1:2], ss.v(F32)[:, 0:1], 1.0 / D, (ss, 0), (ss, 1))
                ot = S4[c % 2]
                stt(ot.v(F32), x1[:, c, :], ss.v(F32)[:, 1:2], C["fg_bc"].v(F32), ALU.mult, ALU.mult,
                    [k3x1(c), (ss, 1), C["fg_bc"]], [ot])
                r0 = ti * T + c * 128
                so = dma("pool", out_d[r0:r0 + 128, :], ot.v(F32), [ot], [], "ot%d" % (c % 2))
                store_ops.append(so)

        env = locals()

        def dbg_hook(hook):
            if debug is None or hook not in debug:
                return
            for (nm, ncols, fn) in debug[hook]:
                ap, reads = fn(env)
                so = dma("pool", dbg_d[nm][:, :], ap, reads, [], "dbg_" + nm)
                store_ops.append(so)

        phase_marks = []
        build.phase_marks = phase_marks

        def mark(name, ti):
            phase_marks.append((name, ti, len(P.ops["pe"]), len(P.ops["act"]), len(P.ops["dve"]), len(P.ops["pool"])))

        if stop_after != "prologue":
            stage_a(0)
        for ti in range(n_tiles):
          try:
              if stop_after in ("prologue", "stage_a"):
                  break
              mark("a_branch", ti)
              a_branch(ti)
              mark("b_branch", ti)
              if ti == 0:
                  dbg_hook("after_a")
              if stop_after == "a_branch":
                  break
              b_branch(ti)
              if stop_after is not None and stop_after.startswith("b_"):
                  break
              if ti == 0:
                  dbg_hook("after_b")
              mark("stage_a", ti)
              mark("ob_merge", ti)
              ob_merge(ti)
              mark("tail", ti)
              if ti == 0:
                  dbg_hook("after_merge")
              tail(ti)
              mark("end", ti)
              if ti == 0:
                  dbg_hook("end")
          except _Stop:
            dbg_hook("stop")
            break

        for so in store_ops:
            so.needed = True
        P.finalize()
        counts = {k: len(v) for k, v in P.ops.items()}
        print("op counts", counts)
        blk = es.enter_context(nc.Block())

        @blk.tensor
        def _(e):
            P.emit("pe", e)

        @blk.scalar
        def _(e):
            P.emit("act", e)

        @blk.vector
        def _(e):
            P.emit("dve", e)

        @blk.gpsimd
        def _(e):
            P.emit("pool", e, final_waits=store_ops)

        @blk.sync
        def _(e):
            P.emit("sp", e)
    return nc


def host_consts(inp):
    f = np.float32
    k = np.arange(128)
    c = {}
    c["ident_b"] = np.eye(128, dtype=f)
    tri = (k[:, None] <= k[None, :]).astype(f)
    c["tri_b"] = tri
    c["tri_f"] = tri
    c["sut_b"] = (k[:, None] > k[None, :]).astype(f)
    c["ones_f"] = np.ones((128, 128), f)

    def pk(v):
        return np.ascontiguousarray(np.asarray(v, f).reshape(-1, 128).T)
    c["ng"] = pk(inp["norm_g"][0])
    c["png"] = pk(inp["ple_norm_g"][0])
    c["sng"] = pk(inp["ssm_norm_g"][0])
    c["lng"] = pk(inp["ln_a_g"][0])
    c["lnb"] = pk(inp["ln_a_b"][0])
    cw = np.asarray(inp["conv_w"][0], f)
    c["convw"] = np.ascontiguousarray(cw.reshape(4, 24, 128).transpose(2, 1, 0)).reshape(128, 96)
    c["convb"] = pk(inp["conv_b"][0])

    def bc(v):
        v = np.asarray(v, f).reshape(1, -1)
        return np.ascontiguousarray(np.broadcast_to(v, (128, v.shape[1])))
    c["fg_bc"] = bc(inp["final_g"])
    c["dtb_bc"] = bc(inp["dt_bias"][0])
    c["alog_bc"] = bc(inp["a_log"][0])
    c["dskip_bc"] = bc(inp["d_skip"][0])
    ws = np.asarray(inp["w_s"][0], f)
    c["wsT"] = np.ascontiguousarray(ws.transpose(2, 0, 1)).reshape(128, 512)
    c["bs_bc"] = bc(np.asarray(inp["b_s"][0], f).reshape(-1))
    return c


def make_in_maps(inp, n_cores, n_tiles):
    c = host_consts(inp)
    ntok = n_tiles * T
    base = {
        "w_in": np.ascontiguousarray(np.asarray(inp["w_in"][0], np.float32)),
        "w_oa": np.ascontiguousarray(np.asarray(inp["w_oa"][0], np.float32)),
        "w_ob": np.ascontiguousarray(np.asarray(inp["w_ob"][0], np.float32)),
        "w_out": np.ascontiguousarray(np.asarray(inp["w_out"][0], np.float32)),
        "w_pg": np.ascontiguousarray(np.asarray(inp["w_pg"][0], np.float32)),
        "w_ple": np.ascontiguousarray(np.asarray(inp["w_ple"][0], np.float32)),
    }
    for k, v in c.items():
        base["c_" + k] = v
    maps = []
    for b in range(n_cores):
        m = dict(base)
        m["x"] = np.ascontiguousarray(np.asarray(inp["x"][b, :ntok], np.float32))
        m["p"] = np.ascontiguousarray(np.asarray(inp["p"][0, b, :ntok], np.float32))
        maps.append(m)
    return maps


def kernel(**inputs):
    n_tiles = SEQ // T
    nc = build(n_tiles)
    maps = make_in_maps(inputs, NB, n_tiles)
    res = run_bass_kernel_spmd(nc, maps, core_ids=list(range(NB)))
    out = np.stack([np.asarray(r["out"], np.float32) for r in res.results], axis=0)
    return out
```

```python
import numpy as np
from contextlib import ExitStack
import concourse.bass as bass
import concourse.mybir as mybir
from concourse.bass_utils import run_bass_kernel_spmd

F32 = mybir.dt.float32
BF16 = mybir.dt.bfloat16
AF = mybir.ActivationFunctionType
ALU = mybir.AluOpType

D = 1024
SEQ = 8192
NB = 8
PLE = 256
T = 512
NCH = 4
EPS = 1e-6
NIN = 10272
C_U, C_V, C_ZA, C_ZB, C_XBC, C_DT, C_GA, C_GB = 0, 1024, 2048, 3072, 5120, 8192, 8224, 9248
MAXV = 30000
import os
STOPC = int(os.environ.get('STOPC', '0'))
USE_POW = int(os.environ.get('USE_POW', '0'))
NO_SELF = tuple(x for x in os.environ.get('NO_SELF', '').split(',') if x)


class _Stop(Exception):
    pass


class Buf:
    def __init__(self, name, raw=None, nbytes=0):
        self.name = name
        self.raw = raw
        self.nbytes = nbytes
        self.recs = {}
        self.ranges = {}

    def v(self, dt=BF16, pat=None, **kw):
        ap = self.raw if dt == BF16 else self.raw.bitcast(dt)
        if pat is not None:
            ap = ap.rearrange(pat, **kw)
        return ap


class Rec:
    __slots__ = ("writer", "readers")

    def __init__(self):
        self.writer = None
        self.readers = []


class Op:
    __slots__ = ("eng", "fn", "deps", "dma", "token", "needed", "seq")


def _acc(a):
    if not isinstance(a, tuple):
        return (a, None)
    if len(a) == 4:
        a[0].ranges[a[1]] = (a[2], a[3])
        return (a[0], a[1])
    return a


def _conf(buf, k1, k2):
    if k1 == k2 or k1 is None or k2 is None:
        return True
    r1 = buf.ranges.get(k1)
    r2 = buf.ranges.get(k2)
    if r1 is None or r2 is None:
        return False
    return r1[0] < r2[1] and r2[0] < r1[1]


class Prog:
    def __init__(self, nc, es):
        self.nc = nc
        self.es = es
        self.ops = {e: [] for e in ("pe", "act", "dve", "pool", "sp")}
        self.dma_sems = {}
        self.dma_counts = {}
        self.all_ops = []

    def _rec(self, buf, key):
        r = buf.recs.get(key)
        if r is None:
            r = buf.recs[key] = Rec()
        return r

    def add(self, eng, fn, reads=(), writes=(), dma=None, reg_reads=True):
        op = Op()
        op.eng = eng
        op.fn = fn
        op.dma = dma is not None
        op.needed = False
        op.token = None
        deps = []
        xr = [a for a in reads if getattr(_acc(a)[0], "excl", False)]
        if xr:
            reads = [a for a in reads if not getattr(_acc(a)[0], "excl", False)]
            writes = list(writes) + [_acc(a)[0] for a in xr]
        writes = [(_acc(a)[0] if getattr(_acc(a)[0], "excl", False) else a) for a in writes]
        for a in reads:
            buf, key = _acc(a)
            for k, r in buf.recs.items():
                if r.writer is not None and _conf(buf, key, k):
                    deps.append(r.writer)
        for a in writes:
            buf, key = _acc(a)
            for k, r in buf.recs.items():
                if _conf(buf, key, k):
                    if r.writer is not None:
                        deps.append(r.writer)
                    deps.extend(r.readers)
        for a in (reads if reg_reads else ()):
            buf, key = _acc(a)
            r = self._rec(buf, key)
            if not op.dma:
                r.readers = [x for x in r.readers if x.dma or x.eng != eng]
            r.readers.append(op)
        for a in writes:
            buf, key = _acc(a)
            if key is None:
                buf.recs = {}
            r = self._rec(buf, key)
            r.writer = op
            r.readers = []
        ud = []
        seen = set()
        for d in deps:
            if id(d) in seen or d is op:
                continue
            seen.add(id(d))
            if not (eng == "pe" and d.eng == "pe" and not d.dma and not op.dma):
                d.needed = True
            ud.append(d)
        op.deps = ud
        if op.dma:
            if dma not in self.dma_sems:
                self.dma_sems[dma] = self.es.enter_context(self.nc.semaphore("d_" + dma))
                self.dma_counts[dma] = 0
            self.dma_counts[dma] += 16
            op.token = (self.dma_sems[dma], self.dma_counts[dma])
        self.ops[eng].append(op)
        self.all_ops.append(op)
        return op

    def finalize(self):
        self.eng_sems = {}
        for eng, lst in self.ops.items():
            n = 0
            sems = []
            for op in lst:
                if op.dma or not op.needed:
                    continue
                ep = n // MAXV
                if ep >= len(sems):
                    sems.append(self.es.enter_context(self.nc.semaphore("e_%s%d" % (eng, ep))))
                op.token = (sems[ep], n % MAXV + 1)
                n += 1
            self.eng_sems[eng] = sems

    def emit(self, engname, e, final_waits=()):
        seen = {}
        for op in self.ops[engname]:
            need = {}
            for d in op.deps:
                if (not d.dma) and d.eng == engname and (engname == "pe" or engname in NO_SELF):
                    continue
                sem, val = d.token
                k = id(sem)
                if k not in need or need[k][1] < val:
                    need[k] = (sem, val)
            for k, (sem, val) in need.items():
                if seen.get(k, 0) >= val:
                    continue
                e.wait_ge(sem, val)
                seen[k] = val
            ins = op.fn(e)
            if op.dma:
                ins.then_inc(op.token[0], 16)
            elif op.needed:
                ins.then_inc(op.token[0], 1)
        for d in final_waits:
            sem, val = d.token
            if seen.get(id(sem), 0) >= val:
                continue
            e.wait_ge(sem, val)
            seen[id(sem)] = val


def weight_units():
    u = []
    for i in range(2):
        u.append(("u%d" % i, "w_in", 8, C_U + 512 * i, 512, "ng"))
    for i in range(2):
        u.append(("za%d" % i, "w_in", 8, C_ZA + 512 * i, 512, "ng"))
    for i in range(2):
        u.append(("v%d" % i, "w_in", 8, C_V + 512 * i, 512, "ng"))
    for i in range(2):
        u.append(("ga%d" % i, "w_in", 8, C_GA + 512 * i, 512, "ng"))
    for i in range(2):
        u.append(("oa%d" % i, "w_oa", 8, 512 * i, 512, None))
    u.append(("dt", "w_in", 8, C_DT, 32, "ng"))
    for g in range(4):
        u.append(("xs%d" % g, "w_in", 8, C_XBC + 512 * g, 512, "ng"))
    for g in range(4):
        u.append(("bm%d" % g, "w_in", 8, C_XBC + 2048 + 128 * g, 128, "ng"))
        u.append(("cm%d" % g, "w_in", 8, C_XBC + 2560 + 128 * g, 128, "ng"))
    for g in range(4):
        u.append(("zb%d" % g, "w_in", 8, C_ZB + 512 * g, 512, "ng"))
    for i in range(2):
        u.append(("gb%d" % i, "w_in", 8, C_GB + 512 * i, 512, "ng"))
    for i in range(4):
        u.append(("ob%d" % i, "w_ob", 16, 256 * i, 256, "sng"))
    for i in range(2):
        u.append(("out%d" % i, "w_out", 8, 512 * i, 512, None))
    for i in range(2):
        u.append(("pg%d" % i, "w_pg", 8, 512 * i, 512, "png"))
    for i in range(2):
        u.append(("ple%d" % i, "w_ple", 2, 512 * i, 512, None))
    return u


CONST_SPECS = [
    ("ident_b", [128], BF16), ("tri_b", [128], BF16), ("sut_b", [128], BF16),
    ("tri_f", [128], F32), ("ones_f", [128], F32),
    ("ng", [8], F32), ("png", [8], F32), ("sng", [16], F32), ("lng", [8], F32), ("lnb", [8], F32),
    ("convw", [96], F32), ("convb", [24], F32),
    ("fg_bc", [1024], F32), ("dtb_bc", [32], F32), ("alog_bc", [32], F32), ("dskip_bc", [32], F32),
    ("wsT", [512], F32), ("bs_bc", [512], F32),
]


def build(n_tiles, debug=None, stop_after=None):
    nc = bass.Bass("TRN2", target_bir_lowering=False)
    ntok = n_tiles * T
    dram = {}
    dram["x"] = nc.dram_tensor("x", [ntok, D], F32, kind="ExternalInput").ap()
    dram["p"] = nc.dram_tensor("p", [ntok, PLE], F32, kind="ExternalInput").ap()
    dram["w_in"] = nc.dram_tensor("w_in", [D, NIN], F32, kind="ExternalInput").ap()
    dram["w_oa"] = nc.dram_tensor("w_oa", [D, D], F32, kind="ExternalInput").ap()
    dram["w_ob"] = nc.dram_tensor("w_ob", [2 * D, D], F32, kind="ExternalInput").ap()
    dram["w_out"] = nc.dram_tensor("w_out", [D, D], F32, kind="ExternalInput").ap()
    dram["w_pg"] = nc.dram_tensor("w_pg", [D, D], F32, kind="ExternalInput").ap()
    dram["w_ple"] = nc.dram_tensor("w_ple", [PLE, D], F32, kind="ExternalInput").ap()
    for name, shp, dt in CONST_SPECS:
        dram[name] = nc.dram_tensor("c_" + name, [128] + shp, F32, kind="ExternalInput").ap()
    out_d = nc.dram_tensor("out", [ntok, D], F32, kind="ExternalOutput").ap()
    units = weight_units()
    uoff = {}
    off = 0
    for (nm, src, K, c0, ncol, rs) in units:
        uoff[nm] = off
        off += 128 * K * ncol
    wq = nc.dram_tensor("wq", [off], BF16, kind="Internal").ap()
    dbg_d = {}
    if debug is not None:
        for hook, lst in debug.items():
            for (nm, ncols, fn) in lst:
                dbg_d[nm] = nc.dram_tensor("dbg_" + nm, [128, ncols], F32, kind="ExternalOutput").ap()

    es = ExitStack()
    with es:
        P = Prog(nc, es)
        ARENA = 206 * 1024
        arena = es.enter_context(nc.sbuf_tensor("arena", [128, ARENA // 2], BF16))
        apos = [0]

        def alloc(name, nbytes):
            nbytes = (nbytes + 63) // 64 * 64
            o = apos[0]
            apos[0] += nbytes
            assert apos[0] <= ARENA, (name, apos[0])
            return Buf(name, arena[:, o // 2:(o + nbytes) // 2], nbytes)

        pm = []
        for i in range(6):
            t = es.enter_context(nc.psum_tensor("pm%d" % i, [128, 512], F32))
            b = Buf("pm%d" % i)
            b.ap = t
            b.excl = True
            pm.append(b)
        ptb = []
        for i in range(2):
            t = es.enter_context(nc.psum_tensor("pt%d" % i, [128, 1024], BF16))
            b = Buf("pt%d" % i)
            b.ap = t
            b.excl = True
            ptb.append(b)
        pmi = [0]
        pti = [0]

        spool = {"pm0": [0, [0, 1, 2]], "pm1": [0, [3, 4, 5]], "pt0": [0, [0]], "pt1": [0, [1]]}

        def nextpm(pool=None):
            if pool is not None:
                st = spool[pool]
                b = pm[st[1][st[0] % len(st[1])]]
                st[0] += 1
                return b
            b = pm[pmi[0] % 6]
            pmi[0] += 1
            return b

        def nextpt(pool=None):
            if pool is not None:
                st = spool[pool]
                b = ptb[st[1][st[0] % len(st[1])]]
                st[0] += 1
                return b
            b = ptb[pti[0] % 2]
            pti[0] += 1
            return b

        xa = [alloc("xa%d" % i, 4096) for i in range(2)]
        hT = [alloc("hT%d" % i, 8192) for i in range(2)]
        hb = [alloc("hb%d" % i, 2048) for i in range(2)]
        mg = alloc("mg", 8192)
        ybT = alloc("ybT", 16384)
        state = alloc("state", 8192)
        prevbf = alloc("prevbf", 4096)
        ring = [alloc("ring%d" % i, 8192) for i in range(4)]
        wdt = alloc("wdt", 512)
        dtr = alloc("dtr", 512)
        dtt = alloc("dtt", 512)
        dA = alloc("dA", 512)
        halo = alloc("halo", 24 * 3 * 2)
        junk = alloc("junk", 2048)
        small = [alloc("small%d" % i, 256) for i in range(4)]
        P1 = alloc("P1", 8192)
        P2 = alloc("P2", 8192)
        P3 = alloc("P3", 16384)
        t1 = [alloc("t1_%d" % i, 2048) for i in range(2)]
        gab = [alloc("gab%d" % i, 1024) for i in range(2)]
        raw = [alloc("raw%d" % i, 515 * 2) for i in range(3)]
        acc = [alloc("acc%d" % i, 2048) for i in range(2)]
        Xdt = [alloc("Xdt%d" % i, 1024) for i in range(2)]
        XD = [alloc("XD%d" % i, 1024) for i in range(2)]
        XdD = [alloc("XdD%d" % i, 1024) for i in range(2)]
        Btm = [alloc("Btm%d" % i, 256) for i in range(2)]
        S4 = [alloc("S4_%d" % i, 4096) for i in range(2)]
        Lb = [alloc("L%d" % i, 2048) for i in range(2)]
        Gb = [alloc("G%d" % i, 2048) for i in range(2)]
        CBm = [alloc("CBm%d" % i, 256) for i in range(2)]
        y1 = [alloc("y1_%d" % i, 2048) for i in range(2)]
        ybb = [alloc("yb%d" % i, 1024) for i in range(2)]
        tmpst = [alloc("tmp%d" % i, 2048) for i in range(2)]
        ssd_s = [alloc("ssd_s%d" % i, 1024) for i in range(2)]
        C = {}
        for name, shp, dt in CONST_SPECS:
            n = shp[0]
            if name == "wsT":
                C[name] = y1[0]
            elif name == "bs_bc":
                C[name] = y1[1]
            else:
                C[name] = alloc("c_" + name, n * (2 if dt == BF16 else 4))
        a_bc = alloc("a_bc", 32 * 4)
        wsTm_f = tmpst[0]
        wsTm_b = alloc("wsTm_b", 512 * 2)
        BR = alloc("BR", 1024 * 4)
        neghalf = alloc("neghalf", 64)
        print("arena used", apos[0])

        def k1(j): return (P1, j, j * 1024, (j + 1) * 1024)
        def k1hp(c): return (P1, ("hp", c), 0, 8192)
        def k2(j): return (P2, j, j * 1024, (j + 1) * 1024)
        def k2zb(s): return (P2, ("zb", s), s * 4096, (s + 1) * 4096)
        def k2gate(c, i): return (P2, ("gate", c, i), c * 2048 + i * 1024, c * 2048 + (i + 1) * 1024)
        def k3vg(s): return (P3, ("vg", s), s * 4096, (s + 1) * 4096)
        def k3vn(c): return (P3, ("vn", c), 8192 + c * 2048, 8192 + (c + 1) * 2048)
        def k3xc(s): return (P3, ("xc", s), s * 6144, (s + 1) * 6144)
        def k3x1(c): return (P3, ("x1", c), c * 4096, (c + 1) * 4096)
        def kL(q, hh): return (Lb[q], hh, hh * 1024, (hh + 1) * 1024)
        def kLpT(c): return (Lb[0], ("pT", c), 0, 2048)
        def kS4(q, ab): return (S4[q], ab, 0, 2048) if ab == "a" else (S4[q], ab, 2048, 4096)

        ident_b = C["ident_b"].v(BF16)
        tri_b = C["tri_b"].v(BF16)
        sut_b = C["sut_b"].v(BF16)
        tri_f = C["tri_f"].v(F32)
        ones_f = C["ones_f"].v(F32)

        ssem_cnt = [0]
        store_ops = []

        def act(out, in_, func, reads, writes, bias=None, scale=None, accum=None):
            kw = {}
            if bias is not None:
                kw["bias"] = bias
            if scale is not None:
                kw["scale"] = scale
            if accum is not None:
                kw["accum_out"] = accum
            return P.add("act", lambda e: e.activation(out=out, in_=in_, func=func, **kw), reads, writes)

        def tt(eng, out, in0, in1, op, reads, writes):
            return P.add(eng, lambda e: e.tensor_tensor(out=out, in0=in0, in1=in1, op=op), reads, writes)

        def ts(eng, out, in0, s1, s2, op0, op1, reads, writes):
            if op1 is None:
                return P.add(eng, lambda e: e.tensor_scalar(out=out, in0=in0, scalar1=s1, scalar2=None, op0=op0), reads, writes)
            return P.add(eng, lambda e: e.tensor_scalar(out=out, in0=in0, scalar1=s1, scalar2=s2, op0=op0, op1=op1), reads, writes)

        def stt(out, in0, scalar, in1, op0, op1, reads, writes):
            return P.add("dve", lambda e: e.scalar_tensor_tensor(out=out, in0=in0, scalar=scalar, in1=in1, op0=op0, op1=op1), reads, writes)

        def cp(eng, out, in_, reads, writes):
            if eng == "act":
                return P.add("act", lambda e: e.copy(out=out, in_=in_), reads, writes)
            return P.add(eng, lambda e: e.tensor_copy(out=out, in_=in_), reads, writes)

        mm_pend = []

        def mm(out, lhsT, rhs, start, stop, reads, writes):
            if not stop:
                mm_pend.extend(reads)
                return P.add("pe", lambda e: e.matmul(out, lhsT, rhs, start=start, stop=stop), reads, writes, reg_reads=False)
            allr = list(reads) + list(mm_pend)
            del mm_pend[:]
            return P.add("pe", lambda e: e.matmul(out, lhsT, rhs, start=start, stop=stop), allr, writes)

        def tp(out, in_, reads, writes, ident=None):
            idn = ident_b if ident is None else ident
            return P.add("pe", lambda e: e.transpose(out, in_, idn), list(reads) + [C["ident_b"]], writes)

        def dma(q, out, in_, reads, writes, sem):
            return P.add(q, lambda e: e.dma_start(out=out, in_=in_), reads, writes, dma=sem)

        def rsqrt_chain(dst_ap, src_ap, mulc, srcbuf, dstbuf):
            ts("dve", dst_ap, src_ap, mulc, EPS, ALU.mult, ALU.add, [srcbuf], [dstbuf])
            act(dst_ap, dst_ap, AF.Ln, [dstbuf], [dstbuf])
            act(dst_ap, dst_ap, AF.Exp, [dstbuf], [dstbuf], scale=-0.5)

        for name, shp, dt in CONST_SPECS:
            b = C[name]
            dma("pool", b.v(dt)[:, :shp[0]], dram[name], [], [b], "const")
        P.add("pool", lambda e: e.memset(junk.v(BF16)[:, 0:8], 0.0), list(C.values()), list(C.values()))
        act(a_bc.v(F32), C["alog_bc"].v(F32), AF.Exp, [C["alog_bc"]], [a_bc])
        ts("dve", a_bc.v(F32), a_bc.v(F32), -1.0, None, ALU.mult, None, [a_bc], [a_bc])
        wsT3 = C["wsT"].v(F32, "p (g t) -> p g t", g=4)
        tt("dve", wsTm_f.v(F32, "p (g t) -> p g t", g=4), wsT3,
           tri_f.unsqueeze(1).to_broadcast([128, 4, 128]), ALU.mult, [C["wsT"], C["tri_f"]], [wsTm_f])
        cp("dve", wsTm_b.v(BF16), wsTm_f.v(F32), [wsTm_f], [wsTm_b])
        bk = nextpm()
        mm(bk.ap[:, :], ones_f, wsTm_f.v(F32), True, True, [C["ones_f"], wsTm_f], [bk])
        BR3 = BR.v(F32, "p (j t) -> p j t", j=8)
        bs3 = C["bs_bc"].v(F32, "p (g t) -> p g t", g=4)
        lnb = C["lnb"].v(F32)
        lng = C["lng"].v(F32)
        for j in range(8):
            g = j // 2
            stt(BR3[:, j, :], bk.ap[:, g * 128:(g + 1) * 128], lnb[:, j:j + 1], bs3[:, g, :], ALU.mult, ALU.add,
                [bk, C["lnb"], C["bs_bc"]], [(BR, j)])
        P.add("pool", lambda e: e.memset(neghalf.v(F32), -0.5), [], [neghalf])
        P.add("pool", lambda e: e.memset(state.v(F32), 0.0), [], [state])
        P.add("pool", lambda e: e.memset(prevbf.v(BF16), 0.0), [], [prevbf])
        P.add("pool", lambda e: e.memset(halo.v(BF16), 0.0), [], [halo])

        wq_bufs = {nm: Buf("wq_" + nm) for (nm, *_r) in units}
        stg_in = [t1[0], t1[1], acc[0], acc[1], Lb[0], Lb[1], Gb[0], Gb[1]]
        stg_out = [gab[0], gab[1], Xdt[0], Xdt[1], XD[0], XD[1], XdD[0], XdD[1]]
        NSTG = 8
        cnt = 0
        for (nm, src, K, c0, ncol, rs) in units:
            dst = wq[uoff[nm]:uoff[nm] + 128 * K * ncol].rearrange("(p k c) -> p k c", p=128, k=K)
            for k in range(K):
                si = stg_in[cnt % NSTG]
                so = stg_out[cnt % NSTG]
                dma("sp", si.v(F32)[:, :ncol], dram[src][k * 128:(k + 1) * 128, c0:c0 + ncol], [], [si], "stgi%d" % (cnt % NSTG))
                eng = "act" if cnt % 2 == 0 else "dve"
                if rs is None:
                    cp(eng, so.v(BF16)[:, :ncol], si.v(F32)[:, :ncol], [si], [so])
                else:
                    sc = C[rs].v(F32)[:, k:k + 1]
                    if eng == "act":
                        act(so.v(BF16)[:, :ncol], si.v(F32)[:, :ncol], AF.Identity, [si, C[rs]], [so], scale=sc)
                    else:
                        ts("dve", so.v(BF16)[:, :ncol], si.v(F32)[:, :ncol], sc, None, ALU.mult, None, [si, C[rs]], [so])
                dma("pool", dst[:, k, :], so.v(BF16)[:, :ncol], [so], [(wq_bufs[nm], k)], "stgo%d" % (cnt % NSTG))
                cnt += 1

        ring_i = [0]

        def load_unit(nm):
            (_, src, K, c0, ncol, rs) = [u for u in units if u[0] == nm][0]
            if nm == "dt":
                slot = wdt
                semn = "wdt"
            else:
                slot = ring[ring_i[0] % 4]
                semn = "ring%d" % (ring_i[0] % 4)
                ring_i[0] += 1
            srcap = wq[uoff[nm]:uoff[nm] + 128 * K * ncol].rearrange("(p k c) -> p k c", p=128, k=K)
            view = slot.v(BF16)[:, :K * ncol].rearrange("p (k c) -> p k c", k=K)
            dma("sp", view, srcap, [wq_bufs[nm]], [slot], semn)
            return slot, view

        def zipper(gens):
            gens = list(gens)
            while gens:
                for gen in list(gens):
                    try:
                        next(gen)
                    except StopIteration:
                        gens.remove(gen)

        def stage_a(ti):
            for _ in stage_a_gen(ti):
                pass

        def stage_a_gen(ti):
            sl = ti % 2
            hT3 = hT[sl].v(BF16, "p (k t) -> p k t", k=8)
            for c in range(NCH):
                xb = xa[c % 2]
                r0 = ti * T + c * 128
                dma("sp", xb.v(F32), dram["x"][r0:r0 + 128, :], [], [xb], "xa%d" % (c % 2))
                ss = small[0]
                hbb = hb[c % 2]
                act(hbb.v(BF16), xb.v(F32), AF.Square, [xb], [(ss, 0), hbb], accum=ss.v(F32)[:, 0:1])
                rsqrt_chain(ss.v(F32)[:, 1:2], ss.v(F32)[:, 0:1], 1.0 / D, (ss, 0), (ss, 1))
                yield
                act(hbb.v(BF16), xb.v(F32), AF.Identity, [xb, (ss, 1)], [hbb], scale=ss.v(F32)[:, 1:2])
                pb_ = nextpt()
                for k in range(8):
                    tp(pb_.ap[:, k * 128:(k + 1) * 128], hbb.v(BF16)[:, k * 128:(k + 1) * 128], [hbb], [(pb_, k)])
                cp("dve", hT3[:, :, c * 128:(c + 1) * 128], pb_.ap.rearrange("p (k t) -> p k t", k=8), [pb_], [(hT[sl], c)])
                yield

        def fm_proj(view, hT3, jj, hTbuf, slot):
            bk = nextpm()
            for k in range(8):
                mm(bk.ap[:, :], view[:, k, jj * 128:(jj + 1) * 128], hT3[:, k, :], k == 0, k == 7, [slot, hTbuf], [bk])
            return bk

        def a_branch(ti):
            sl = ti % 2
            hTb = hT[sl]
            hT3 = hTb.v(BF16, "p (k t) -> p k t", k=8)
            ug3 = P1.v(BF16, "p (k t) -> p k t", k=8)
            zs3 = P2.v(BF16, "p (k t) -> p k t", k=8)
            sv = [load_unit("v0"), load_unit("v1")]
            vg = [Buf("vg0", P3.raw[:, 0:2048], 4096), Buf("vg1", P3.raw[:, 2048:4096], 4096)]
            vn = Buf("vn", P3.raw[:, 4096:8192], 8192)
            vn3 = vn.v(BF16, "p (c d) -> p c d", c=4)
            st = small[1]
            stv = st.v(F32)
            for cp_ in range(2):
                for s in range(2):
                    c = cp_ * 2 + s
                    vgv = vg[s].v(F32)
                    for i in range(2):
                        slot, view = sv[i]
                        bk = nextpm()
                        for k in range(8):
                            mm(bk.ap[:, :], hT3[:, k, c * 128:(c + 1) * 128], view[:, k, :], k == 0, k == 7, [slot, hTb], [bk])
                        act(vgv[:, i * 512:(i + 1) * 512], bk.ap[:, :], AF.Gelu_apprx_tanh, [bk], [k3vg(s)])
                        P.add("dve", lambda e, i=i, s=s, vgv=vgv: e.bn_stats(out=stv[:, s * 12 + i * 6:s * 12 + (i + 1) * 6], in_=vgv[:, i * 512:(i + 1) * 512]),
                              [k3vg(s)], [(st, ("bs", s, i))])
                    P.add("dve", lambda e, s=s: e.bn_aggr(out=stv[:, 32 + 2 * s:34 + 2 * s], in_=stv[:, s * 12:(s + 1) * 12]),
                          [(st, ("bs", s, 0)), (st, ("bs", s, 1))], [(st, ("ag", s))])
                    ts("dve", stv[:, 40 + s:41 + s], stv[:, 33 + 2 * s:34 + 2 * s], 1.0, EPS, ALU.mult, ALU.add, [(st, ("ag", s))], [(st, "rs")])
                act(stv[:, 40:42], stv[:, 40:42], AF.Ln, [(st, "rs")], [(st, "rs")])
                act(stv[:, 40:42], stv[:, 40:42], AF.Exp, [(st, "rs")], [(st, "rs")], scale=-0.5)
                for s in range(2):
                    c = cp_ * 2 + s
                    ts("dve", vn3[:, c, :], vg[s].v(F32), stv[:, 32 + 2 * s:33 + 2 * s], stv[:, 40 + s:41 + s], ALU.subtract, ALU.mult,
                       [k3vg(s), (st, ("ag", s)), (st, "rs")], [k3vn(c)])
            for i in range(2):
                slot, view = load_unit("u%d" % i)
                for jj in range(4):
                    j = i * 4 + jj
                    bk = fm_proj(view, hT3, jj, hTb, slot)
                    act(ug3[:, j, :], bk.ap[:, :], AF.Gelu_apprx_tanh, [bk], [k1(j)])
            for i in range(2):
                slot, view = load_unit("za%d" % i)
                for jj in range(4):
                    j = i * 4 + jj
                    bk = fm_proj(view, hT3, jj, hTb, slot)
                    act(zs3[:, j, :], bk.ap[:, :], AF.Silu, [bk], [k2(j)])
                    tt("dve", ug3[:, j, :], ug3[:, j, :], zs3[:, j, :], ALU.mult, [k1(j), k2(j)], [k1(j)])
            wsb3 = wsTm_b.v(BF16, "p (g t) -> p g t", g=4)
            for j in range(8):
                g = j // 2
                bk = nextpm()
                for c in range(NCH):
                    mm(bk.ap[:, c * 128:(c + 1) * 128], vn3[:, c, j * 128:(j + 1) * 128], wsb3[:, g, :], True, True,
                       [k3vn(c), wsTm_b], [bk])
                tb = t1[j % 2]
                stt(tb.v(F32, "p (c t) -> p c t", c=4), bk.ap.rearrange("p (c t) -> p c t", c=4), lng[:, j:j + 1],
                    BR3[:, j, :].unsqueeze(1).to_broadcast([128, 4, 128]), ALU.mult, ALU.add,
                    [bk, C["lng"], (BR, j)], [tb])
                tt("dve", ug3[:, j, :], tb.v(F32), ug3[:, j, :], ALU.mult, [tb, k1(j)], [k1(j)])
            mg3 = mg.v(BF16, "p (k t) -> p k t", k=8)
            sga = [load_unit("ga0"), load_unit("ga1")]
            soa = [load_unit("oa0"), load_unit("oa1")]
            for j in range(8):
                i, jj = j // 4, j % 4
                bk = fm_proj(sga[i][1], hT3, jj, hTb, sga[i][0])
                gt = gab[j % 2]
                act(gt.v(BF16), bk.ap[:, :], AF.Sigmoid, [bk], [gt])
                bk2 = nextpm()
                for k in range(8):
                    mm(bk2.ap[:, :], soa[i][1][:, k, jj * 128:(jj + 1) * 128], ug3[:, k, :], k == 0, k == 7, [soa[i][0], P1], [bk2])
                tt("dve", mg3[:, j, :], bk2.ap[:, :], gt.v(BF16), ALU.mult, [bk2, gt], [(mg, j)])

        def b_branch(ti):
            sl = ti % 2
            hTb = hT[sl]
            hT3 = hTb.v(BF16, "p (k t) -> p k t", k=8)
            slot, view = load_unit("dt")
            bk = nextpm()
            for c in range(NCH):
                for k in range(8):
                    mm(bk.ap[:, c * 32:(c + 1) * 32], hT3[:, k, c * 128:(c + 1) * 128], view[:, k, :], k == 0, k == 7, [slot, hTb], [bk])
            dtr3 = dtr.v(F32, "p (c h) -> p c h", c=4)
            tt("dve", dtr3, bk.ap[:, 0:128].rearrange("p (c h) -> p c h", c=4),
               C["dtb_bc"].v(F32).unsqueeze(1).to_broadcast([128, 4, 32]), ALU.add, [bk, C["dtb_bc"]], [dtr])
            stt(dtt.v(F32), dtr.v(F32), -1.0, dtr.v(F32), ALU.mult, ALU.max, [dtr], [dtt])
            act(dtt.v(F32), dtt.v(F32), AF.Exp, [dtt], [dtt], scale=-1.0)
            act(dtt.v(F32), dtt.v(F32), AF.Ln, [dtt], [dtt], bias=1.0)
            stt(dtt.v(F32), dtr.v(F32), 0.0, dtt.v(F32), ALU.max, ALU.add, [dtr, dtt], [dtt])
            tt("dve", dA.v(F32, "p (c h) -> p c h", c=4), dtt.v(F32, "p (c h) -> p c h", c=4),
               a_bc.v(F32).unsqueeze(1).to_broadcast([128, 4, 32]), ALU.mult, [dtt, a_bc], [dA])
            dt3 = dtt.v(F32, "p (c h) -> p c h", c=4)
            dA3 = dA.v(F32, "p (c h) -> p c h", c=4)
            if stop_after == "b_dt":
                raise _Stop()
            halo3 = halo.v(BF16)[:, :72].rearrange("p (j t) -> p j t", j=24)
            cw = C["convw"].v(F32, "p (j k) -> p j k", j=24)
            cb = C["convb"].v(F32)
            ybT3 = ybT.v(BF16, "p (k t) -> p k t", k=16)
            st4 = state.v(F32, "p (g f) -> p g f", g=4)
            pv4 = prevbf.v(BF16, "p (g f) -> p g f", g=4)
            groupctx = {}

            def proj_group(g):
                xcb = Buf("xc", P3.raw[:, (g % 2) * 3072:(g % 2) * 3072 + 3072], 6144)
                xck = ("xc", g % 2)
                xc3 = xcb.v(BF16, "p (j t) -> p j t", j=6)
                wu = [load_unit("xs%d" % g)]
                s_bm = load_unit("bm%d" % g)
                s_cm = load_unit("cm%d" % g)
                tiles = [(wu[0], jj, g * 4 + jj) for jj in range(4)] + [(s_bm, 0, 16 + g), (s_cm, 0, 20 + g)]
                for li, ((slot, view), jj, jglob) in enumerate(tiles):
                    bk = fm_proj(view, hT3, jj, hTb, slot)
                    rw = raw[(g * 6 + li) % 3]
                    rv = rw.v(BF16)
                    cp("act", rv[:, 3:515], bk.ap[:, :], [bk], [(rw, "b")])
                    cp("pool", rv[:, 0:3], halo3[:, jglob, :], [(halo, jglob)], [(rw, "h")])
                    ab = acc[(g * 6 + li) % 2]
                    av = ab.v(F32)
                    ts("dve", av, bk.ap[:, :], cw[:, jglob, 3:4], cb[:, jglob:jglob + 1], ALU.mult, ALU.add,
                       [bk, C["convw"], C["convb"]], [ab])
                    cp("pool", halo3[:, jglob, :], rv[:, 512:515], [(rw, "b")], [(halo, jglob)])
                    pa = tmpst[(g * 6 + li) % 2]
                    pb4 = y1[(g * 6 + li) % 2]
                    ts("pool", pa.v(F32), rv[:, 0:512], cw[:, jglob, 0:1], 0.0, ALU.mult, ALU.add, [rw, C["convw"]], [pa])
                    ts("pool", pb4.v(F32), rv[:, 1:513], cw[:, jglob, 1:2], 0.0, ALU.mult, ALU.add, [rw, C["convw"]], [pb4])
                    tt("pool", pa.v(F32), pa.v(F32), pb4.v(F32), ALU.add, [pa, pb4], [pa])
                    stt(av, rv[:, 2:514], cw[:, jglob, 2:3], av, ALU.mult, ALU.add, [rw, ab, C["convw"]], [ab])
                    tt("dve", av, av, pa.v(F32), ALU.add, [ab, pa], [ab])
                    act(xc3[:, li, :], av, AF.Silu, [ab], [k3xc(g % 2)])
                if stop_after == "b_conv":
                    raise _Stop()
                zbb = Buf("zb", P2.raw[:, (g % 2) * 2048:(g % 2) * 2048 + 2048], 4096)
                zbk = ("zb", g % 2)
                zb3 = zbb.v(BF16, "p (c f) -> p c f", c=4)
                slot, view = load_unit("zb%d" % g)
                for c in range(NCH):
                    bk = nextpm()
                    for k in range(8):
                        mm(bk.ap[:, :], hT3[:, k, c * 128:(c + 1) * 128], view[:, k, :], k == 0, k == 7, [slot, hTb], [bk])
                    act(zb3[:, c, :], bk.ap[:, :], AF.Silu, [bk], [k2zb(g % 2)])
                if stop_after == "b_zb":
                    raise _Stop()
                groupctx[g] = (xc3, zb3)

            def ssd_iter(g, c, q):
                xc3, zb3 = groupctx[g]
                xk = k3xc(g % 2)
                zk = k2zb(g % 2)
                hs = slice(g * 8, g * 8 + 8)
                ssb = ssd_s[q]
                sv_ = ssb.v(F32)
                cs_ap, ecs_ap, dec_ap, cd_ap = sv_[:, 0:8], sv_[:, 8:16], sv_[:, 16:24], sv_[:, 24:32]
                dAh = ssb.v(BF16)[:, 64:72]
                dAl = ssb.v(BF16)[:, 72:80]
                csl = slice(c * 128, (c + 1) * 128)
                bk = nextpm("pm%d" % q)
                mm(bk.ap[:, 0:8], tri_f, dA3[:, c, hs], True, True, [C["tri_f"], dA], [(bk, 0)])
                mm(bk.ap[:, 8:16], ones_f, dA3[:, c, hs], True, True, [C["ones_f"], dA], [(bk, 1)])
                cp("dve", dAh, dA3[:, c, hs], [dA], [(ssb, "hi")])
                tt("dve", dAl, dA3[:, c, hs], dAh, ALU.subtract, [dA, (ssb, "hi")], [(ssb, "lo")])
                pb_ = nextpt("pt%d" % q)
                for jj in range(4):
                    tp(pb_.ap[:, jj * 128:(jj + 1) * 128], xc3[:, jj, csl], [xk], [(pb_, jj)])
                tp(pb_.ap[:, 512:640], xc3[:, 4, csl], [xk], [(pb_, 4)])
                yield
                rhi = S4[q].v(BF16)[:, 0:1024].rearrange("p (h l) -> p h l", h=8)
                rlo = S4[q].v(BF16)[:, 1024:2048].rearrange("p (h l) -> p h l", h=8)
                trib = tri_b.unsqueeze(1).to_broadcast([128, 8, 128])
                tt("dve", rhi, trib, dAh.unsqueeze(2).to_broadcast([128, 8, 128]), ALU.mult, [C["tri_b"], (ssb, "hi")], [kS4(q, "a")])
                tt("dve", rlo, trib, dAl.unsqueeze(2).to_broadcast([128, 8, 128]), ALU.mult, [C["tri_b"], (ssb, "lo")], [kS4(q, "b")])
                cp("dve", cs_ap, bk.ap[:, 0:8], [bk], [(ssb, "cs")])
                tt("dve", dec_ap, bk.ap[:, 8:16], cs_ap, ALU.subtract, [bk, (ssb, "cs")], [(ssb, "dec")])
                act(ecs_ap, bk.ap[:, 0:8], AF.Exp, [bk], [(ssb, "ecs")])
                act(cd_ap, bk.ap[:, 8:16], AF.Exp, [bk], [(ssb, "cd")])
                act(dec_ap, dec_ap, AF.Exp, [(ssb, "dec")], [(ssb, "dec")])
                yield
                xv = pb_.ap[:, 0:512].rearrange("p (h d) -> p h d", h=8)
                tt("dve", Xdt[q].v(BF16, "p (h d) -> p h d", h=8), xv,
                   dt3[:, c, hs].unsqueeze(2).to_broadcast([128, 8, 64]), ALU.mult, [pb_, dtt], [Xdt[q]])
                tt("dve", XD[q].v(BF16, "p (h d) -> p h d", h=8), xv,
                   C["dskip_bc"].v(F32)[:, hs].unsqueeze(2).to_broadcast([128, 8, 64]), ALU.mult, [pb_, C["dskip_bc"]], [XD[q]])
                cp("dve", Btm[q].v(BF16), pb_.ap[:, 512:640], [pb_], [Btm[q]])
                bk3 = nextpm("pm%d" % q)
                mm(bk3.ap[:, 0:128], xc3[:, 4, csl], xc3[:, 5, csl], True, True, [xk], [bk3])
                yield
                L3 = Lb[q].v(BF16, "p (h l) -> p h l", h=8)
                for hh in range(2):
                    bk2 = nextpm("pm%d" % q)
                    mm(bk2.ap[:, :], sut_b, S4[q].v(BF16)[:, hh * 512:(hh + 1) * 512], True, False, [C["sut_b"], kS4(q, "a")], [bk2])
                    mm(bk2.ap[:, :], sut_b, S4[q].v(BF16)[:, 1024 + hh * 512:1024 + (hh + 1) * 512], False, True, [C["sut_b"], kS4(q, "b")], [bk2])
                    act(Lb[q].v(BF16)[:, hh * 512:(hh + 1) * 512], bk2.ap[:, :], AF.Exp, [bk2], [kL(q, hh)])
                tt("dve", CBm[q].v(BF16), bk3.ap[:, 0:128], tri_b, ALU.mult, [bk3, C["tri_b"]], [CBm[q]])
                yield
                G3 = Gb[q].v(BF16, "p (h l) -> p h l", h=8)
                tt("dve", G3, L3, CBm[q].v(BF16).unsqueeze(1).to_broadcast([128, 8, 128]), ALU.mult, [Lb[q], CBm[q]], [Gb[q]])
                tt("dve", XdD[q].v(BF16, "p (h d) -> p h d", h=8), Xdt[q].v(BF16, "p (h d) -> p h d", h=8),
                   dec_ap.unsqueeze(2).to_broadcast([128, 8, 64]), ALU.mult, [Xdt[q], (ssb, "dec")], [XdD[q]])
                yield
                bko = nextpm("pm%d" % q)
                mm(bko.ap[:, :], xc3[:, 5, csl], pv4[:, g, :], True, True, [xk, (prevbf, g)], [bko])
                bkd = nextpm("pm%d" % q)
                mm(bkd.ap[:, :], ident_b, XD[q].v(BF16), True, False, [C["ident_b"], XD[q]], [bkd])
                for r in range(8):
                    mm(bkd.ap[:, r * 64:(r + 1) * 64], G3[:, r, :], Xdt[q].v(BF16)[:, r * 64:(r + 1) * 64], False, r == 7, [Gb[q], Xdt[q]], [bkd])
                bks = nextpm("pm%d" % q)
                mm(bks.ap[:, :], Btm[q].v(BF16), XdD[q].v(BF16), True, True, [Btm[q], XdD[q]], [bks])
                tt("pool", tmpst[q].v(F32, "p (h d) -> p h d", h=8), st4[:, g, :].rearrange("p (h d) -> p h d", h=8),
                   cd_ap.unsqueeze(2).to_broadcast([128, 8, 64]), ALU.mult, [(state, g), (ssb, "cd")], [tmpst[q]])
                yield
                yv = y1[q].v(F32)
                tt("dve", y1[q].v(F32, "p (h d) -> p h d", h=8), bko.ap.rearrange("p (h d) -> p h d", h=8),
                   ecs_ap.unsqueeze(2).to_broadcast([128, 8, 64]), ALU.mult, [bko, (ssb, "ecs")], [y1[q]])
                tt("dve", yv, yv, bkd.ap[:, :], ALU.add, [bkd, y1[q]], [y1[q]])
                tt("dve", st4[:, g, :], tmpst[q].v(F32), bks.ap[:, :], ALU.add, [bks, tmpst[q]], [(state, g)])
                cp("act", pv4[:, g, :], st4[:, g, :], [(state, g)], [(prevbf, g)])
                yield
                tt("pool", yv, yv, zb3[:, c, :], ALU.mult, [y1[q], zk], [y1[q]])
                act(ybb[q].v(BF16), yv, AF.Square, [y1[q]], [(ssb, "ss"), ybb[q]], accum=sv_[:, 50:51])
                yield
                rsqrt_chain(sv_[:, 51:52], sv_[:, 50:51], 1.0 / 512, (ssb, "ss"), (ssb, "rs"))
                yield
                act(ybb[q].v(BF16), yv, AF.Identity, [y1[q], (ssb, "rs")], [ybb[q]], scale=sv_[:, 51:52])
                pb2 = nextpt("pt%d" % q)
                for jj in range(4):
                    tp(pb2.ap[:, jj * 128:(jj + 1) * 128], ybb[q].v(BF16)[:, jj * 128:(jj + 1) * 128], [ybb[q]], [(pb2, jj)])
                cp("act", ybT3[:, g * 4:(g + 1) * 4, csl], pb2.ap[:, 0:512].rearrange("p (k t) -> p k t", k=4), [pb2], [(ybT, (g, c))])
                yield

            def stream(g, q):
                for c in range(NCH):
                    yield from ssd_iter(g, c, q)

            def zipper(gens):
                gens = list(gens)
                while gens:
                    for gen in list(gens):
                        try:
                            next(gen)
                        except StopIteration:
                            gens.remove(gen)

            for pair in range(2):
                mark("b_proj%d" % pair, ti)
                proj_group(2 * pair)
                proj_group(2 * pair + 1)
                mark("b_ssd%d" % pair, ti)
                zipper([stream(2 * pair, 0), stream(2 * pair + 1, 1)])

        def ob_merge(ti):
            gens = [ob_merge_gen(ti)]
            if ti + 1 < n_tiles:
                gens.append(stage_a_gen(ti + 1))
            zipper(gens)

        def ob_merge_gen(ti):
            sl = ti % 2
            hTb = hT[sl]
            hT3 = hTb.v(BF16, "p (k t) -> p k t", k=8)
            mg3 = mg.v(BF16, "p (k t) -> p k t", k=8)
            ybT3 = ybT.v(BF16, "p (k t) -> p k t", k=16)
            sgb = [None, None]
            for j in range(8):
                i, jj = j // 4, j % 4
                if jj == 0:
                    sgb[i] = load_unit("gb%d" % i)
                if j % 2 == 0:
                    sob = load_unit("ob%d" % (j // 2))
                    obv = sob[0].v(BF16)[:, :16 * 256].rearrange("p (k c) -> p k c", k=16)
                bk = fm_proj(sgb[i][1], hT3, jj, hTb, sgb[i][0])
                gt = gab[j % 2]
                act(gt.v(BF16), bk.ap[:, :], AF.Sigmoid, [bk], [gt])
                bk2 = nextpm()
                for k in range(16):
                    mm(bk2.ap[:, :], obv[:, k, (j % 2) * 128:(j % 2 + 1) * 128], ybT3[:, k, :], k == 0, k == 15, [sob[0], ybT], [bk2])
                tb = t1[j % 2]
                tt("dve", tb.v(F32), bk2.ap[:, :], gt.v(BF16), ALU.mult, [bk2, gt], [tb])
                tt("pool", mg3[:, j, :], mg3[:, j, :], tb.v(F32), ALU.add, [(mg, j), tb], [(mg, j)])
                yield

        def tail(ti):
            mg3 = mg.v(BF16, "p (k t) -> p k t", k=8)
            x1 = P3.v(F32, "p (c d) -> p c d", c=4)
            hpT3 = P1.v(BF16, "p (k t) -> p k t", k=8)
            gate3 = P2.v(BF16, "p (c d) -> p c d", c=4)
            pT3 = Lb[0].v(BF16)[:, :1024].rearrange("p (k t) -> p k t", k=2)
            for c in range(NCH):
                r0 = ti * T + c * 128
                dma("sp", x1[:, c, :], dram["x"][r0:r0 + 128, :], [], [k3x1(c)], "x1_%d" % c)
            for i in range(2):
                slot, view = load_unit("out%d" % i)
                for c in range(NCH):
                    bk = nextpm()
                    for k in range(8):
                        mm(bk.ap[:, :], mg3[:, k, c * 128:(c + 1) * 128], view[:, k, :], k == 0, k == 7, [slot, mg], [bk])
                    tt("dve", x1[:, c, i * 512:(i + 1) * 512], x1[:, c, i * 512:(i + 1) * 512], bk.ap[:, :], ALU.add,
                       [bk, k3x1(c)], [k3x1(c)])
            for c in range(NCH):
                ss = small[2]
                hpb = acc[c % 2]
                act(hpb.v(BF16)[:, :1024], x1[:, c, :], AF.Square, [k3x1(c)], [(ss, 0), hpb], accum=ss.v(F32)[:, 0:1])
                rsqrt_chain(ss.v(F32)[:, 1:2], ss.v(F32)[:, 0:1], 1.0 / D, (ss, 0), (ss, 1))
                act(hpb.v(BF16)[:, :1024], x1[:, c, :], AF.Identity, [k3x1(c), (ss, 1)], [hpb], scale=ss.v(F32)[:, 1:2])
                pb_ = nextpt()
                for k in range(8):
                    tp(pb_.ap[:, k * 128:(k + 1) * 128], hpb.v(BF16)[:, k * 128:(k + 1) * 128], [hpb], [(pb_, k)])
                cp("dve", hpT3[:, :, c * 128:(c + 1) * 128], pb_.ap.rearrange("p (k t) -> p k t", k=8), [pb_], [k1hp(c)])
                r0 = ti * T + c * 128
                ptb_ = Xdt[c % 2]
                pbb = XD[c % 2]
                dma("sp", ptb_.v(F32), dram["p"][r0:r0 + 128, :], [], [ptb_], "pt%d" % (c % 2))
                cp("dve", pbb.v(BF16)[:, :256], ptb_.v(F32), [ptb_], [pbb])
                pb3 = nextpt()
                for k in range(2):
                    tp(pb3.ap[:, k * 128:(k + 1) * 128], pbb.v(BF16)[:, k * 128:(k + 1) * 128], [pbb], [(pb3, k)])
                cp("act", pT3[:, :, c * 128:(c + 1) * 128], pb3.ap[:, 0:256].rearrange("p (k t) -> p k t", k=2), [pb3], [kLpT(c)])
            for i in range(2):
                slot, view = load_unit("pg%d" % i)
                for c in range(NCH):
                    bk = nextpm()
                    for k in range(8):
                        mm(bk.ap[:, :], hpT3[:, k, c * 128:(c + 1) * 128], view[:, k, :], k == 0, k == 7, [slot, P1], [bk])
                    act(gate3[:, c, i * 512:(i + 1) * 512], bk.ap[:, :], AF.Sigmoid, [bk], [k2gate(c, i)])
            for i in range(2):
                slot, view0 = load_unit("ple%d" % i)
                view = slot.v(BF16)[:, :1024].rearrange("p (k c) -> p k c", k=2)
                for c in range(NCH):
                    bk = nextpm()
                    for k in range(2):
                        mm(bk.ap[:, :], pT3[:, k, c * 128:(c + 1) * 128], view[:, k, :], k == 0, k == 1, [slot, Lb[0]], [bk])
                    tb = t1[(i * 4 + c) % 2]
                    tt("dve", tb.v(F32), bk.ap[:, :], gate3[:, c, i * 512:(i + 1) * 512], ALU.mult, [bk, k2gate(c, i)], [tb])
                    tt("pool", x1[:, c, i * 512:(i + 1) * 512], x1[:, c, i * 512:(i + 1) * 512], tb.v(F32), ALU.add,
                       [k3x1(c), tb], [k3x1(c)])
            for c in range(NCH):
                ss = small[3]
                act(acc[c % 2].v(BF16)[:, :1024], x1[:, c, :], AF.Square, [k3x1(c)], [(ss, 0), acc[c % 2]], accum=ss.v(F32)[:, 0:1])
                rsqrt_chain(ss.v(F32)[:, 1:2], ss.v(F32)[:, 0:1], 1.0 / D, (ss, 0), (ss, 1))
                ot = S4[c % 2]
                stt(ot.v(F32), x1[:, c, :], ss.v(F32)[:, 1:2], C["fg_bc"].v(F32), ALU.mult, ALU.mult,
                    [k3x1(c), (ss, 1), C["fg_bc"]], [ot])
                r0 = ti * T + c * 128
                so = dma("pool", out_d[r0:r0 + 128, :], ot.v(F32), [ot], [], "ot%d" % (c % 2))
                store_ops.append(so)

        env = locals()

        def dbg_hook(hook):
            if debug is None or hook not in debug:
                return
            for (nm, ncols, fn) in debug[hook]:
                ap, reads = fn(env)
                so = dma("pool", dbg_d[nm][:, :], ap, reads, [], "dbg_" + nm)
                store_ops.append(so)

        phase_marks = []
        build.phase_marks = phase_marks

        def mark(name, ti):
            phase_marks.append((name, ti, len(P.ops["pe"]), len(P.ops["act"]), len(P.ops["dve"]), len(P.ops["pool"])))

        if stop_after != "prologue":
            stage_a(0)
        for ti in range(n_tiles):
          try:
              if stop_after in ("prologue", "stage_a"):
                  break
              mark("a_branch", ti)
              a_branch(ti)
              mark("b_branch", ti)
              if ti == 0:
                  dbg_hook("after_a")
              if stop_after == "a_branch":
                  break
              b_branch(ti)
              if stop_after is not None and stop_after.startswith("b_"):
                  break
              if ti == 0:
                  dbg_hook("after_b")
              mark("stage_a", ti)
              mark("ob_merge", ti)
              ob_merge(ti)
              mark("tail", ti)
              if ti == 0:
                  dbg_hook("after_merge")
              tail(ti)
              mark("end", ti)
              if ti == 0:
                  dbg_hook("end")
          except _Stop:
            dbg_hook("stop")
            break

        for so in store_ops:
            so.needed = True
        P.finalize()
        counts = {k: len(v) for k, v in P.ops.items()}
        print("op counts", counts)
        blk = es.enter_context(nc.Block())

        @blk.tensor
        def _(e):
            P.emit("pe", e)

        @blk.scalar
        def _(e):
            P.emit("act", e)

        @blk.vector
        def _(e):
            P.emit("dve", e)

        @blk.gpsimd
        def _(e):
            P.emit("pool", e, final_waits=store_ops)

        @blk.sync
        def _(e):
            P.emit("sp", e)
    return nc


def host_consts(inp):
    f = np.float32
    k = np.arange(128)
    c = {}
    c["ident_b"] = np.eye(128, dtype=f)
    tri = (k[:, None] <= k[None, :]).astype(f)
    c["tri_b"] = tri
    c["tri_f"] = tri
    c["sut_b"] = (k[:, None] > k[None, :]).astype(f)
    c["ones_f"] = np.ones((128, 128), f)

    def pk(v):
        return np.ascontiguousarray(np.asarray(v, f).reshape(-1, 128).T)
    c["ng"] = pk(inp["norm_g"][0])
    c["png"] = pk(inp["ple_norm_g"][0])
    c["sng"] = pk(inp["ssm_norm_g"][0])
    c["lng"] = pk(inp["ln_a_g"][0])
    c["lnb"] = pk(inp["ln_a_b"][0])
    cw = np.asarray(inp["conv_w"][0], f)
    c["convw"] = np.ascontiguousarray(cw.reshape(4, 24, 128).transpose(2, 1, 0)).reshape(128, 96)
    c["convb"] = pk(inp["conv_b"][0])

    def bc(v):
        v = np.asarray(v, f).reshape(1, -1)
        return np.ascontiguousarray(np.broadcast_to(v, (128, v.shape[1])))
    c["fg_bc"] = bc(inp["final_g"])
    c["dtb_bc"] = bc(inp["dt_bias"][0])
    c["alog_bc"] = bc(inp["a_log"][0])
    c["dskip_bc"] = bc(inp["d_skip"][0])
    ws = np.asarray(inp["w_s"][0], f)
    c["wsT"] = np.ascontiguousarray(ws.transpose(2, 0, 1)).reshape(128, 512)
    c["bs_bc"] = bc(np.asarray(inp["b_s"][0], f).reshape(-1))
    return c


def make_in_maps(inp, n_cores, n_tiles):
    c = host_consts(inp)
    ntok = n_tiles * T
    base = {
        "w_in": np.ascontiguousarray(np.asarray(inp["w_in"][0], np.float32)),
        "w_oa": np.ascontiguousarray(np.asarray(inp["w_oa"][0], np.float32)),
        "w_ob": np.ascontiguousarray(np.asarray(inp["w_ob"][0], np.float32)),
        "w_out": np.ascontiguousarray(np.asarray(inp["w_out"][0], np.float32)),
        "w_pg": np.ascontiguousarray(np.asarray(inp["w_pg"][0], np.float32)),
        "w_ple": np.ascontiguousarray(np.asarray(inp["w_ple"][0], np.float32)),
    }
    for k, v in c.items():
        base["c_" + k] = v
    maps = []
    for b in range(n_cores):
        m = dict(base)
        m["x"] = np.ascontiguousarray(np.asarray(inp["x"][b, :ntok], np.float32))
        m["p"] = np.ascontiguousarray(np.asarray(inp["p"][0, b, :ntok], np.float32))
        maps.append(m)
    return maps


def kernel(**inputs):
    n_tiles = SEQ // T
    nc = build(n_tiles)
    maps = make_in_maps(inputs, NB, n_tiles)
    res = run_bass_kernel_spmd(nc, maps, core_ids=list(range(NB)))
    out = np.stack([np.asarray(r["out"], np.float32) for r in res.results], axis=0)
    return out
```

```python
import numpy as np
from contextlib import ExitStack
import concourse.bass as bass
import concourse.mybir as mybir
from concourse.bass_utils import run_bass_kernel_spmd

F32 = mybir.dt.float32
BF16 = mybir.dt.bfloat16
AF = mybir.ActivationFunctionType
ALU = mybir.AluOpType

D = 1024
SEQ = 8192
NB = 8
PLE = 256
T = 512
NCH = 4
EPS = 1e-6
NIN = 10272
C_U, C_V, C_ZA, C_ZB, C_XBC, C_DT, C_GA, C_GB = 0, 1024, 2048, 3072, 5120, 8192, 8224, 9248
MAXV = 30000
import os
STOPC = int(os.environ.get('STOPC', '0'))
USE_POW = int(os.environ.get('USE_POW', '0'))
NO_SELF = tuple(x for x in os.environ.get('NO_SELF', '').split(',') if x)


class _Stop(Exception):
    pass


class Buf:
    def __init__(self, name, raw=None, nbytes=0):
        self.name = name
        self.raw = raw
        self.nbytes = nbytes
        self.recs = {}
        self.ranges = {}

    def v(self, dt=BF16, pat=None, **kw):
        ap = self.raw if dt == BF16 else self.raw.bitcast(dt)
        if pat is not None:
            ap = ap.rearrange(pat, **kw)
        return ap


class Rec:
    __slots__ = ("writer", "readers")

    def __init__(self):
        self.writer = None
        self.readers = []


class Op:
    __slots__ = ("eng", "fn", "deps", "dma", "token", "needed", "seq")


def _acc(a):
    if not isinstance(a, tuple):
        return (a, None)
    if len(a) == 4:
        a[0].ranges[a[1]] = (a[2], a[3])
        return (a[0], a[1])
    return a


def _conf(buf, k1, k2):
    if k1 == k2 or k1 is None or k2 is None:
        return True
    r1 = buf.ranges.get(k1)
    r2 = buf.ranges.get(k2)
    if r1 is None or r2 is None:
        return False
    return r1[0] < r2[1] and r2[0] < r1[1]


class Prog:
    def __init__(self, nc, es):
        self.nc = nc
        self.es = es
        self.ops = {e: [] for e in ("pe", "act", "dve", "pool", "sp")}
        self.dma_sems = {}
        self.dma_counts = {}
        self.all_ops = []

    def _rec(self, buf, key):
        r = buf.recs.get(key)
        if r is None:
            r = buf.recs[key] = Rec()
        return r

    def add(self, eng, fn, reads=(), writes=(), dma=None, reg_reads=True):
        op = Op()
        op.eng = eng
        op.fn = fn
        op.dma = dma is not None
        op.needed = False
        op.token = None
        deps = []
        xr = [a for a in reads if getattr(_acc(a)[0], "excl", False)]
        if xr:
            reads = [a for a in reads if not getattr(_acc(a)[0], "excl", False)]
            writes = list(writes) + [_acc(a)[0] for a in xr]
        writes = [(_acc(a)[0] if getattr(_acc(a)[0], "excl", False) else a) for a in writes]
        for a in reads:
            buf, key = _acc(a)
            for k, r in buf.recs.items():
                if r.writer is not None and _conf(buf, key, k):
                    deps.append(r.writer)
        for a in writes:
            buf, key = _acc(a)
            for k, r in buf.recs.items():
                if _conf(buf, key, k):
                    if r.writer is not None:
                        deps.append(r.writer)
                    deps.extend(r.readers)
        for a in (reads if reg_reads else ()):
            buf, key = _acc(a)
            r = self._rec(buf, key)
            if not op.dma:
                r.readers = [x for x in r.readers if x.dma or x.eng != eng]
            r.readers.append(op)
        for a in writes:
            buf, key = _acc(a)
            if key is None:
                buf.recs = {}
            r = self._rec(buf, key)
            r.writer = op
            r.readers = []
        ud = []
        seen = set()
        for d in deps:
            if id(d) in seen or d is op:
                continue
            seen.add(id(d))
            if not (eng == "pe" and d.eng == "pe" and not d.dma and not op.dma):
                d.needed = True
            ud.append(d)
        op.deps = ud
        if op.dma:
            if dma not in self.dma_sems:
                self.dma_sems[dma] = self.es.enter_context(self.nc.semaphore("d_" + dma))
                self.dma_counts[dma] = 0
            self.dma_counts[dma] += 16
            op.token = (self.dma_sems[dma], self.dma_counts[dma])
        self.ops[eng].append(op)
        self.all_ops.append(op)
        return op

    def finalize(self):
        self.eng_sems = {}
        for eng, lst in self.ops.items():
            n = 0
            sems = []
            for op in lst:
                if op.dma or not op.needed:
                    continue
                ep = n // MAXV
                if ep >= len(sems):
                    sems.append(self.es.enter_context(self.nc.semaphore("e_%s%d" % (eng, ep))))
                op.token = (sems[ep], n % MAXV + 1)
                n += 1
            self.eng_sems[eng] = sems

    def emit(self, engname, e, final_waits=()):
        seen = {}
        for op in self.ops[engname]:
            need = {}
            for d in op.deps:
                if (not d.dma) and d.eng == engname and (engname == "pe" or engname in NO_SELF):
                    continue
                sem, val = d.token
                k = id(sem)
                if k not in need or need[k][1] < val:
                    need[k] = (sem, val)
            for k, (sem, val) in need.items():
                if seen.get(k, 0) >= val:
                    continue
                e.wait_ge(sem, val)
                seen[k] = val
            ins = op.fn(e)
            if op.dma:
                ins.then_inc(op.token[0], 16)
            elif op.needed:
                ins.then_inc(op.token[0], 1)
        for d in final_waits:
            sem, val = d.token
            if seen.get(id(sem), 0) >= val:
                continue
            e.wait_ge(sem, val)
            seen[id(sem)] = val


def weight_units():
    u = []
    for i in range(2):
        u.append(("u%d" % i, "w_in", 8, C_U + 512 * i, 512, "ng"))
    for i in range(2):
        u.append(("za%d" % i, "w_in", 8, C_ZA + 512 * i, 512, "ng"))
    for i in range(2):
        u.append(("v%d" % i, "w_in", 8, C_V + 512 * i, 512, "ng"))
    for i in range(2):
        u.append(("ga%d" % i, "w_in", 8, C_GA + 512 * i, 512, "ng"))
    for i in range(2):
        u.append(("oa%d" % i, "w_oa", 8, 512 * i, 512, None))
    u.append(("dt", "w_in", 8, C_DT, 32, "ng"))
    for g in range(4):
        u.append(("xs%d" % g, "w_in", 8, C_XBC + 512 * g, 512, "ng"))
    for g in range(4):
        u.append(("bm%d" % g, "w_in", 8, C_XBC + 2048 + 128 * g, 128, "ng"))
        u.append(("cm%d" % g, "w_in", 8, C_XBC + 2560 + 128 * g, 128, "ng"))
    for g in range(4):
        u.append(("zb%d" % g, "w_in", 8, C_ZB + 512 * g, 512, "ng"))
    for i in range(2):
        u.append(("gb%d" % i, "w_in", 8, C_GB + 512 * i, 512, "ng"))
    for i in range(4):
        u.append(("ob%d" % i, "w_ob", 16, 256 * i, 256, "sng"))
    for i in range(2):
        u.append(("out%d" % i, "w_out", 8, 512 * i, 512, None))
    for i in range(2):
        u.append(("pg%d" % i, "w_pg", 8, 512 * i, 512, "png"))
    for i in range(2):
        u.append(("ple%d" % i, "w_ple", 2, 512 * i, 512, None))
    return u


CONST_SPECS = [
    ("ident_b", [128], BF16), ("tri_b", [128], BF16), ("sut_b", [128], BF16),
    ("tri_f", [128], F32), ("ones_f", [128], F32),
    ("ng", [8], F32), ("png", [8], F32), ("sng", [16], F32), ("lng", [8], F32), ("lnb", [8], F32),
    ("convw", [96], F32), ("convb", [24], F32),
    ("fg_bc", [1024], F32), ("dtb_bc", [32], F32), ("alog_bc", [32], F32), ("dskip_bc", [32], F32),
    ("wsT", [512], F32), ("bs_bc", [512], F32),
]


def build(n_tiles, debug=None, stop_after=None):
    nc = bass.Bass("TRN2", target_bir_lowering=False)
    ntok = n_tiles * T
    dram = {}
    dram["x"] = nc.dram_tensor("x", [ntok, D], F32, kind="ExternalInput").ap()
    dram["p"] = nc.dram_tensor("p", [ntok, PLE], F32, kind="ExternalInput").ap()
    dram["w_in"] = nc.dram_tensor("w_in", [D, NIN], F32, kind="ExternalInput").ap()
    dram["w_oa"] = nc.dram_tensor("w_oa", [D, D], F32, kind="ExternalInput").ap()
    dram["w_ob"] = nc.dram_tensor("w_ob", [2 * D, D], F32, kind="ExternalInput").ap()
    dram["w_out"] = nc.dram_tensor("w_out", [D, D], F32, kind="ExternalInput").ap()
    dram["w_pg"] = nc.dram_tensor("w_pg", [D, D], F32, kind="ExternalInput").ap()
    dram["w_ple"] = nc.dram_tensor("w_ple", [PLE, D], F32, kind="ExternalInput").ap()
    for name, shp, dt in CONST_SPECS:
        dram[name] = nc.dram_tensor("c_" + name, [128] + shp, F32, kind="ExternalInput").ap()
    out_d = nc.dram_tensor("out", [ntok, D], F32, kind="ExternalOutput").ap()
    units = weight_units()
    uoff = {}
    off = 0
    for (nm, src, K, c0, ncol, rs) in units:
        uoff[nm] = off
        off += 128 * K * ncol
    wq = nc.dram_tensor("wq", [off], BF16, kind="Internal").ap()
    dbg_d = {}
    if debug is not None:
        for hook, lst in debug.items():
            for (nm, ncols, fn) in lst:
                dbg_d[nm] = nc.dram_tensor("dbg_" + nm, [128, ncols], F32, kind="ExternalOutput").ap()

    es = ExitStack()
    with es:
        P = Prog(nc, es)
        ARENA = 206 * 1024
        arena = es.enter_context(nc.sbuf_tensor("arena", [128, ARENA // 2], BF16))
        apos = [0]

        def alloc(name, nbytes):
            nbytes = (nbytes + 63) // 64 * 64
            o = apos[0]
            apos[0] += nbytes
            assert apos[0] <= ARENA, (name, apos[0])
            return Buf(name, arena[:, o // 2:(o + nbytes) // 2], nbytes)

        pm = []
        for i in range(6):
            t = es.enter_context(nc.psum_tensor("pm%d" % i, [128, 512], F32))
            b = Buf("pm%d" % i)
            b.ap = t
            b.excl = True
            pm.append(b)
        ptb = []
        for i in range(2):
            t = es.enter_context(nc.psum_tensor("pt%d" % i, [128, 1024], BF16))
            b = Buf("pt%d" % i)
            b.ap = t
            b.excl = True
            ptb.append(b)
        pmi = [0]
        pti = [0]

        spool = {"pm0": [0, [0, 1, 2]], "pm1": [0, [3, 4, 5]], "pt0": [0, [0]], "pt1": [0, [1]]}

        def nextpm(pool=None):
            if pool is not None:
                st = spool[pool]
                b = pm[st[1][st[0] % len(st[1])]]
                st[0] += 1
                return b
            b = pm[pmi[0] % 6]
            pmi[0] += 1
            return b

        def nextpt(pool=None):
            if pool is not None:
                st = spool[pool]
                b = ptb[st[1][st[0] % len(st[1])]]
                st[0] += 1
                return b
            b = ptb[pti[0] % 2]
            pti[0] += 1
            return b

        xa = [alloc("xa%d" % i, 4096) for i in range(2)]
        hT = [alloc("hT%d" % i, 8192) for i in range(2)]
        hb = [alloc("hb%d" % i, 2048) for i in range(2)]
        mg = alloc("mg", 8192)
        ybT = alloc("ybT", 16384)
        state = alloc("state", 8192)
        prevbf = alloc("prevbf", 4096)
        ring = [alloc("ring%d" % i, 8192) for i in range(4)]
        wdt = alloc("wdt", 512)
        dtr = alloc("dtr", 512)
        dtt = alloc("dtt", 512)
        dA = alloc("dA", 512)
        halo = alloc("halo", 24 * 3 * 2)
        junk = alloc("junk", 2048)
        small = [alloc("small%d" % i, 256) for i in range(4)]
        P1 = alloc("P1", 8192)
        P2 = alloc("P2", 8192)
        P3 = alloc("P3", 16384)
        t1 = [alloc("t1_%d" % i, 2048) for i in range(2)]
        gab = [alloc("gab%d" % i, 1024) for i in range(2)]
        raw = [alloc("raw%d" % i, 515 * 2) for i in range(3)]
        acc = [alloc("acc%d" % i, 2048) for i in range(2)]
        Xdt = [alloc("Xdt%d" % i, 1024) for i in range(2)]
        XD = [alloc("XD%d" % i, 1024) for i in range(2)]
        XdD = [alloc("XdD%d" % i, 1024) for i in range(2)]
        Btm = [alloc("Btm%d" % i, 256) for i in range(2)]
        S4 = [alloc("S4_%d" % i, 4096) for i in range(2)]
        Lb = [alloc("L%d" % i, 2048) for i in range(2)]
        Gb = [alloc("G%d" % i, 2048) for i in range(2)]
        CBm = [alloc("CBm%d" % i, 256) for i in range(2)]
        y1 = [alloc("y1_%d" % i, 2048) for i in range(2)]
        ybb = [alloc("yb%d" % i, 1024) for i in range(2)]
        tmpst = [alloc("tmp%d" % i, 2048) for i in range(2)]
        ssd_s = [alloc("ssd_s%d" % i, 1024) for i in range(2)]
        C = {}
        for name, shp, dt in CONST_SPECS:
            n = shp[0]
            if name == "wsT":
                C[name] = y1[0]
            elif name == "bs_bc":
                C[name] = y1[1]
            else:
                C[name] = alloc("c_" + name, n * (2 if dt == BF16 else 4))
        a_bc = alloc("a_bc", 32 * 4)
        wsTm_f = tmpst[0]
        wsTm_b = alloc("wsTm_b", 512 * 2)
        BR = alloc("BR", 1024 * 4)
        neghalf = alloc("neghalf", 64)
        print("arena used", apos[0])

        def k1(j): return (P1, j, j * 1024, (j + 1) * 1024)
        def k1hp(c): return (P1, ("hp", c), 0, 8192)
        def k2(j): return (P2, j, j * 1024, (j + 1) * 1024)
        def k2zb(s): return (P2, ("zb", s), s * 4096, (s + 1) * 4096)
        def k2gate(c, i): return (P2, ("gate", c, i), c * 2048 + i * 1024, c * 2048 + (i + 1) * 1024)
        def k3vg(s): return (P3, ("vg", s), s * 4096, (s + 1) * 4096)
        def k3vn(c): return (P3, ("vn", c), 8192 + c * 2048, 8192 + (c + 1) * 2048)
        def k3xc(s): return (P3, ("xc", s), s * 6144, (s + 1) * 6144)
        def k3x1(c): return (P3, ("x1", c), c * 4096, (c + 1) * 4096)
        def kL(q, hh): return (Lb[q], hh, hh * 1024, (hh + 1) * 1024)
        def kLpT(c): return (Lb[0], ("pT", c), 0, 2048)
        def kS4(q, ab): return (S4[q], ab, 0, 2048) if ab == "a" else (S4[q], ab, 2048, 4096)

        ident_b = C["ident_b"].v(BF16)
        tri_b = C["tri_b"].v(BF16)
        sut_b = C["sut_b"].v(BF16)
        tri_f = C["tri_f"].v(F32)
        ones_f = C["ones_f"].v(F32)

        ssem_cnt = [0]
        store_ops = []

        def act(out, in_, func, reads, writes, bias=None, scale=None, accum=None):
            kw = {}
            if bias is not None:
                kw["bias"] = bias
            if scale is not None:
                kw["scale"] = scale
            if accum is not None:
                kw["accum_out"] = accum
            return P.add("act", lambda e: e.activation(out=out, in_=in_, func=func, **kw), reads, writes)

        def tt(eng, out, in0, in1, op, reads, writes):
            return P.add(eng, lambda e: e.tensor_tensor(out=out, in0=in0, in1=in1, op=op), reads, writes)

        def ts(eng, out, in0, s1, s2, op0, op1, reads, writes):
            if op1 is None:
                return P.add(eng, lambda e: e.tensor_scalar(out=out, in0=in0, scalar1=s1, scalar2=None, op0=op0), reads, writes)
            return P.add(eng, lambda e: e.tensor_scalar(out=out, in0=in0, scalar1=s1, scalar2=s2, op0=op0, op1=op1), reads, writes)

        def stt(out, in0, scalar, in1, op0, op1, reads, writes):
            return P.add("dve", lambda e: e.scalar_tensor_tensor(out=out, in0=in0, scalar=scalar, in1=in1, op0=op0, op1=op1), reads, writes)

        def cp(eng, out, in_, reads, writes):
            if eng == "act":
                return P.add("act", lambda e: e.copy(out=out, in_=in_), reads, writes)
            return P.add(eng, lambda e: e.tensor_copy(out=out, in_=in_), reads, writes)

        mm_pend = []

        def mm(out, lhsT, rhs, start, stop, reads, writes):
            if not stop:
                mm_pend.extend(reads)
                return P.add("pe", lambda e: e.matmul(out, lhsT, rhs, start=start, stop=stop), reads, writes, reg_reads=False)
            allr = list(reads) + list(mm_pend)
            del mm_pend[:]
            return P.add("pe", lambda e: e.matmul(out, lhsT, rhs, start=start, stop=stop), allr, writes)

        def tp(out, in_, reads, writes, ident=None):
            idn = ident_b if ident is None else ident
            return P.add("pe", lambda e: e.transpose(out, in_, idn), list(reads) + [C["ident_b"]], writes)

        def dma(q, out, in_, reads, writes, sem):
            return P.add(q, lambda e: e.dma_start(out=out, in_=in_), reads, writes, dma=sem)

        def rsqrt_chain(dst_ap, src_ap, mulc, srcbuf, dstbuf):
            ts("dve", dst_ap, src_ap, mulc, EPS, ALU.mult, ALU.add, [srcbuf], [dstbuf])
            act(dst_ap, dst_ap, AF.Ln, [dstbuf], [dstbuf])
            act(dst_ap, dst_ap, AF.Exp, [dstbuf], [dstbuf], scale=-0.5)

        for name, shp, dt in CONST_SPECS:
            b = C[name]
            dma("pool", b.v(dt)[:, :shp[0]], dram[name], [], [b], "const")
        P.add("pool", lambda e: e.memset(junk.v(BF16)[:, 0:8], 0.0), list(C.values()), list(C.values()))
        act(a_bc.v(F32), C["alog_bc"].v(F32), AF.Exp, [C["alog_bc"]], [a_bc])
        ts("dve", a_bc.v(F32), a_bc.v(F32), -1.0, None, ALU.mult, None, [a_bc], [a_bc])
        wsT3 = C["wsT"].v(F32, "p (g t) -> p g t", g=4)
        tt("dve", wsTm_f.v(F32, "p (g t) -> p g t", g=4), wsT3,
           tri_f.unsqueeze(1).to_broadcast([128, 4, 128]), ALU.mult, [C["wsT"], C["tri_f"]], [wsTm_f])
        cp("dve", wsTm_b.v(BF16), wsTm_f.v(F32), [wsTm_f], [wsTm_b])
        bk = nextpm()
        mm(bk.ap[:, :], ones_f, wsTm_f.v(F32), True, True, [C["ones_f"], wsTm_f], [bk])
        BR3 = BR.v(F32, "p (j t) -> p j t", j=8)
        bs3 = C["bs_bc"].v(F32, "p (g t) -> p g t", g=4)
        lnb = C["lnb"].v(F32)
        lng = C["lng"].v(F32)
        for j in range(8):
            g = j // 2
            stt(BR3[:, j, :], bk.ap[:, g * 128:(g + 1) * 128], lnb[:, j:j + 1], bs3[:, g, :], ALU.mult, ALU.add,
                [bk, C["lnb"], C["bs_bc"]], [(BR, j)])
        P.add("pool", lambda e: e.memset(neghalf.v(F32), -0.5), [], [neghalf])
        P.add("pool", lambda e: e.memset(state.v(F32), 0.0), [], [state])
        P.add("pool", lambda e: e.memset(prevbf.v(BF16), 0.0), [], [prevbf])
        P.add("pool", lambda e: e.memset(halo.v(BF16), 0.0), [], [halo])

        wq_bufs = {nm: Buf("wq_" + nm) for (nm, *_r) in units}
        stg_in = [t1[0], t1[1], acc[0], acc[1], Lb[0], Lb[1], Gb[0], Gb[1]]
        stg_out = [gab[0], gab[1], Xdt[0], Xdt[1], XD[0], XD[1], XdD[0], XdD[1]]
        NSTG = 8
        cnt = 0
        for (nm, src, K, c0, ncol, rs) in units:
            dst = wq[uoff[nm]:uoff[nm] + 128 * K * ncol].rearrange("(p k c) -> p k c", p=128, k=K)
            for k in range(K):
                si = stg_in[cnt % NSTG]
                so = stg_out[cnt % NSTG]
                dma("sp", si.v(F32)[:, :ncol], dram[src][k * 128:(k + 1) * 128, c0:c0 + ncol], [], [si], "stgi%d" % (cnt % NSTG))
                eng = "act" if cnt % 2 == 0 else "dve"
                if rs is None:
                    cp(eng, so.v(BF16)[:, :ncol], si.v(F32)[:, :ncol], [si], [so])
                else:
                    sc = C[rs].v(F32)[:, k:k + 1]
                    if eng == "act":
                        act(so.v(BF16)[:, :ncol], si.v(F32)[:, :ncol], AF.Identity, [si, C[rs]], [so], scale=sc)
                    else:
                        ts("dve", so.v(BF16)[:, :ncol], si.v(F32)[:, :ncol], sc, None, ALU.mult, None, [si, C[rs]], [so])
                dma("pool", dst[:, k, :], so.v(BF16)[:, :ncol], [so], [(wq_bufs[nm], k)], "stgo%d" % (cnt % NSTG))
                cnt += 1

        ring_i = [0]

        def load_unit(nm):
            (_, src, K, c0, ncol, rs) = [u for u in units if u[0] == nm][0]
            if nm == "dt":
                slot = wdt
                semn = "wdt"
            else:
                slot = ring[ring_i[0] % 4]
                semn = "ring%d" % (ring_i[0] % 4)
                ring_i[0] += 1
            srcap = wq[uoff[nm]:uoff[nm] + 128 * K * ncol].rearrange("(p k c) -> p k c", p=128, k=K)
            view = slot.v(BF16)[:, :K * ncol].rearrange("p (k c) -> p k c", k=K)
            dma("sp", view, srcap, [wq_bufs[nm]], [slot], semn)
            return slot, view

        def zipper(gens):
            gens = list(gens)
            while gens:
                for gen in list(gens):
                    try:
                        next(gen)
                    except StopIteration:
                        gens.remove(gen)

        def stage_a(ti):
            for _ in stage_a_gen(ti):
                pass

        def stage_a_gen(ti):
            sl = ti % 2
            hT3 = hT[sl].v(BF16, "p (k t) -> p k t", k=8)
            for c in range(NCH):
                xb = xa[c % 2]
                r0 = ti * T + c * 128
                dma("sp", xb.v(F32), dram["x"][r0:r0 + 128, :], [], [xb], "xa%d" % (c % 2))
                ss = small[0]
                hbb = hb[c % 2]
                act(hbb.v(BF16), xb.v(F32), AF.Square, [xb], [(ss, 0), hbb], accum=ss.v(F32)[:, 0:1])
                rsqrt_chain(ss.v(F32)[:, 1:2], ss.v(F32)[:, 0:1], 1.0 / D, (ss, 0), (ss, 1))
                yield
                act(hbb.v(BF16), xb.v(F32), AF.Identity, [xb, (ss, 1)], [hbb], scale=ss.v(F32)[:, 1:2])
                pb_ = nextpt()
                for k in range(8):
                    tp(pb_.ap[:, k * 128:(k + 1) * 128], hbb.v(BF16)[:, k * 128:(k + 1) * 128], [hbb], [(pb_, k)])
                cp("dve", hT3[:, :, c * 128:(c + 1) * 128], pb_.ap.rearrange("p (k t) -> p k t", k=8), [pb_], [(hT[sl], c)])
                yield

        def fm_proj(view, hT3, jj, hTbuf, slot):
            bk = nextpm()
            for k in range(8):
                mm(bk.ap[:, :], view[:, k, jj * 128:(jj + 1) * 128], hT3[:, k, :], k == 0, k == 7, [slot, hTbuf], [bk])
            return bk

        def a_branch(ti):
            sl = ti % 2
            hTb = hT[sl]
            hT3 = hTb.v(BF16, "p (k t) -> p k t", k=8)
            ug3 = P1.v(BF16, "p (k t) -> p k t", k=8)
            zs3 = P2.v(BF16, "p (k t) -> p k t", k=8)
            sv = [load_unit("v0"), load_unit("v1")]
            vg = [Buf("vg0", P3.raw[:, 0:2048], 4096), Buf("vg1", P3.raw[:, 2048:4096], 4096)]
            vn = Buf("vn", P3.raw[:, 4096:8192], 8192)
            vn3 = vn.v(BF16, "p (c d) -> p c d", c=4)
            st = small[1]
            stv = st.v(F32)
            for cp_ in range(2):
                for s in range(2):
                    c = cp_ * 2 + s
                    vgv = vg[s].v(F32)
                    for i in range(2):
                        slot, view = sv[i]
                        bk = nextpm()
                        for k in range(8):
                            mm(bk.ap[:, :], hT3[:, k, c * 128:(c + 1) * 128], view[:, k, :], k == 0, k == 7, [slot, hTb], [bk])
                        act(vgv[:, i * 512:(i + 1) * 512], bk.ap[:, :], AF.Gelu_apprx_tanh, [bk], [k3vg(s)])
                        P.add("dve", lambda e, i=i, s=s, vgv=vgv: e.bn_stats(out=stv[:, s * 12 + i * 6:s * 12 + (i + 1) * 6], in_=vgv[:, i * 512:(i + 1) * 512]),
                              [k3vg(s)], [(st, ("bs", s, i))])
                    P.add("dve", lambda e, s=s: e.bn_aggr(out=stv[:, 32 + 2 * s:34 + 2 * s], in_=stv[:, s * 12:(s + 1) * 12]),
                          [(st, ("bs", s, 0)), (st, ("bs", s, 1))], [(st, ("ag", s))])
                    ts("dve", stv[:, 40 + s:41 + s], stv[:, 33 + 2 * s:34 + 2 * s], 1.0, EPS, ALU.mult, ALU.add, [(st, ("ag", s))], [(st, "rs")])
                act(stv[:, 40:42], stv[:, 40:42], AF.Ln, [(st, "rs")], [(st, "rs")])
                act(stv[:, 40:42], stv[:, 40:42], AF.Exp, [(st, "rs")], [(st, "rs")], scale=-0.5)
                for s in range(2):
                    c = cp_ * 2 + s
                    ts("dve", vn3[:, c, :], vg[s].v(F32), stv[:, 32 + 2 * s:33 + 2 * s], stv[:, 40 + s:41 + s], ALU.subtract, ALU.mult,
                       [k3vg(s), (st, ("ag", s)), (st, "rs")], [k3vn(c)])
            for i in range(2):
                slot, view = load_unit("u%d" % i)
                for jj in range(4):
                    j = i * 4 + jj
                    bk = fm_proj(view, hT3, jj, hTb, slot)
                    act(ug3[:, j, :], bk.ap[:, :], AF.Gelu_apprx_tanh, [bk], [k1(j)])
            for i in range(2):
                slot, view = load_unit("za%d" % i)
                for jj in range(4):
                    j = i * 4 + jj
                    bk = fm_proj(view, hT3, jj, hTb, slot)
                    act(zs3[:, j, :], bk.ap[:, :], AF.Silu, [bk], [k2(j)])
                    tt("dve", ug3[:, j, :], ug3[:, j, :], zs3[:, j, :], ALU.mult, [k1(j), k2(j)], [k1(j)])
            wsb3 = wsTm_b.v(BF16, "p (g t) -> p g t", g=4)
            for j in range(8):
                g = j // 2
                bk = nextpm()
                for c in range(NCH):
                    mm(bk.ap[:, c * 128:(c + 1) * 128], vn3[:, c, j * 128:(j + 1) * 128], wsb3[:, g, :], True, True,
                       [k3vn(c), wsTm_b], [bk])
                tb = t1[j % 2]
                stt(tb.v(F32, "p (c t) -> p c t", c=4), bk.ap.rearrange("p (c t) -> p c t", c=4), lng[:, j:j + 1],
                    BR3[:, j, :].unsqueeze(1).to_broadcast([128, 4, 128]), ALU.mult, ALU.add,
                    [bk, C["lng"], (BR, j)], [tb])
                tt("dve", ug3[:, j, :], tb.v(F32), ug3[:, j, :], ALU.mult, [tb, k1(j)], [k1(j)])
            mg3 = mg.v(BF16, "p (k t) -> p k t", k=8)
            sga = [load_unit("ga0"), load_unit("ga1")]
            soa = [load_unit("oa0"), load_unit("oa1")]
            for j in range(8):
                i, jj = j // 4, j % 4
                bk = fm_proj(sga[i][1], hT3, jj, hTb, sga[i][0])
                gt = gab[j % 2]
                act(gt.v(BF16), bk.ap[:, :], AF.Sigmoid, [bk], [gt])
                bk2 = nextpm()
                for k in range(8):
                    mm(bk2.ap[:, :], soa[i][1][:, k, jj * 128:(jj + 1) * 128], ug3[:, k, :], k == 0, k == 7, [soa[i][0], P1], [bk2])
                tt("dve", mg3[:, j, :], bk2.ap[:, :], gt.v(BF16), ALU.mult, [bk2, gt], [(mg, j)])

        def b_branch(ti):
            sl = ti % 2
            hTb = hT[sl]
            hT3 = hTb.v(BF16, "p (k t) -> p k t", k=8)
            slot, view = load_unit("dt")
            bk = nextpm()
            for c in range(NCH):
                for k in range(8):
                    mm(bk.ap[:, c * 32:(c + 1) * 32], hT3[:, k, c * 128:(c + 1) * 128], view[:, k, :], k == 0, k == 7, [slot, hTb], [bk])
            dtr3 = dtr.v(F32, "p (c h) -> p c h", c=4)
            tt("dve", dtr3, bk.ap[:, 0:128].rearrange("p (c h) -> p c h", c=4),
               C["dtb_bc"].v(F32).unsqueeze(1).to_broadcast([128, 4, 32]), ALU.add, [bk, C["dtb_bc"]], [dtr])
            stt(dtt.v(F32), dtr.v(F32), -1.0, dtr.v(F32), ALU.mult, ALU.max, [dtr], [dtt])
            act(dtt.v(F32), dtt.v(F32), AF.Exp, [dtt], [dtt], scale=-1.0)
            act(dtt.v(F32), dtt.v(F32), AF.Ln, [dtt], [dtt], bias=1.0)
            stt(dtt.v(F32), dtr.v(F32), 0.0, dtt.v(F32), ALU.max, ALU.add, [dtr, dtt], [dtt])
            tt("dve", dA.v(F32, "p (c h) -> p c h", c=4), dtt.v(F32, "p (c h) -> p c h", c=4),
               a_bc.v(F32).unsqueeze(1).to_broadcast([128, 4, 32]), ALU.mult, [dtt, a_bc], [dA])
            dt3 = dtt.v(F32, "p (c h) -> p c h", c=4)
            dA3 = dA.v(F32, "p (c h) -> p c h", c=4)
            if stop_after == "b_dt":
                raise _Stop()
            halo3 = halo.v(BF16)[:, :72].rearrange("p (j t) -> p j t", j=24)
            cw = C["convw"].v(F32, "p (j k) -> p j k", j=24)
            cb = C["convb"].v(F32)
            ybT3 = ybT.v(BF16, "p (k t) -> p k t", k=16)
            st4 = state.v(F32, "p (g f) -> p g f", g=4)
            pv4 = prevbf.v(BF16, "p (g f) -> p g f", g=4)
            groupctx = {}

            def proj_group(g):
                xcb = Buf("xc", P3.raw[:, (g % 2) * 3072:(g % 2) * 3072 + 3072], 6144)
                xck = ("xc", g % 2)
                xc3 = xcb.v(BF16, "p (j t) -> p j t", j=6)
                wu = [load_unit("xs%d" % g)]
                s_bm = load_unit("bm%d" % g)
                s_cm = load_unit("cm%d" % g)
                tiles = [(wu[0], jj, g * 4 + jj) for jj in range(4)] + [(s_bm, 0, 16 + g), (s_cm, 0, 20 + g)]
                for li, ((slot, view), jj, jglob) in enumerate(tiles):
                    bk = fm_proj(view, hT3, jj, hTb, slot)
                    rw = raw[(g * 6 + li) % 3]
                    rv = rw.v(BF16)
                    cp("act", rv[:, 3:515], bk.ap[:, :], [bk], [(rw, "b")])
                    cp("pool", rv[:, 0:3], halo3[:, jglob, :], [(halo, jglob)], [(rw, "h")])
                    ab = acc[(g * 6 + li) % 2]
                    av = ab.v(F32)
                    act(av, bk.ap[:, :], AF.Identity, [bk, C["convw"], C["convb"]], [ab],
                        scale=cw[:, jglob, 3:4], bias=cb[:, jglob:jglob + 1])
                    cp("pool", halo3[:, jglob, :], rv[:, 512:515], [(rw, "b")], [(halo, jglob)])
                    for kk in (2, 1, 0):
                        stt(av, rv[:, kk:kk + 512], cw[:, jglob, kk:kk + 1], av, ALU.mult, ALU.add, [rw, ab, C["convw"]], [ab])
                    act(xc3[:, li, :], av, AF.Silu, [ab], [k3xc(g % 2)])
                if stop_after == "b_conv":
                    raise _Stop()
                zbb = Buf("zb", P2.raw[:, (g % 2) * 2048:(g % 2) * 2048 + 2048], 4096)
                zbk = ("zb", g % 2)
                zb3 = zbb.v(BF16, "p (c f) -> p c f", c=4)
                slot, view = load_unit("zb%d" % g)
                for c in range(NCH):
                    bk = nextpm()
                    for k in range(8):
                        mm(bk.ap[:, :], hT3[:, k, c * 128:(c + 1) * 128], view[:, k, :], k == 0, k == 7, [slot, hTb], [bk])
                    act(zb3[:, c, :], bk.ap[:, :], AF.Silu, [bk], [k2zb(g % 2)])
                if stop_after == "b_zb":
                    raise _Stop()
                groupctx[g] = (xc3, zb3)

            def ssd_iter(g, c, q):
                xc3, zb3 = groupctx[g]
                xk = k3xc(g % 2)
                zk = k2zb(g % 2)
                hs = slice(g * 8, g * 8 + 8)
                ssb = ssd_s[q]
                sv_ = ssb.v(F32)
                cs_ap, ecs_ap, dec_ap, cd_ap = sv_[:, 0:8], sv_[:, 8:16], sv_[:, 16:24], sv_[:, 24:32]
                dAh = ssb.v(BF16)[:, 64:72]
                dAl = ssb.v(BF16)[:, 72:80]
                csl = slice(c * 128, (c + 1) * 128)
                bk = nextpm("pm%d" % q)
                mm(bk.ap[:, 0:8], tri_f, dA3[:, c, hs], True, True, [C["tri_f"], dA], [(bk, 0)])
                mm(bk.ap[:, 8:16], ones_f, dA3[:, c, hs], True, True, [C["ones_f"], dA], [(bk, 1)])
                cp("dve", dAh, dA3[:, c, hs], [dA], [(ssb, "hi")])
                tt("dve", dAl, dA3[:, c, hs], dAh, ALU.subtract, [dA, (ssb, "hi")], [(ssb, "lo")])
                pb_ = nextpt("pt%d" % q)
                for jj in range(4):
                    tp(pb_.ap[:, jj * 128:(jj + 1) * 128], xc3[:, jj, csl], [xk], [(pb_, jj)])
                tp(pb_.ap[:, 512:640], xc3[:, 4, csl], [xk], [(pb_, 4)])
                yield
                rhi = S4[q].v(BF16)[:, 0:1024].rearrange("p (h l) -> p h l", h=8)
                rlo = S4[q].v(BF16)[:, 1024:2048].rearrange("p (h l) -> p h l", h=8)
                trib = tri_b.unsqueeze(1).to_broadcast([128, 8, 128])
                tt("dve", rhi, trib, dAh.unsqueeze(2).to_broadcast([128, 8, 128]), ALU.mult, [C["tri_b"], (ssb, "hi")], [kS4(q, "a")])
                tt("dve", rlo, trib, dAl.unsqueeze(2).to_broadcast([128, 8, 128]), ALU.mult, [C["tri_b"], (ssb, "lo")], [kS4(q, "b")])
                cp("dve", cs_ap, bk.ap[:, 0:8], [bk], [(ssb, "cs")])
                tt("dve", dec_ap, bk.ap[:, 8:16], cs_ap, ALU.subtract, [bk, (ssb, "cs")], [(ssb, "dec")])
                act(ecs_ap, bk.ap[:, 0:8], AF.Exp, [bk], [(ssb, "ecs")])
                act(cd_ap, bk.ap[:, 8:16], AF.Exp, [bk], [(ssb, "cd")])
                act(dec_ap, dec_ap, AF.Exp, [(ssb, "dec")], [(ssb, "dec")])
                yield
                xv = pb_.ap[:, 0:512].rearrange("p (h d) -> p h d", h=8)
                tt("dve", Xdt[q].v(BF16, "p (h d) -> p h d", h=8), xv,
                   dt3[:, c, hs].unsqueeze(2).to_broadcast([128, 8, 64]), ALU.mult, [pb_, dtt], [Xdt[q]])
                tt("dve", XD[q].v(BF16, "p (h d) -> p h d", h=8), xv,
                   C["dskip_bc"].v(F32)[:, hs].unsqueeze(2).to_broadcast([128, 8, 64]), ALU.mult, [pb_, C["dskip_bc"]], [XD[q]])
                cp("dve", Btm[q].v(BF16), pb_.ap[:, 512:640], [pb_], [Btm[q]])
                bk3 = nextpm("pm%d" % q)
                mm(bk3.ap[:, 0:128], xc3[:, 4, csl], xc3[:, 5, csl], True, True, [xk], [bk3])
                yield
                L3 = Lb[q].v(BF16, "p (h l) -> p h l", h=8)
                for hh in range(2):
                    bk2 = nextpm("pm%d" % q)
                    mm(bk2.ap[:, :], sut_b, S4[q].v(BF16)[:, hh * 512:(hh + 1) * 512], True, False, [C["sut_b"], kS4(q, "a")], [bk2])
                    mm(bk2.ap[:, :], sut_b, S4[q].v(BF16)[:, 1024 + hh * 512:1024 + (hh + 1) * 512], False, True, [C["sut_b"], kS4(q, "b")], [bk2])
                    act(Lb[q].v(BF16)[:, hh * 512:(hh + 1) * 512], bk2.ap[:, :], AF.Exp, [bk2], [kL(q, hh)])
                tt("dve", CBm[q].v(BF16), bk3.ap[:, 0:128], tri_b, ALU.mult, [bk3, C["tri_b"]], [CBm[q]])
                yield
                G3 = Gb[q].v(BF16, "p (h l) -> p h l", h=8)
                tt("dve", G3, L3, CBm[q].v(BF16).unsqueeze(1).to_broadcast([128, 8, 128]), ALU.mult, [Lb[q], CBm[q]], [Gb[q]])
                tt("dve", XdD[q].v(BF16, "p (h d) -> p h d", h=8), Xdt[q].v(BF16, "p (h d) -> p h d", h=8),
                   dec_ap.unsqueeze(2).to_broadcast([128, 8, 64]), ALU.mult, [Xdt[q], (ssb, "dec")], [XdD[q]])
                yield
                bko = nextpm("pm%d" % q)
                mm(bko.ap[:, :], xc3[:, 5, csl], pv4[:, g, :], True, True, [xk, (prevbf, g)], [bko])
                bkd = nextpm("pm%d" % q)
                mm(bkd.ap[:, :], ident_b, XD[q].v(BF16), True, False, [C["ident_b"], XD[q]], [bkd])
                for r in range(8):
                    mm(bkd.ap[:, r * 64:(r + 1) * 64], G3[:, r, :], Xdt[q].v(BF16)[:, r * 64:(r + 1) * 64], False, r == 7, [Gb[q], Xdt[q]], [bkd])
                bks = nextpm("pm%d" % q)
                mm(bks.ap[:, :], Btm[q].v(BF16), XdD[q].v(BF16), True, True, [Btm[q], XdD[q]], [bks])
                tt("pool", tmpst[q].v(F32, "p (h d) -> p h d", h=8), st4[:, g, :].rearrange("p (h d) -> p h d", h=8),
                   cd_ap.unsqueeze(2).to_broadcast([128, 8, 64]), ALU.mult, [(state, g), (ssb, "cd")], [tmpst[q]])
                yield
                yv = y1[q].v(F32)
                tt("dve", y1[q].v(F32, "p (h d) -> p h d", h=8), bko.ap.rearrange("p (h d) -> p h d", h=8),
                   ecs_ap.unsqueeze(2).to_broadcast([128, 8, 64]), ALU.mult, [bko, (ssb, "ecs")], [y1[q]])
                tt("dve", yv, yv, bkd.ap[:, :], ALU.add, [bkd, y1[q]], [y1[q]])
                tt("dve", st4[:, g, :], tmpst[q].v(F32), bks.ap[:, :], ALU.add, [bks, tmpst[q]], [(state, g)])
                cp("act", pv4[:, g, :], st4[:, g, :], [(state, g)], [(prevbf, g)])
                yield
                tt("pool", yv, yv, zb3[:, c, :], ALU.mult, [y1[q], zk], [y1[q]])
                act(ybb[q].v(BF16), yv, AF.Square, [y1[q]], [(ssb, "ss"), ybb[q]], accum=sv_[:, 50:51])
                yield
                rsqrt_chain(sv_[:, 51:52], sv_[:, 50:51], 1.0 / 512, (ssb, "ss"), (ssb, "rs"))
                yield
                act(ybb[q].v(BF16), yv, AF.Identity, [y1[q], (ssb, "rs")], [ybb[q]], scale=sv_[:, 51:52])
                pb2 = nextpt("pt%d" % q)
                for jj in range(4):
                    tp(pb2.ap[:, jj * 128:(jj + 1) * 128], ybb[q].v(BF16)[:, jj * 128:(jj + 1) * 128], [ybb[q]], [(pb2, jj)])
                cp("act", ybT3[:, g * 4:(g + 1) * 4, csl], pb2.ap[:, 0:512].rearrange("p (k t) -> p k t", k=4), [pb2], [(ybT, (g, c))])
                yield

            def stream(g, q):
                for c in range(NCH):
                    yield from ssd_iter(g, c, q)

            def zipper(gens):
                gens = list(gens)
                while gens:
                    for gen in list(gens):
                        try:
                            next(gen)
                        except StopIteration:
                            gens.remove(gen)

            for pair in range(2):
                mark("b_proj%d" % pair, ti)
                proj_group(2 * pair)
                proj_group(2 * pair + 1)
                mark("b_ssd%d" % pair, ti)
                zipper([stream(2 * pair, 0), stream(2 * pair + 1, 1)])

        def ob_merge(ti):
            gens = [ob_merge_gen(ti)]
            if ti + 1 < n_tiles:
                gens.append(stage_a_gen(ti + 1))
            zipper(gens)

        def ob_merge_gen(ti):
            sl = ti % 2
            hTb = hT[sl]
            hT3 = hTb.v(BF16, "p (k t) -> p k t", k=8)
            mg3 = mg.v(BF16, "p (k t) -> p k t", k=8)
            ybT3 = ybT.v(BF16, "p (k t) -> p k t", k=16)
            sgb = [None, None]
            for j in range(8):
                i, jj = j // 4, j % 4
                if jj == 0:
                    sgb[i] = load_unit("gb%d" % i)
                if j % 2 == 0:
                    sob = load_unit("ob%d" % (j // 2))
                    obv = sob[0].v(BF16)[:, :16 * 256].rearrange("p (k c) -> p k c", k=16)
                bk = fm_proj(sgb[i][1], hT3, jj, hTb, sgb[i][0])
                gt = gab[j % 2]
                act(gt.v(BF16), bk.ap[:, :], AF.Sigmoid, [bk], [gt])
                bk2 = nextpm()
                for k in range(16):
                    mm(bk2.ap[:, :], obv[:, k, (j % 2) * 128:(j % 2 + 1) * 128], ybT3[:, k, :], k == 0, k == 15, [sob[0], ybT], [bk2])
                tb = t1[j % 2]
                tt("dve", tb.v(F32), bk2.ap[:, :], gt.v(BF16), ALU.mult, [bk2, gt], [tb])
                tt("pool", mg3[:, j, :], mg3[:, j, :], tb.v(F32), ALU.add, [(mg, j), tb], [(mg, j)])
                yield

        def tail(ti):
            mg3 = mg.v(BF16, "p (k t) -> p k t", k=8)
            x1 = P3.v(F32, "p (c d) -> p c d", c=4)
            hpT3 = P1.v(BF16, "p (k t) -> p k t", k=8)
            gate3 = P2.v(BF16, "p (c d) -> p c d", c=4)
            pT3 = Lb[0].v(BF16)[:, :1024].rearrange("p (k t) -> p k t", k=2)
            for c in range(NCH):
                r0 = ti * T + c * 128
                dma("sp", x1[:, c, :], dram["x"][r0:r0 + 128, :], [], [k3x1(c)], "x1_%d" % c)
            for i in range(2):
                slot, view = load_unit("out%d" % i)
                for c in range(NCH):
                    bk = nextpm()
                    for k in range(8):
                        mm(bk.ap[:, :], mg3[:, k, c * 128:(c + 1) * 128], view[:, k, :], k == 0, k == 7, [slot, mg], [bk])
                    tt("dve", x1[:, c, i * 512:(i + 1) * 512], x1[:, c, i * 512:(i + 1) * 512], bk.ap[:, :], ALU.add,
                       [bk, k3x1(c)], [k3x1(c)])
            for c in range(NCH):
                ss = small[2]
                hpb = acc[c % 2]
                act(hpb.v(BF16)[:, :1024], x1[:, c, :], AF.Square, [k3x1(c)], [(ss, 0), hpb], accum=ss.v(F32)[:, 0:1])
                rsqrt_chain(ss.v(F32)[:, 1:2], ss.v(F32)[:, 0:1], 1.0 / D, (ss, 0), (ss, 1))
                act(hpb.v(BF16)[:, :1024], x1[:, c, :], AF.Identity, [k3x1(c), (ss, 1)], [hpb], scale=ss.v(F32)[:, 1:2])
                pb_ = nextpt()
                for k in range(8):
                    tp(pb_.ap[:, k * 128:(k + 1) * 128], hpb.v(BF16)[:, k * 128:(k + 1) * 128], [hpb], [(pb_, k)])
                cp("dve", hpT3[:, :, c * 128:(c + 1) * 128], pb_.ap.rearrange("p (k t) -> p k t", k=8), [pb_], [k1hp(c)])
                r0 = ti * T + c * 128
                ptb_ = Xdt[c % 2]
                pbb = XD[c % 2]
                dma("sp", ptb_.v(F32), dram["p"][r0:r0 + 128, :], [], [ptb_], "pt%d" % (c % 2))
                cp("dve", pbb.v(BF16)[:, :256], ptb_.v(F32), [ptb_], [pbb])
                pb3 = nextpt()
                for k in range(2):
                    tp(pb3.ap[:, k * 128:(k + 1) * 128], pbb.v(BF16)[:, k * 128:(k + 1) * 128], [pbb], [(pb3, k)])
                cp("act", pT3[:, :, c * 128:(c + 1) * 128], pb3.ap[:, 0:256].rearrange("p (k t) -> p k t", k=2), [pb3], [kLpT(c)])
            for i in range(2):
                slot, view = load_unit("pg%d" % i)
                for c in range(NCH):
                    bk = nextpm()
                    for k in range(8):
                        mm(bk.ap[:, :], hpT3[:, k, c * 128:(c + 1) * 128], view[:, k, :], k == 0, k == 7, [slot, P1], [bk])
                    act(gate3[:, c, i * 512:(i + 1) * 512], bk.ap[:, :], AF.Sigmoid, [bk], [k2gate(c, i)])
            for i in range(2):
                slot, view0 = load_unit("ple%d" % i)
                view = slot.v(BF16)[:, :1024].rearrange("p (k c) -> p k c", k=2)
                for c in range(NCH):
                    bk = nextpm()
                    for k in range(2):
                        mm(bk.ap[:, :], pT3[:, k, c * 128:(c + 1) * 128], view[:, k, :], k == 0, k == 1, [slot, Lb[0]], [bk])
                    tb = t1[(i * 4 + c) % 2]
                    tt("dve", tb.v(F32), bk.ap[:, :], gate3[:, c, i * 512:(i + 1) * 512], ALU.mult, [bk, k2gate(c, i)], [tb])
                    tt("pool", x1[:, c, i * 512:(i + 1) * 512], x1[:, c, i * 512:(i + 1) * 512], tb.v(F32), ALU.add,
                       [k3x1(c), tb], [k3x1(c)])
            for c in range(NCH):
                ss = small[3]
                act(acc[c % 2].v(BF16)[:, :1024], x1[:, c, :], AF.Square, [k3x1(c)], [(ss, 0), acc[c % 2]], accum=ss.v(F32)[:, 0:1])
                rsqrt_chain(ss.v(F32)[:, 1:2], ss.v(F32)[:, 0:1], 1.0 / D, (ss, 0), (ss, 1))
                ot = S4[c % 2]
                stt(ot.v(F32), x1[:, c, :], ss.v(F32)[:, 1:2], C["fg_bc"].v(F32), ALU.mult, ALU.mult,
                    [k3x1(c), (ss, 1), C["fg_bc"]], [ot])
                r0 = ti * T + c * 128
                so = dma("pool", out_d[r0:r0 + 128, :], ot.v(F32), [ot], [], "ot%d" % (c % 2))
                store_ops.append(so)

        env = locals()

        def dbg_hook(hook):
            if debug is None or hook not in debug:
                return
            for (nm, ncols, fn) in debug[hook]:
                ap, reads = fn(env)
                so = dma("pool", dbg_d[nm][:, :], ap, reads, [], "dbg_" + nm)
                store_ops.append(so)

        phase_marks = []
        build.phase_marks = phase_marks

        def mark(name, ti):
            phase_marks.append((name, ti, len(P.ops["pe"]), len(P.ops["act"]), len(P.ops["dve"]), len(P.ops["pool"])))

        if stop_after != "prologue":
            stage_a(0)
        for ti in range(n_tiles):
          try:
              if stop_after in ("prologue", "stage_a"):
                  break
              mark("a_branch", ti)
              a_branch(ti)
              mark("b_branch", ti)
              if ti == 0:
                  dbg_hook("after_a")
              if stop_after == "a_branch":
                  break
              b_branch(ti)
              if stop_after is not None and stop_after.startswith("b_"):
                  break
              if ti == 0:
                  dbg_hook("after_b")
              mark("stage_a", ti)
              mark("ob_merge", ti)
              ob_merge(ti)
              mark("tail", ti)
              if ti == 0:
                  dbg_hook("after_merge")
              tail(ti)
              mark("end", ti)
              if ti == 0:
                  dbg_hook("end")
          except _Stop:
            dbg_hook("stop")
            break

        for so in store_ops:
            so.needed = True
        P.finalize()
        counts = {k: len(v) for k, v in P.ops.items()}
        print("op counts", counts)
        blk = es.enter_context(nc.Block())

        @blk.tensor
        def _(e):
            P.emit("pe", e)

        @blk.scalar
        def _(e):
            P.emit("act", e)

        @blk.vector
        def _(e):
            P.emit("dve", e)

        @blk.gpsimd
        def _(e):
            P.emit("pool", e, final_waits=store_ops)

        @blk.sync
        def _(e):
            P.emit("sp", e)
    return nc


def host_consts(inp):
    f = np.float32
    k = np.arange(128)
    c = {}
    c["ident_b"] = np.eye(128, dtype=f)
    tri = (k[:, None] <= k[None, :]).astype(f)
    c["tri_b"] = tri
    c["tri_f"] = tri
    c["sut_b"] = (k[:, None] > k[None, :]).astype(f)
    c["ones_f"] = np.ones((128, 128), f)

    def pk(v):
        return np.ascontiguousarray(np.asarray(v, f).reshape(-1, 128).T)
    c["ng"] = pk(inp["norm_g"][0])
    c["png"] = pk(inp["ple_norm_g"][0])
    c["sng"] = pk(inp["ssm_norm_g"][0])
    c["lng"] = pk(inp["ln_a_g"][0])
    c["lnb"] = pk(inp["ln_a_b"][0])
    cw = np.asarray(inp["conv_w"][0], f)
    c["convw"] = np.ascontiguousarray(cw.reshape(4, 24, 128).transpose(2, 1, 0)).reshape(128, 96)
    c["convb"] = pk(inp["conv_b"][0])

    def bc(v):
        v = np.asarray(v, f).reshape(1, -1)
        return np.ascontiguousarray(np.broadcast_to(v, (128, v.shape[1])))
    c["fg_bc"] = bc(inp["final_g"])
    c["dtb_bc"] = bc(inp["dt_bias"][0])
    c["alog_bc"] = bc(inp["a_log"][0])
    c["dskip_bc"] = bc(inp["d_skip"][0])
    ws = np.asarray(inp["w_s"][0], f)
    c["wsT"] = np.ascontiguousarray(ws.transpose(2, 0, 1)).reshape(128, 512)
    c["bs_bc"] = bc(np.asarray(inp["b_s"][0], f).reshape(-1))
    return c


def make_in_maps(inp, n_cores, n_tiles):
    c = host_consts(inp)
    ntok = n_tiles * T
    base = {
        "w_in": np.ascontiguousarray(np.asarray(inp["w_in"][0], np.float32)),
        "w_oa": np.ascontiguousarray(np.asarray(inp["w_oa"][0], np.float32)),
        "w_ob": np.ascontiguousarray(np.asarray(inp["w_ob"][0], np.float32)),
        "w_out": np.ascontiguousarray(np.asarray(inp["w_out"][0], np.float32)),
        "w_pg": np.ascontiguousarray(np.asarray(inp["w_pg"][0], np.float32)),
        "w_ple": np.ascontiguousarray(np.asarray(inp["w_ple"][0], np.float32)),
    }
    for k, v in c.items():
        base["c_" + k] = v
    maps = []
    for b in range(n_cores):
        m = dict(base)
        m["x"] = np.ascontiguousarray(np.asarray(inp["x"][b, :ntok], np.float32))
        m["p"] = np.ascontiguousarray(np.asarray(inp["p"][0, b, :ntok], np.float32))
        maps.append(m)
    return maps


def kernel(**inputs):
    n_tiles = SEQ // T
    nc = build(n_tiles)
    maps = make_in_maps(inputs, NB, n_tiles)
    res = run_bass_kernel_spmd(nc, maps, core_ids=list(range(NB)))
    out = np.stack([np.asarray(r["out"], np.float32) for r in res.results], axis=0)
    return out
```

```python
import numpy as np
from contextlib import ExitStack
import concourse.bass as bass
import concourse.mybir as mybir
from concourse.bass_utils import run_bass_kernel_spmd

F32 = mybir.dt.float32
BF16 = mybir.dt.bfloat16
AF = mybir.ActivationFunctionType
ALU = mybir.AluOpType

D = 1024
SEQ = 8192
NB = 8
PLE = 256
T = 512
NCH = 4
EPS = 1e-6
NIN = 10272
C_U, C_V, C_ZA, C_ZB, C_XBC, C_DT, C_GA, C_GB = 0, 1024, 2048, 3072, 5120, 8192, 8224, 9248
MAXV = 30000
import os
STOPC = int(os.environ.get('STOPC', '0'))
USE_POW = int(os.environ.get('USE_POW', '0'))
NO_SELF = tuple(x for x in os.environ.get('NO_SELF', '').split(',') if x)


class _Stop(Exception):
    pass


class Buf:
    def __init__(self, name, raw=None, nbytes=0):
        self.name = name
        self.raw = raw
        self.nbytes = nbytes
        self.recs = {}
        self.ranges = {}

    def v(self, dt=BF16, pat=None, **kw):
        ap = self.raw if dt == BF16 else self.raw.bitcast(dt)
        if pat is not None:
            ap = ap.rearrange(pat, **kw)
        return ap


class Rec:
    __slots__ = ("writer", "readers")

    def __init__(self):
        self.writer = None
        self.readers = []


class Op:
    __slots__ = ("eng", "fn", "deps", "dma", "token", "needed", "seq")


def _acc(a):
    if not isinstance(a, tuple):
        return (a, None)
    if len(a) == 4:
        a[0].ranges[a[1]] = (a[2], a[3])
        return (a[0], a[1])
    return a


def _conf(buf, k1, k2):
    if k1 == k2 or k1 is None or k2 is None:
        return True
    r1 = buf.ranges.get(k1)
    r2 = buf.ranges.get(k2)
    if r1 is None or r2 is None:
        return False
    return r1[0] < r2[1] and r2[0] < r1[1]


class Prog:
    def __init__(self, nc, es):
        self.nc = nc
        self.es = es
        self.ops = {e: [] for e in ("pe", "act", "dve", "pool", "sp")}
        self.dma_sems = {}
        self.dma_counts = {}
        self.all_ops = []

    def _rec(self, buf, key):
        r = buf.recs.get(key)
        if r is None:
            r = buf.recs[key] = Rec()
        return r

    def add(self, eng, fn, reads=(), writes=(), dma=None, reg_reads=True):
        op = Op()
        op.eng = eng
        op.fn = fn
        op.dma = dma is not None
        op.needed = False
        op.token = None
        deps = []
        xr = [a for a in reads if getattr(_acc(a)[0], "excl", False)]
        if xr:
            reads = [a for a in reads if not getattr(_acc(a)[0], "excl", False)]
            writes = list(writes) + [_acc(a)[0] for a in xr]
        writes = [(_acc(a)[0] if getattr(_acc(a)[0], "excl", False) else a) for a in writes]
        for a in reads:
            buf, key = _acc(a)
            for k, r in buf.recs.items():
                if r.writer is not None and _conf(buf, key, k):
                    deps.append(r.writer)
        for a in writes:
            buf, key = _acc(a)
            for k, r in buf.recs.items():
                if _conf(buf, key, k):
                    if r.writer is not None:
                        deps.append(r.writer)
                    deps.extend(r.readers)
        for a in (reads if reg_reads else ()):
            buf, key = _acc(a)
            r = self._rec(buf, key)
            if not op.dma:
                r.readers = [x for x in r.readers if x.dma or x.eng != eng]
            r.readers.append(op)
        for a in writes:
            buf, key = _acc(a)
            if key is None:
                buf.recs = {}
            r = self._rec(buf, key)
            r.writer = op
            r.readers = []
        ud = []
        seen = set()
        for d in deps:
            if id(d) in seen or d is op:
                continue
            seen.add(id(d))
            if not (eng == "pe" and d.eng == "pe" and not d.dma and not op.dma):
                d.needed = True
            ud.append(d)
        op.deps = ud
        if op.dma:
            if dma not in self.dma_sems:
                self.dma_sems[dma] = self.es.enter_context(self.nc.semaphore("d_" + dma))
                self.dma_counts[dma] = 0
            self.dma_counts[dma] += 16
            op.token = (self.dma_sems[dma], self.dma_counts[dma])
        self.ops[eng].append(op)
        self.all_ops.append(op)
        return op

    def finalize(self):
        self.eng_sems = {}
        for eng, lst in self.ops.items():
            n = 0
            sems = []
            for op in lst:
                if op.dma or not op.needed:
                    continue
                ep = n // MAXV
                if ep >= len(sems):
                    sems.append(self.es.enter_context(self.nc.semaphore("e_%s%d" % (eng, ep))))
                op.token = (sems[ep], n % MAXV + 1)
                n += 1
            self.eng_sems[eng] = sems

    def emit(self, engname, e, final_waits=()):
        seen = {}
        for op in self.ops[engname]:
            need = {}
            for d in op.deps:
                if (not d.dma) and d.eng == engname and (engname == "pe" or engname in NO_SELF):
                    continue
                sem, val = d.token
                k = id(sem)
                if k not in need or need[k][1] < val:
                    need[k] = (sem, val)
            for k, (sem, val) in need.items():
                if seen.get(k, 0) >= val:
                    continue
                e.wait_ge(sem, val)
                seen[k] = val
            ins = op.fn(e)
            if op.dma:
                ins.then_inc(op.token[0], 16)
            elif op.needed:
                ins.then_inc(op.token[0], 1)
        for d in final_waits:
            sem, val = d.token
            if seen.get(id(sem), 0) >= val:
                continue
            e.wait_ge(sem, val)
            seen[id(sem)] = val


def weight_units():
    u = []
    for i in range(2):
        u.append(("u%d" % i, "w_in", 8, C_U + 512 * i, 512, "ng"))
    for i in range(2):
        u.append(("za%d" % i, "w_in", 8, C_ZA + 512 * i, 512, "ng"))
    for i in range(2):
        u.append(("v%d" % i, "w_in", 8, C_V + 512 * i, 512, "ng"))
    for i in range(2):
        u.append(("ga%d" % i, "w_in", 8, C_GA + 512 * i, 512, "ng"))
    for i in range(2):
        u.append(("oa%d" % i, "w_oa", 8, 512 * i, 512, None))
    u.append(("dt", "w_in", 8, C_DT, 32, "ng"))
    for g in range(4):
        u.append(("xs%d" % g, "w_in", 8, C_XBC + 512 * g, 512, "ng"))
    for g in range(4):
        u.append(("bm%d" % g, "w_in", 8, C_XBC + 2048 + 128 * g, 128, "ng"))
        u.append(("cm%d" % g, "w_in", 8, C_XBC + 2560 + 128 * g, 128, "ng"))
    for g in range(4):
        u.append(("zb%d" % g, "w_in", 8, C_ZB + 512 * g, 512, "ng"))
    for i in range(2):
        u.append(("gb%d" % i, "w_in", 8, C_GB + 512 * i, 512, "ng"))
    for i in range(4):
        u.append(("ob%d" % i, "w_ob", 16, 256 * i, 256, "sng"))
    for i in range(2):
        u.append(("out%d" % i, "w_out", 8, 512 * i, 512, None))
    for i in range(2):
        u.append(("pg%d" % i, "w_pg", 8, 512 * i, 512, "png"))
    for i in range(2):
        u.append(("ple%d" % i, "w_ple", 2, 512 * i, 512, None))
    return u


CONST_SPECS = [
    ("ident_b", [128], BF16), ("tri_b", [128], BF16), ("sut_b", [128], BF16),
    ("tri_f", [128], F32), ("ones_f", [128], F32),
    ("ng", [8], F32), ("png", [8], F32), ("sng", [16], F32), ("lng", [8], F32), ("lnb", [8], F32),
    ("convw", [96], F32), ("convb", [24], F32),
    ("fg_bc", [1024], F32), ("dtb_bc", [32], F32), ("alog_bc", [32], F32), ("dskip_bc", [32], F32),
    ("wsT", [512], F32), ("bs_bc", [512], F32),
]


def build(n_tiles, debug=None, stop_after=None):
    nc = bass.Bass("TRN2", target_bir_lowering=False)
    ntok = n_tiles * T
    dram = {}
    dram["x"] = nc.dram_tensor("x", [ntok, D], F32, kind="ExternalInput").ap()
    dram["p"] = nc.dram_tensor("p", [ntok, PLE], F32, kind="ExternalInput").ap()
    dram["w_in"] = nc.dram_tensor("w_in", [D, NIN], F32, kind="ExternalInput").ap()
    dram["w_oa"] = nc.dram_tensor("w_oa", [D, D], F32, kind="ExternalInput").ap()
    dram["w_ob"] = nc.dram_tensor("w_ob", [2 * D, D], F32, kind="ExternalInput").ap()
    dram["w_out"] = nc.dram_tensor("w_out", [D, D], F32, kind="ExternalInput").ap()
    dram["w_pg"] = nc.dram_tensor("w_pg", [D, D], F32, kind="ExternalInput").ap()
    dram["w_ple"] = nc.dram_tensor("w_ple", [PLE, D], F32, kind="ExternalInput").ap()
    for name, shp, dt in CONST_SPECS:
        dram[name] = nc.dram_tensor("c_" + name, [128] + shp, F32, kind="ExternalInput").ap()
    out_d = nc.dram_tensor("out", [ntok, D], F32, kind="ExternalOutput").ap()
    units = weight_units()
    uoff = {}
    off = 0
    for (nm, src, K, c0, ncol, rs) in units:
        uoff[nm] = off
        off += 128 * K * ncol
    wq = nc.dram_tensor("wq", [off], BF16, kind="Internal").ap()
    dbg_d = {}
    if debug is not None:
        for hook, lst in debug.items():
            for (nm, ncols, fn) in lst:
                dbg_d[nm] = nc.dram_tensor("dbg_" + nm, [128, ncols], F32, kind="ExternalOutput").ap()

    es = ExitStack()
    with es:
        P = Prog(nc, es)
        ARENA = 206 * 1024
        arena = es.enter_context(nc.sbuf_tensor("arena", [128, ARENA // 2], BF16))
        apos = [0]

        def alloc(name, nbytes):
            nbytes = (nbytes + 63) // 64 * 64
            o = apos[0]
            apos[0] += nbytes
            assert apos[0] <= ARENA, (name, apos[0])
            return Buf(name, arena[:, o // 2:(o + nbytes) // 2], nbytes)

        pm = []
        for i in range(6):
            t = es.enter_context(nc.psum_tensor("pm%d" % i, [128, 512], F32))
            b = Buf("pm%d" % i)
            b.ap = t
            b.excl = True
            pm.append(b)
        ptb = []
        for i in range(2):
            t = es.enter_context(nc.psum_tensor("pt%d" % i, [128, 1024], BF16))
            b = Buf("pt%d" % i)
            b.ap = t
            b.excl = True
            ptb.append(b)
        pmi = [0]
        pti = [0]

        spool = {"pm0": [0, [0, 1, 2]], "pm1": [0, [3, 4, 5]], "pt0": [0, [0]], "pt1": [0, [1]]}

        def nextpm(pool=None):
            if pool is not None:
                st = spool[pool]
                b = pm[st[1][st[0] % len(st[1])]]
                st[0] += 1
                return b
            b = pm[pmi[0] % 6]
            pmi[0] += 1
            return b

        def nextpt(pool=None):
            if pool is not None:
                st = spool[pool]
                b = ptb[st[1][st[0] % len(st[1])]]
                st[0] += 1
                return b
            b = ptb[pti[0] % 2]
            pti[0] += 1
            return b

        xa = [alloc("xa%d" % i, 4096) for i in range(2)]
        hT = [alloc("hT%d" % i, 8192) for i in range(2)]
        hb = [alloc("hb%d" % i, 2048) for i in range(2)]
        mg = alloc("mg", 8192)
        ybT = alloc("ybT", 16384)
        state = alloc("state", 8192)
        prevbf = alloc("prevbf", 4096)
        ring = [alloc("ring%d" % i, 8192) for i in range(4)]
        wdt = alloc("wdt", 512)
        dtr = alloc("dtr", 512)
        dtt = alloc("dtt", 512)
        dA = alloc("dA", 512)
        halo = alloc("halo", 24 * 3 * 2)
        junk = alloc("junk", 2048)
        small = [alloc("small%d" % i, 256) for i in range(4)]
        P1 = alloc("P1", 8192)
        P2 = alloc("P2", 8192)
        P3 = alloc("P3", 16384)
        t1 = [alloc("t1_%d" % i, 2048) for i in range(2)]
        gab = [alloc("gab%d" % i, 1024) for i in range(2)]
        raw = [alloc("raw%d" % i, 515 * 2) for i in range(3)]
        acc = [alloc("acc%d" % i, 2048) for i in range(2)]
        Xdt = [alloc("Xdt%d" % i, 1024) for i in range(2)]
        XD = [alloc("XD%d" % i, 1024) for i in range(2)]
        XdD = [alloc("XdD%d" % i, 1024) for i in range(2)]
        Btm = [alloc("Btm%d" % i, 256) for i in range(2)]
        S4 = [alloc("S4_%d" % i, 4096) for i in range(2)]
        Lb = [alloc("L%d" % i, 2048) for i in range(2)]
        Gb = [alloc("G%d" % i, 2048) for i in range(2)]
        CBm = [alloc("CBm%d" % i, 256) for i in range(2)]
        y1 = [alloc("y1_%d" % i, 2048) for i in range(2)]
        ybb = [alloc("yb%d" % i, 1024) for i in range(2)]
        tmpst = [alloc("tmp%d" % i, 2048) for i in range(2)]
        ssd_s = [alloc("ssd_s%d" % i, 1024) for i in range(2)]
        C = {}
        for name, shp, dt in CONST_SPECS:
            n = shp[0]
            if name == "wsT":
                C[name] = y1[0]
            elif name == "bs_bc":
                C[name] = y1[1]
            else:
                C[name] = alloc("c_" + name, n * (2 if dt == BF16 else 4))
        a_bc = alloc("a_bc", 32 * 4)
        wsTm_f = tmpst[0]
        wsTm_b = alloc("wsTm_b", 512 * 2)
        BR = alloc("BR", 1024 * 4)
        neghalf = alloc("neghalf", 64)
        print("arena used", apos[0])

        def k1(j): return (P1, j, j * 1024, (j + 1) * 1024)
        def k1hp(c): return (P1, ("hp", c), 0, 8192)
        def k2(j): return (P2, j, j * 1024, (j + 1) * 1024)
        def k2zb(s): return (P2, ("zb", s), s * 4096, (s + 1) * 4096)
        def k2gate(c, i): return (P2, ("gate", c, i), c * 2048 + i * 1024, c * 2048 + (i + 1) * 1024)
        def k3vg(s): return (P3, ("vg", s), s * 4096, (s + 1) * 4096)
        def k3vn(c): return (P3, ("vn", c), 8192 + c * 2048, 8192 + (c + 1) * 2048)
        def k3xc(s): return (P3, ("xc", s), s * 6144, (s + 1) * 6144)
        def k3x1(c): return (P3, ("x1", c), c * 4096, (c + 1) * 4096)
        def kL(q, hh): return (Lb[q], hh, hh * 1024, (hh + 1) * 1024)
        def kLpT(c): return (Lb[0], ("pT", c), 0, 2048)
        def kS4(q, ab): return (S4[q], ab, 0, 2048) if ab == "a" else (S4[q], ab, 2048, 4096)

        ident_b = C["ident_b"].v(BF16)
        tri_b = C["tri_b"].v(BF16)
        sut_b = C["sut_b"].v(BF16)
        tri_f = C["tri_f"].v(F32)
        ones_f = C["ones_f"].v(F32)

        ssem_cnt = [0]
        store_ops = []

        def act(out, in_, func, reads, writes, bias=None, scale=None, accum=None):
            kw = {}
            if bias is not None:
                kw["bias"] = bias
            if scale is not None:
                kw["scale"] = scale
            if accum is not None:
                kw["accum_out"] = accum
            return P.add("act", lambda e: e.activation(out=out, in_=in_, func=func, **kw), reads, writes)

        def tt(eng, out, in0, in1, op, reads, writes):
            return P.add(eng, lambda e: e.tensor_tensor(out=out, in0=in0, in1=in1, op=op), reads, writes)

        def ts(eng, out, in0, s1, s2, op0, op1, reads, writes):
            if op1 is None:
                return P.add(eng, lambda e: e.tensor_scalar(out=out, in0=in0, scalar1=s1, scalar2=None, op0=op0), reads, writes)
            return P.add(eng, lambda e: e.tensor_scalar(out=out, in0=in0, scalar1=s1, scalar2=s2, op0=op0, op1=op1), reads, writes)

        def stt(out, in0, scalar, in1, op0, op1, reads, writes):
            return P.add("dve", lambda e: e.scalar_tensor_tensor(out=out, in0=in0, scalar=scalar, in1=in1, op0=op0, op1=op1), reads, writes)

        def cp(eng, out, in_, reads, writes):
            if eng == "act":
                return P.add("act", lambda e: e.copy(out=out, in_=in_), reads, writes)
            return P.add(eng, lambda e: e.tensor_copy(out=out, in_=in_), reads, writes)

        mm_pend = []

        def mm(out, lhsT, rhs, start, stop, reads, writes):
            if not stop:
                mm_pend.extend(reads)
                return P.add("pe", lambda e: e.matmul(out, lhsT, rhs, start=start, stop=stop), reads, writes, reg_reads=False)
            allr = list(reads) + list(mm_pend)
            del mm_pend[:]
            return P.add("pe", lambda e: e.matmul(out, lhsT, rhs, start=start, stop=stop), allr, writes)

        def tp(out, in_, reads, writes, ident=None):
            idn = ident_b if ident is None else ident
            return P.add("pe", lambda e: e.transpose(out, in_, idn), list(reads) + [C["ident_b"]], writes)

        def dma(q, out, in_, reads, writes, sem):
            return P.add(q, lambda e: e.dma_start(out=out, in_=in_), reads, writes, dma=sem)

        def rsqrt_chain(dst_ap, src_ap, mulc, srcbuf, dstbuf):
            ts("dve", dst_ap, src_ap, mulc, EPS, ALU.mult, ALU.add, [srcbuf], [dstbuf])
            act(dst_ap, dst_ap, AF.Ln, [dstbuf], [dstbuf])
            act(dst_ap, dst_ap, AF.Exp, [dstbuf], [dstbuf], scale=-0.5)

        for name, shp, dt in CONST_SPECS:
            b = C[name]
            dma("pool", b.v(dt)[:, :shp[0]], dram[name], [], [b], "const")
        P.add("pool", lambda e: e.memset(junk.v(BF16)[:, 0:8], 0.0), list(C.values()), list(C.values()))
        act(a_bc.v(F32), C["alog_bc"].v(F32), AF.Exp, [C["alog_bc"]], [a_bc])
        ts("dve", a_bc.v(F32), a_bc.v(F32), -1.0, None, ALU.mult, None, [a_bc], [a_bc])
        wsT3 = C["wsT"].v(F32, "p (g t) -> p g t", g=4)
        tt("dve", wsTm_f.v(F32, "p (g t) -> p g t", g=4), wsT3,
           tri_f.unsqueeze(1).to_broadcast([128, 4, 128]), ALU.mult, [C["wsT"], C["tri_f"]], [wsTm_f])
        cp("dve", wsTm_b.v(BF16), wsTm_f.v(F32), [wsTm_f], [wsTm_b])
        bk = nextpm()
        mm(bk.ap[:, :], ones_f, wsTm_f.v(F32), True, True, [C["ones_f"], wsTm_f], [bk])
        BR3 = BR.v(F32, "p (j t) -> p j t", j=8)
        bs3 = C["bs_bc"].v(F32, "p (g t) -> p g t", g=4)
        lnb = C["lnb"].v(F32)
        lng = C["lng"].v(F32)
        for j in range(8):
            g = j // 2
            stt(BR3[:, j, :], bk.ap[:, g * 128:(g + 1) * 128], lnb[:, j:j + 1], bs3[:, g, :], ALU.mult, ALU.add,
                [bk, C["lnb"], C["bs_bc"]], [(BR, j)])
        P.add("pool", lambda e: e.memset(neghalf.v(F32), -0.5), [], [neghalf])
        P.add("pool", lambda e: e.memset(state.v(F32), 0.0), [], [state])
        P.add("pool", lambda e: e.memset(prevbf.v(BF16), 0.0), [], [prevbf])
        P.add("pool", lambda e: e.memset(halo.v(BF16), 0.0), [], [halo])

        wq_bufs = {nm: Buf("wq_" + nm) for (nm, *_r) in units}
        stg_in = [t1[0], t1[1], acc[0], acc[1], Lb[0], Lb[1], Gb[0], Gb[1]]
        stg_out = [gab[0], gab[1], Xdt[0], Xdt[1], XD[0], XD[1], XdD[0], XdD[1]]
        NSTG = 8
        cnt = 0
        for (nm, src, K, c0, ncol, rs) in units:
            dst = wq[uoff[nm]:uoff[nm] + 128 * K * ncol].rearrange("(p k c) -> p k c", p=128, k=K)
            for k in range(K):
                si = stg_in[cnt % NSTG]
                so = stg_out[cnt % NSTG]
                dma("sp", si.v(F32)[:, :ncol], dram[src][k * 128:(k + 1) * 128, c0:c0 + ncol], [], [si], "stgi%d" % (cnt % NSTG))
                eng = "act" if cnt % 2 == 0 else "dve"
                if rs is None:
                    cp(eng, so.v(BF16)[:, :ncol], si.v(F32)[:, :ncol], [si], [so])
                else:
                    sc = C[rs].v(F32)[:, k:k + 1]
                    if eng == "act":
                        act(so.v(BF16)[:, :ncol], si.v(F32)[:, :ncol], AF.Identity, [si, C[rs]], [so], scale=sc)
                    else:
                        ts("dve", so.v(BF16)[:, :ncol], si.v(F32)[:, :ncol], sc, None, ALU.mult, None, [si, C[rs]], [so])
                dma("pool", dst[:, k, :], so.v(BF16)[:, :ncol], [so], [(wq_bufs[nm], k)], "stgo%d" % (cnt % NSTG))
                cnt += 1

        ring_i = [0]

        def load_unit(nm):
            (_, src, K, c0, ncol, rs) = [u for u in units if u[0] == nm][0]
            if nm == "dt":
                slot = wdt
                semn = "wdt"
            else:
                slot = ring[ring_i[0] % 4]
                semn = "ring%d" % (ring_i[0] % 4)
                ring_i[0] += 1
            srcap = wq[uoff[nm]:uoff[nm] + 128 * K * ncol].rearrange("(p k c) -> p k c", p=128, k=K)
            view = slot.v(BF16)[:, :K * ncol].rearrange("p (k c) -> p k c", k=K)
            dma("sp", view, srcap, [wq_bufs[nm]], [slot], semn)
            return slot, view

        def zipper(gens):
            gens = list(gens)
            while gens:
                for gen in list(gens):
                    try:
                        next(gen)
                    except StopIteration:
                        gens.remove(gen)

        def stage_a(ti):
            for _ in stage_a_gen(ti):
                pass

        def stage_a_gen(ti):
            sl = ti % 2
            hT3 = hT[sl].v(BF16, "p (k t) -> p k t", k=8)
            for c in range(NCH):
                xb = xa[c % 2]
                r0 = ti * T + c * 128
                dma("sp", xb.v(F32), dram["x"][r0:r0 + 128, :], [], [xb], "xa%d" % (c % 2))
                ss = small[0]
                hbb = hb[c % 2]
                a0_, a1_ = ss.v(F32)[:, 2 * c:2 * c + 1], ss.v(F32)[:, 2 * c + 1:2 * c + 2]
                act(hbb.v(BF16), xb.v(F32), AF.Square, [xb], [(ss, ("s", c)), hbb], accum=a0_)
                rsqrt_chain(a1_, a0_, 1.0 / D, (ss, ("s", c)), (ss, ("r", c)))
                yield
                act(hbb.v(BF16), xb.v(F32), AF.Identity, [xb, (ss, ("r", c))], [hbb], scale=a1_)
                pb_ = nextpt()
                for k in range(8):
                    tp(pb_.ap[:, k * 128:(k + 1) * 128], hbb.v(BF16)[:, k * 128:(k + 1) * 128], [hbb], [(pb_, k)])
                cp("dve", hT3[:, :, c * 128:(c + 1) * 128], pb_.ap.rearrange("p (k t) -> p k t", k=8), [pb_], [(hT[sl], c)])
                yield

        def fm_proj(view, hT3, jj, hTbuf, slot):
            bk = nextpm()
            for k in range(8):
                mm(bk.ap[:, :], view[:, k, jj * 128:(jj + 1) * 128], hT3[:, k, :], k == 0, k == 7, [slot, hTbuf], [bk])
            return bk

        def a_branch(ti):
            sl = ti % 2
            hTb = hT[sl]
            hT3 = hTb.v(BF16, "p (k t) -> p k t", k=8)
            ug3 = P1.v(BF16, "p (k t) -> p k t", k=8)
            zs3 = P2.v(BF16, "p (k t) -> p k t", k=8)
            sv = [load_unit("v0"), load_unit("v1")]
            vg = [Buf("vg0", P3.raw[:, 0:2048], 4096), Buf("vg1", P3.raw[:, 2048:4096], 4096)]
            vn = Buf("vn", P3.raw[:, 4096:8192], 8192)
            vn3 = vn.v(BF16, "p (c d) -> p c d", c=4)
            st = small[1]
            stv = st.v(F32)
            for cp_ in range(2):
                for s in range(2):
                    c = cp_ * 2 + s
                    vgv = vg[s].v(F32)
                    for i in range(2):
                        slot, view = sv[i]
                        bk = nextpm()
                        for k in range(8):
                            mm(bk.ap[:, :], hT3[:, k, c * 128:(c + 1) * 128], view[:, k, :], k == 0, k == 7, [slot, hTb], [bk])
                        act(vgv[:, i * 512:(i + 1) * 512], bk.ap[:, :], AF.Gelu_apprx_tanh, [bk], [k3vg(s)])
                        P.add("dve", lambda e, i=i, s=s, vgv=vgv: e.bn_stats(out=stv[:, s * 12 + i * 6:s * 12 + (i + 1) * 6], in_=vgv[:, i * 512:(i + 1) * 512]),
                              [k3vg(s)], [(st, ("bs", s, i))])
                    P.add("dve", lambda e, s=s: e.bn_aggr(out=stv[:, 32 + 2 * s:34 + 2 * s], in_=stv[:, s * 12:(s + 1) * 12]),
                          [(st, ("bs", s, 0)), (st, ("bs", s, 1))], [(st, ("ag", s))])
                    ts("dve", stv[:, 40 + s:41 + s], stv[:, 33 + 2 * s:34 + 2 * s], 1.0, EPS, ALU.mult, ALU.add, [(st, ("ag", s))], [(st, "rs")])
                act(stv[:, 40:42], stv[:, 40:42], AF.Ln, [(st, "rs")], [(st, "rs")])
                act(stv[:, 40:42], stv[:, 40:42], AF.Exp, [(st, "rs")], [(st, "rs")], scale=-0.5)
                for s in range(2):
                    c = cp_ * 2 + s
                    ts("dve", vn3[:, c, :], vg[s].v(F32), stv[:, 32 + 2 * s:33 + 2 * s], stv[:, 40 + s:41 + s], ALU.subtract, ALU.mult,
                       [k3vg(s), (st, ("ag", s)), (st, "rs")], [k3vn(c)])
            for i in range(2):
                slot, view = load_unit("u%d" % i)
                for jj in range(4):
                    j = i * 4 + jj
                    bk = fm_proj(view, hT3, jj, hTb, slot)
                    act(ug3[:, j, :], bk.ap[:, :], AF.Gelu_apprx_tanh, [bk], [k1(j)])
            for i in range(2):
                slot, view = load_unit("za%d" % i)
                for jj in range(4):
                    j = i * 4 + jj
                    bk = fm_proj(view, hT3, jj, hTb, slot)
                    act(zs3[:, j, :], bk.ap[:, :], AF.Silu, [bk], [k2(j)])
                    tt("dve", ug3[:, j, :], ug3[:, j, :], zs3[:, j, :], ALU.mult, [k1(j), k2(j)], [k1(j)])
            wsb3 = wsTm_b.v(BF16, "p (g t) -> p g t", g=4)
            for j in range(8):
                g = j // 2
                bk = nextpm()
                for c in range(NCH):
                    mm(bk.ap[:, c * 128:(c + 1) * 128], vn3[:, c, j * 128:(j + 1) * 128], wsb3[:, g, :], True, True,
                       [k3vn(c), wsTm_b], [bk])
                tb = t1[j % 2]
                stt(tb.v(F32, "p (c t) -> p c t", c=4), bk.ap.rearrange("p (c t) -> p c t", c=4), lng[:, j:j + 1],
                    BR3[:, j, :].unsqueeze(1).to_broadcast([128, 4, 128]), ALU.mult, ALU.add,
                    [bk, C["lng"], (BR, j)], [tb])
                tt("dve", ug3[:, j, :], tb.v(F32), ug3[:, j, :], ALU.mult, [tb, k1(j)], [k1(j)])
            mg3 = mg.v(BF16, "p (k t) -> p k t", k=8)
            sga = [load_unit("ga0"), load_unit("ga1")]
            soa = [load_unit("oa0"), load_unit("oa1")]
            for j in range(8):
                i, jj = j // 4, j % 4
                bk = fm_proj(sga[i][1], hT3, jj, hTb, sga[i][0])
                gt = gab[j % 2]
                act(gt.v(BF16), bk.ap[:, :], AF.Sigmoid, [bk], [gt])
                bk2 = nextpm()
                for k in range(8):
                    mm(bk2.ap[:, :], soa[i][1][:, k, jj * 128:(jj + 1) * 128], ug3[:, k, :], k == 0, k == 7, [soa[i][0], P1], [bk2])
                tt("dve", mg3[:, j, :], bk2.ap[:, :], gt.v(BF16), ALU.mult, [bk2, gt], [(mg, j)])

        def b_branch(ti):
            sl = ti % 2
            hTb = hT[sl]
            hT3 = hTb.v(BF16, "p (k t) -> p k t", k=8)
            slot, view = load_unit("dt")
            bk = nextpm()
            for c in range(NCH):
                for k in range(8):
                    mm(bk.ap[:, c * 32:(c + 1) * 32], hT3[:, k, c * 128:(c + 1) * 128], view[:, k, :], k == 0, k == 7, [slot, hTb], [bk])
            dtr3 = dtr.v(F32, "p (c h) -> p c h", c=4)
            tt("dve", dtr3, bk.ap[:, 0:128].rearrange("p (c h) -> p c h", c=4),
               C["dtb_bc"].v(F32).unsqueeze(1).to_broadcast([128, 4, 32]), ALU.add, [bk, C["dtb_bc"]], [dtr])
            stt(dtt.v(F32), dtr.v(F32), -1.0, dtr.v(F32), ALU.mult, ALU.max, [dtr], [dtt])
            act(dtt.v(F32), dtt.v(F32), AF.Exp, [dtt], [dtt], scale=-1.0)
            act(dtt.v(F32), dtt.v(F32), AF.Ln, [dtt], [dtt], bias=1.0)
            stt(dtt.v(F32), dtr.v(F32), 0.0, dtt.v(F32), ALU.max, ALU.add, [dtr, dtt], [dtt])
            tt("dve", dA.v(F32, "p (c h) -> p c h", c=4), dtt.v(F32, "p (c h) -> p c h", c=4),
               a_bc.v(F32).unsqueeze(1).to_broadcast([128, 4, 32]), ALU.mult, [dtt, a_bc], [dA])
            dt3 = dtt.v(F32, "p (c h) -> p c h", c=4)
            dA3 = dA.v(F32, "p (c h) -> p c h", c=4)
            dAh_all = dtr.v(BF16)[:, 0:128]
            dAl_all = dtr.v(BF16)[:, 128:256]
            cp("dve", dAh_all, dA.v(F32), [dA, dtr], [(dtr, "hi")])
            tt("dve", dAl_all, dA.v(F32), dAh_all, ALU.subtract, [dA, (dtr, "hi")], [(dtr, "lo")])
            dAh3 = dAh_all.rearrange("p (c h) -> p c h", c=4)
            dAl3 = dAl_all.rearrange("p (c h) -> p c h", c=4)
            if stop_after == "b_dt":
                raise _Stop()
            halo3 = halo.v(BF16)[:, :72].rearrange("p (j t) -> p j t", j=24)
            cw = C["convw"].v(F32, "p (j k) -> p j k", j=24)
            cb = C["convb"].v(F32)
            ybT3 = ybT.v(BF16, "p (k t) -> p k t", k=16)
            st4 = state.v(F32, "p (g f) -> p g f", g=4)
            pv4 = prevbf.v(BF16, "p (g f) -> p g f", g=4)
            groupctx = {}

            def proj_group(g):
                xcb = Buf("xc", P3.raw[:, (g % 2) * 3072:(g % 2) * 3072 + 3072], 6144)
                xck = ("xc", g % 2)
                xc3 = xcb.v(BF16, "p (j t) -> p j t", j=6)
                wu = [load_unit("xs%d" % g)]
                s_bm = load_unit("bm%d" % g)
                s_cm = load_unit("cm%d" % g)
                tiles = [(wu[0], jj, g * 4 + jj) for jj in range(4)] + [(s_bm, 0, 16 + g), (s_cm, 0, 20 + g)]
                for li, ((slot, view), jj, jglob) in enumerate(tiles):
                    bk = fm_proj(view, hT3, jj, hTb, slot)
                    rw = raw[(g * 6 + li) % 3]
                    rv = rw.v(BF16)
                    cp("act", rv[:, 3:515], bk.ap[:, :], [bk], [(rw, "b")])
                    cp("pool", rv[:, 0:3], halo3[:, jglob, :], [(halo, jglob)], [(rw, "h")])
                    ab = acc[(g * 6 + li) % 2]
                    av = ab.v(F32)
                    act(av, bk.ap[:, :], AF.Identity, [bk, C["convw"], C["convb"]], [ab],
                        scale=cw[:, jglob, 3:4], bias=cb[:, jglob:jglob + 1])
                    cp("pool", halo3[:, jglob, :], rv[:, 512:515], [(rw, "b")], [(halo, jglob)])
                    for kk in (2, 1, 0):
                        stt(av, rv[:, kk:kk + 512], cw[:, jglob, kk:kk + 1], av, ALU.mult, ALU.add, [rw, ab, C["convw"]], [ab])
                    act(xc3[:, li, :], av, AF.Silu, [ab], [k3xc(g % 2)])
                if stop_after == "b_conv":
                    raise _Stop()
                zbb = Buf("zb", P2.raw[:, (g % 2) * 2048:(g % 2) * 2048 + 2048], 4096)
                zbk = ("zb", g % 2)
                zb3 = zbb.v(BF16, "p (c f) -> p c f", c=4)
                slot, view = load_unit("zb%d" % g)
                for c in range(NCH):
                    bk = nextpm()
                    for k in range(8):
                        mm(bk.ap[:, :], hT3[:, k, c * 128:(c + 1) * 128], view[:, k, :], k == 0, k == 7, [slot, hTb], [bk])
                    act(zb3[:, c, :], bk.ap[:, :], AF.Silu, [bk], [k2zb(g % 2)])
                if stop_after == "b_zb":
                    raise _Stop()
                groupctx[g] = (xc3, zb3)

            def ssd_iter(g, c, q):
                xc3, zb3 = groupctx[g]
                xk = k3xc(g % 2)
                zk = k2zb(g % 2)
                hs = slice(g * 8, g * 8 + 8)
                ssb = ssd_s[q]
                sv_ = ssb.v(F32)
                cs_ap, ecs_ap, dec_ap, cd_ap = sv_[:, 0:8], sv_[:, 8:16], sv_[:, 16:24], sv_[:, 24:32]
                dAh = dAh3[:, c, hs]
                dAl = dAl3[:, c, hs]
                csl = slice(c * 128, (c + 1) * 128)
                bk = nextpm("pm%d" % q)
                mm(bk.ap[:, 0:8], tri_f, dA3[:, c, hs], True, True, [C["tri_f"], dA], [(bk, 0)])
                mm(bk.ap[:, 8:16], ones_f, dA3[:, c, hs], True, True, [C["ones_f"], dA], [(bk, 1)])
                pb_ = nextpt("pt%d" % q)
                for jj in range(4):
                    tp(pb_.ap[:, jj * 128:(jj + 1) * 128], xc3[:, jj, csl], [xk], [(pb_, jj)])
                tp(pb_.ap[:, 512:640], xc3[:, 4, csl], [xk], [(pb_, 4)])
                yield
                rhi = S4[q].v(BF16)[:, 0:1024].rearrange("p (h l) -> p h l", h=8)
                rlo = S4[q].v(BF16)[:, 1024:2048].rearrange("p (h l) -> p h l", h=8)
                trib = tri_b.unsqueeze(1).to_broadcast([128, 8, 128])
                tt("dve", rhi, trib, dAh.unsqueeze(2).to_broadcast([128, 8, 128]), ALU.mult, [C["tri_b"], (dtr, "hi")], [kS4(q, "a")])
                tt("dve", rlo, trib, dAl.unsqueeze(2).to_broadcast([128, 8, 128]), ALU.mult, [C["tri_b"], (dtr, "lo")], [kS4(q, "b")])
                cp("dve", cs_ap, bk.ap[:, 0:8], [bk], [(ssb, "cs")])
                tt("dve", dec_ap, bk.ap[:, 8:16], cs_ap, ALU.subtract, [bk, (ssb, "cs")], [(ssb, "dec")])
                act(ecs_ap, bk.ap[:, 0:8], AF.Exp, [bk], [(ssb, "ecs")])
                act(cd_ap, bk.ap[:, 8:16], AF.Exp, [bk], [(ssb, "cd")])
                act(dec_ap, dec_ap, AF.Exp, [(ssb, "dec")], [(ssb, "dec")])
                yield
                xv = pb_.ap[:, 0:512].rearrange("p (h d) -> p h d", h=8)
                tt("dve", Xdt[q].v(BF16, "p (h d) -> p h d", h=8), xv,
                   dt3[:, c, hs].unsqueeze(2).to_broadcast([128, 8, 64]), ALU.mult, [pb_, dtt], [Xdt[q]])
                tt("dve", XD[q].v(BF16, "p (h d) -> p h d", h=8), xv,
                   C["dskip_bc"].v(F32)[:, hs].unsqueeze(2).to_broadcast([128, 8, 64]), ALU.mult, [pb_, C["dskip_bc"]], [XD[q]])
                cp("dve", Btm[q].v(BF16), pb_.ap[:, 512:640], [pb_], [Btm[q]])
                bk3 = nextpm("pm%d" % q)
                mm(bk3.ap[:, 0:128], xc3[:, 4, csl], xc3[:, 5, csl], True, True, [xk], [bk3])
                yield
                L3 = Lb[q].v(BF16, "p (h l) -> p h l", h=8)
                for hh in range(2):
                    bk2 = nextpm("pm%d" % q)
                    mm(bk2.ap[:, :], sut_b, S4[q].v(BF16)[:, hh * 512:(hh + 1) * 512], True, False, [C["sut_b"], kS4(q, "a")], [bk2])
                    mm(bk2.ap[:, :], sut_b, S4[q].v(BF16)[:, 1024 + hh * 512:1024 + (hh + 1) * 512], False, True, [C["sut_b"], kS4(q, "b")], [bk2])
                    act(Lb[q].v(BF16)[:, hh * 512:(hh + 1) * 512], bk2.ap[:, :], AF.Exp, [bk2], [kL(q, hh)])
                tt("dve", CBm[q].v(BF16), bk3.ap[:, 0:128], tri_b, ALU.mult, [bk3, C["tri_b"]], [CBm[q]])
                yield
                G3 = Gb[q].v(BF16, "p (h l) -> p h l", h=8)
                tt("dve", G3, L3, CBm[q].v(BF16).unsqueeze(1).to_broadcast([128, 8, 128]), ALU.mult, [Lb[q], CBm[q]], [Gb[q]])
                tt("dve", XdD[q].v(BF16, "p (h d) -> p h d", h=8), Xdt[q].v(BF16, "p (h d) -> p h d", h=8),
                   dec_ap.unsqueeze(2).to_broadcast([128, 8, 64]), ALU.mult, [Xdt[q], (ssb, "dec")], [XdD[q]])
                yield
                bko = nextpm("pm%d" % q)
                mm(bko.ap[:, :], xc3[:, 5, csl], pv4[:, g, :], True, True, [xk, (prevbf, g)], [bko])
                bkd = nextpm("pm%d" % q)
                mm(bkd.ap[:, :], ident_b, XD[q].v(BF16), True, False, [C["ident_b"], XD[q]], [bkd])
                for r in range(8):
                    mm(bkd.ap[:, r * 64:(r + 1) * 64], G3[:, r, :], Xdt[q].v(BF16)[:, r * 64:(r + 1) * 64], False, r == 7, [Gb[q], Xdt[q]], [bkd])
                bks = nextpm("pm%d" % q)
                mm(bks.ap[:, :], Btm[q].v(BF16), XdD[q].v(BF16), True, True, [Btm[q], XdD[q]], [bks])
                tt("pool", tmpst[q].v(F32, "p (h d) -> p h d", h=8), st4[:, g, :].rearrange("p (h d) -> p h d", h=8),
                   cd_ap.unsqueeze(2).to_broadcast([128, 8, 64]), ALU.mult, [(state, g), (ssb, "cd")], [tmpst[q]])
                yield
                yv = y1[q].v(F32)
                tt("dve", y1[q].v(F32, "p (h d) -> p h d", h=8), bko.ap.rearrange("p (h d) -> p h d", h=8),
                   ecs_ap.unsqueeze(2).to_broadcast([128, 8, 64]), ALU.mult, [bko, (ssb, "ecs")], [y1[q]])
                tt("dve", yv, yv, bkd.ap[:, :], ALU.add, [bkd, y1[q]], [y1[q]])
                tt("dve", st4[:, g, :], tmpst[q].v(F32), bks.ap[:, :], ALU.add, [bks, tmpst[q]], [(state, g)])
                cp("act", pv4[:, g, :], st4[:, g, :], [(state, g)], [(prevbf, g)])
                yield
                tt("pool", yv, yv, zb3[:, c, :], ALU.mult, [y1[q], zk], [y1[q]])
                act(ybb[q].v(BF16), yv, AF.Square, [y1[q]], [(ssb, "ss"), ybb[q]], accum=sv_[:, 50:51])
                yield
                rsqrt_chain(sv_[:, 51:52], sv_[:, 50:51], 1.0 / 512, (ssb, "ss"), (ssb, "rs"))
                yield
                act(ybb[q].v(BF16), yv, AF.Identity, [y1[q], (ssb, "rs")], [ybb[q]], scale=sv_[:, 51:52])
                pb2 = nextpt("pt%d" % q)
                for jj in range(4):
                    tp(pb2.ap[:, jj * 128:(jj + 1) * 128], ybb[q].v(BF16)[:, jj * 128:(jj + 1) * 128], [ybb[q]], [(pb2, jj)])
                cp("act", ybT3[:, g * 4:(g + 1) * 4, csl], pb2.ap[:, 0:512].rearrange("p (k t) -> p k t", k=4), [pb2], [(ybT, (g, c))])
                yield

            def stream(g, q):
                for c in range(NCH):
                    yield from ssd_iter(g, c, q)

            def zipper(gens):
                gens = list(gens)
                while gens:
                    for gen in list(gens):
                        try:
                            next(gen)
                        except StopIteration:
                            gens.remove(gen)

            for pair in range(2):
                mark("b_proj%d" % pair, ti)
                proj_group(2 * pair)
                proj_group(2 * pair + 1)
                mark("b_ssd%d" % pair, ti)
                zipper([stream(2 * pair, 0), stream(2 * pair + 1, 1)])

        def ob_merge(ti):
            gens = [ob_merge_gen(ti)]
            if ti + 1 < n_tiles:
                gens.append(stage_a_gen(ti + 1))
            zipper(gens)

        def ob_merge_gen(ti):
            sl = ti % 2
            hTb = hT[sl]
            hT3 = hTb.v(BF16, "p (k t) -> p k t", k=8)
            mg3 = mg.v(BF16, "p (k t) -> p k t", k=8)
            ybT3 = ybT.v(BF16, "p (k t) -> p k t", k=16)
            sgb = [None, None]
            for j in range(8):
                i, jj = j // 4, j % 4
                if jj == 0:
                    sgb[i] = load_unit("gb%d" % i)
                if j % 2 == 0:
                    sob = load_unit("ob%d" % (j // 2))
                    obv = sob[0].v(BF16)[:, :16 * 256].rearrange("p (k c) -> p k c", k=16)
                bk = fm_proj(sgb[i][1], hT3, jj, hTb, sgb[i][0])
                gt = gab[j % 2]
                act(gt.v(BF16), bk.ap[:, :], AF.Sigmoid, [bk], [gt])
                bk2 = nextpm()
                for k in range(16):
                    mm(bk2.ap[:, :], obv[:, k, (j % 2) * 128:(j % 2 + 1) * 128], ybT3[:, k, :], k == 0, k == 15, [sob[0], ybT], [bk2])
                tb = t1[j % 2]
                tt("dve", tb.v(F32), bk2.ap[:, :], gt.v(BF16), ALU.mult, [bk2, gt], [tb])
                tt("pool", mg3[:, j, :], mg3[:, j, :], tb.v(F32), ALU.add, [(mg, j), tb], [(mg, j)])
                yield

        def tail(ti):
            mg3 = mg.v(BF16, "p (k t) -> p k t", k=8)
            x1 = P3.v(F32, "p (c d) -> p c d", c=4)
            hpT3 = P1.v(BF16, "p (k t) -> p k t", k=8)
            gate3 = P2.v(BF16, "p (c d) -> p c d", c=4)
            pT3 = Lb[0].v(BF16)[:, :1024].rearrange("p (k t) -> p k t", k=2)
            for c in range(NCH):
                r0 = ti * T + c * 128
                dma("sp", x1[:, c, :], dram["x"][r0:r0 + 128, :], [], [k3x1(c)], "x1_%d" % c)
            so_ = [load_unit("out0"), load_unit("out1")]
            ss = small[2]

            def chain_pre(c):
                hpb = acc[c % 2]
                s0_, s1_ = ss.v(F32)[:, 2 * c:2 * c + 1], ss.v(F32)[:, 2 * c + 1:2 * c + 2]
                act(hpb.v(BF16)[:, :1024], x1[:, c, :], AF.Square, [k3x1(c)], [(ss, ("s", c)), hpb], accum=s0_)
                rsqrt_chain(s1_, s0_, 1.0 / D, (ss, ("s", c)), (ss, ("r", c)))
                act(hpb.v(BF16)[:, :1024], x1[:, c, :], AF.Identity, [k3x1(c), (ss, ("r", c))], [hpb], scale=s1_)

            def chain_tp(c):
                hpb = acc[c % 2]
                pb_ = nextpt()
                for k in range(8):
                    tp(pb_.ap[:, k * 128:(k + 1) * 128], hpb.v(BF16)[:, k * 128:(k + 1) * 128], [hpb], [(pb_, k)])
                cp("dve", hpT3[:, :, c * 128:(c + 1) * 128], pb_.ap.rearrange("p (k t) -> p k t", k=8), [pb_], [k1hp(c)])

            for c in range(NCH):
                for i in range(2):
                    slot, view = so_[i]
                    bk = nextpm()
                    for k in range(8):
                        mm(bk.ap[:, :], mg3[:, k, c * 128:(c + 1) * 128], view[:, k, :], k == 0, k == 7, [slot, mg], [bk])
                    tt("dve", x1[:, c, i * 512:(i + 1) * 512], x1[:, c, i * 512:(i + 1) * 512], bk.ap[:, :], ALU.add,
                       [bk, k3x1(c)], [k3x1(c)])
                chain_pre(c)
                if c >= 1:
                    chain_tp(c - 1)
            chain_tp(NCH - 1)
            for c in range(NCH):
                r0 = ti * T + c * 128
                ptb_ = Xdt[c % 2]
                pbb = XD[c % 2]
                dma("sp", ptb_.v(F32), dram["p"][r0:r0 + 128, :], [], [ptb_], "pt%d" % (c % 2))
                cp("dve", pbb.v(BF16)[:, :256], ptb_.v(F32), [ptb_], [pbb])
                pb3 = nextpt()
                for k in range(2):
                    tp(pb3.ap[:, k * 128:(k + 1) * 128], pbb.v(BF16)[:, k * 128:(k + 1) * 128], [pbb], [(pb3, k)])
                cp("act", pT3[:, :, c * 128:(c + 1) * 128], pb3.ap[:, 0:256].rearrange("p (k t) -> p k t", k=2), [pb3], [kLpT(c)])
            for i in range(2):
                slot, view = load_unit("pg%d" % i)
                for c in range(NCH):
                    bk = nextpm()
                    for k in range(8):
                        mm(bk.ap[:, :], hpT3[:, k, c * 128:(c + 1) * 128], view[:, k, :], k == 0, k == 7, [slot, P1], [bk])
                    act(gate3[:, c, i * 512:(i + 1) * 512], bk.ap[:, :], AF.Sigmoid, [bk], [k2gate(c, i)])
            for i in range(2):
                slot, view0 = load_unit("ple%d" % i)
                view = slot.v(BF16)[:, :1024].rearrange("p (k c) -> p k c", k=2)
                for c in range(NCH):
                    bk = nextpm()
                    for k in range(2):
                        mm(bk.ap[:, :], pT3[:, k, c * 128:(c + 1) * 128], view[:, k, :], k == 0, k == 1, [slot, Lb[0]], [bk])
                    tb = t1[(i * 4 + c) % 2]
                    tt("dve", tb.v(F32), bk.ap[:, :], gate3[:, c, i * 512:(i + 1) * 512], ALU.mult, [bk, k2gate(c, i)], [tb])
                    tt("pool", x1[:, c, i * 512:(i + 1) * 512], x1[:, c, i * 512:(i + 1) * 512], tb.v(F32), ALU.add,
                       [k3x1(c), tb], [k3x1(c)])
            for c in range(NCH):
                ss = small[3]
                f0_, f1_ = ss.v(F32)[:, 2 * c:2 * c + 1], ss.v(F32)[:, 2 * c + 1:2 * c + 2]
                act(acc[c % 2].v(BF16)[:, :1024], x1[:, c, :], AF.Square, [k3x1(c)], [(ss, ("s", c)), acc[c % 2]], accum=f0_)
                rsqrt_chain(f1_, f0_, 1.0 / D, (ss, ("s", c)), (ss, ("r", c)))
                ot = S4[c % 2]
                stt(ot.v(F32), x1[:, c, :], f1_, C["fg_bc"].v(F32), ALU.mult, ALU.mult,
                    [k3x1(c), (ss, ("r", c)), C["fg_bc"]], [ot])
                r0 = ti * T + c * 128
                so = dma("pool", out_d[r0:r0 + 128, :], ot.v(F32), [ot], [], "ot%d" % (c % 2))
                store_ops.append(so)

        env = locals()

        def dbg_hook(hook):
            if debug is None or hook not in debug:
                return
            for (nm, ncols, fn) in debug[hook]:
                ap, reads = fn(env)
                so = dma("pool", dbg_d[nm][:, :], ap, reads, [], "dbg_" + nm)
                store_ops.append(so)

        phase_marks = []
        build.phase_marks = phase_marks

        def mark(name, ti):
            phase_marks.append((name, ti, len(P.ops["pe"]), len(P.ops["act"]), len(P.ops["dve"]), len(P.ops["pool"])))

        if stop_after != "prologue":
            stage_a(0)
        for ti in range(n_tiles):
          try:
              if stop_after in ("prologue", "stage_a"):
                  break
              mark("a_branch", ti)
              a_branch(ti)
              mark("b_branch", ti)
              if ti == 0:
                  dbg_hook("after_a")
              if stop_after == "a_branch":
                  break
              b_branch(ti)
              if stop_after is not None and stop_after.startswith("b_"):
                  break
              if ti == 0:
                  dbg_hook("after_b")
              mark("stage_a", ti)
              mark("ob_merge", ti)
              ob_merge(ti)
              mark("tail", ti)
              if ti == 0:
                  dbg_hook("after_merge")
              tail(ti)
              mark("end", ti)
              if ti == 0:
                  dbg_hook("end")
          except _Stop:
            dbg_hook("stop")
            break

        for so in store_ops:
            so.needed = True
        P.finalize()
        counts = {k: len(v) for k, v in P.ops.items()}
        print("op counts", counts)
        blk = es.enter_context(nc.Block())

        @blk.tensor
        def _(e):
            P.emit("pe", e)

        @blk.scalar
        def _(e):
            P.emit("act", e)

        @blk.vector
        def _(e):
            P.emit("dve", e)

        @blk.gpsimd
        def _(e):
            P.emit("pool", e, final_waits=store_ops)

        @blk.sync
        def _(e):
            P.emit("sp", e)
    return nc


def host_consts(inp):
    f = np.float32
    k = np.arange(128)
    c = {}
    c["ident_b"] = np.eye(128, dtype=f)
    tri = (k[:, None] <= k[None, :]).astype(f)
    c["tri_b"] = tri
    c["tri_f"] = tri
    c["sut_b"] = (k[:, None] > k[None, :]).astype(f)
    c["ones_f"] = np.ones((128, 128), f)

    def pk(v):
        return np.ascontiguousarray(np.asarray(v, f).reshape(-1, 128).T)
    c["ng"] = pk(inp["norm_g"][0])
    c["png"] = pk(inp["ple_norm_g"][0])
    c["sng"] = pk(inp["ssm_norm_g"][0])
    c["lng"] = pk(inp["ln_a_g"][0])
    c["lnb"] = pk(inp["ln_a_b"][0])
    cw = np.asarray(inp["conv_w"][0], f)
    c["convw"] = np.ascontiguousarray(cw.reshape(4, 24, 128).transpose(2, 1, 0)).reshape(128, 96)
    c["convb"] = pk(inp["conv_b"][0])

    def bc(v):
        v = np.asarray(v, f).reshape(1, -1)
        return np.ascontiguousarray(np.broadcast_to(v, (128, v.shape[1])))
    c["fg_bc"] = bc(inp["final_g"])
    c["dtb_bc"] = bc(inp["dt_bias"][0])
    c["alog_bc"] = bc(inp["a_log"][0])
    c["dskip_bc"] = bc(inp["d_skip"][0])
    ws = np.asarray(inp["w_s"][0], f)
    c["wsT"] = np.ascontiguousarray(ws.transpose(2, 0, 1)).reshape(128, 512)
    c["bs_bc"] = bc(np.asarray(inp["b_s"][0], f).reshape(-1))
    return c


def make_in_maps(inp, n_cores, n_tiles):
    c = host_consts(inp)
    ntok = n_tiles * T
    base = {
        "w_in": np.ascontiguousarray(np.asarray(inp["w_in"][0], np.float32)),
        "w_oa": np.ascontiguousarray(np.asarray(inp["w_oa"][0], np.float32)),
        "w_ob": np.ascontiguousarray(np.asarray(inp["w_ob"][0], np.float32)),
        "w_out": np.ascontiguousarray(np.asarray(inp["w_out"][0], np.float32)),
        "w_pg": np.ascontiguousarray(np.asarray(inp["w_pg"][0], np.float32)),
        "w_ple": np.ascontiguousarray(np.asarray(inp["w_ple"][0], np.float32)),
    }
    for k, v in c.items():
        base["c_" + k] = v
    maps = []
    for b in range(n_cores):
        m = dict(base)
        m["x"] = np.ascontiguousarray(np.asarray(inp["x"][b, :ntok], np.float32))
        m["p"] = np.ascontiguousarray(np.asarray(inp["p"][0, b, :ntok], np.float32))
        maps.append(m)
    return maps


def kernel(**inputs):
    n_tiles = SEQ // T
    nc = build(n_tiles)
    maps = make_in_maps(inputs, NB, n_tiles)
    res = run_bass_kernel_spmd(nc, maps, core_ids=list(range(NB)))
    out = np.stack([np.asarray(r["out"], np.float32) for r in res.results], axis=0)
    return out
```
